# Optimizing a Trainium2 kernel written in Bass

```python
import math
import jax
import jax.numpy as jnp
from jax import lax
import numpy as np

D_MODEL = 1024
BATCH = 16
SEQ = 2048
DEPTH = 1

D_MIX = D_MODEL
ATT_HEADS = 4
ATT_QK_DIM = 64
ATT_V_DIM = 2 * ATT_QK_DIM
D_ATT = ATT_HEADS * ATT_V_DIM
M_HEADS = 4
D_MLSTM = D_MIX - D_ATT
M_DIM = D_MLSTM // M_HEADS
D_PROJ = 3 * D_ATT + 4 * D_MLSTM + 2 * M_HEADS
CONV_W = 4
CHUNK = 64
Q_BLOCK = 128
N_BUCKETS = 32
MAX_DIST = 128
N_KEYS = 128
N_EXPERTS = N_KEYS * N_KEYS
PEER_HEADS = 8
PEER_TOPK = 16
PEER_QDIM = 256
PEER_BLOCK = 128
EPS = 1e-6

kernel_name = 'hybrid_diffattn_mlstm_peer'


def _rms(x):
    xf = x.astype(jnp.float32)
    return xf * lax.rsqrt(jnp.mean(xf * xf, axis=-1, keepdims=True) + EPS)


def rmsnorm(x, g):
    return _rms(x).astype(x.dtype) * g


def modulate(h, shift, scale):
    return h * (1.0 + scale[:, None, :]) + shift[:, None, :]


def rel_bucket(n):
    max_exact = N_BUCKETS // 2
    nf = jnp.maximum(n, 1).astype(jnp.float32)
    large = max_exact + (jnp.log(nf / max_exact) / math.log(MAX_DIST / max_exact)
                         * (N_BUCKETS - max_exact)).astype(jnp.int32)
    large = jnp.minimum(large, N_BUCKETS - 1)
    return jnp.where(n < max_exact, n, large)


def causal_conv(x, w, b):
    S = x.shape[1]
    xp = jnp.pad(x, ((0, 0), (CONV_W - 1, 0), (0, 0)))
    out = xp[:, 0:S] * w[0]
    for j in range(1, CONV_W):
        out = out + xp[:, j:j + S] * w[j]
    return out + b


def diff_attention(q, k, v, rel_bias, lam_q1, lam_k1, lam_q2, lam_k2, sub_g, lambda_init):
    B, S = q.shape[0], q.shape[1]
    lam = (jnp.exp(jnp.sum(lam_q1 * lam_k1).astype(jnp.float32))
           - jnp.exp(jnp.sum(lam_q2 * lam_k2).astype(jnp.float32)) + lambda_init)
    scale = ATT_QK_DIM ** -0.5
    outs = []
    for qb in range(S // Q_BLOCK):
        s0 = qb * Q_BLOCK
        s1 = s0 + Q_BLOCK
        logits = jnp.einsum('bqhtd,bkhtd->bhtqk', q[:, s0:s1], k[:, :s1]).astype(jnp.float32) * scale
        rel = (s0 + jnp.arange(Q_BLOCK, dtype=jnp.int32))[:, None] - jnp.arange(s1, dtype=jnp.int32)[None, :]
        bias = jnp.transpose(rel_bias[rel_bucket(jnp.maximum(rel, 0))], (2, 0, 1)).astype(jnp.float32)
        logits = jnp.where(rel >= 0, logits + bias[None, :, None], jnp.finfo(jnp.float32).min)
        p = jax.nn.softmax(logits, axis=-1)
        a = p[:, :, 0] - lam * p[:, :, 1]
        outs.append(jnp.einsum('bhqk,bkhe->bqhe', a.astype(v.dtype), v[:, :s1]))
    o = jnp.concatenate(outs, axis=1)
    o = _rms(o).astype(v.dtype) * sub_g * (1.0 - lambda_init)
    return o.reshape(B, S, D_ATT)


def mlstm_chunkwise(q, k, v, i_pre, f_pre):
    B, S, H, d = q.shape
    NC = S // CHUNK

    def chunks(t):
        return t.reshape(B, NC, CHUNK, H, d).transpose(0, 3, 1, 2, 4)

    q, k, v = chunks(q), chunks(k), chunks(v)
    ig = i_pre.reshape(B, NC, CHUNK, H).transpose(0, 3, 1, 2)
    logf = jax.nn.log_sigmoid(f_pre).reshape(B, NC, CHUNK, H).transpose(0, 3, 1, 2)
    b = jnp.cumsum(logf, axis=-1)
    b_last = b[..., -1]
    a_end = b_last[..., None] - b + ig

    def step(carry, xs):
        C, n, m = carry
        k_c, v_c, a_c, bl = xs
        m_new = jnp.maximum(bl + m, jnp.max(a_c, axis=-1))
        decay = jnp.exp(bl + m - m_new)
        w = jnp.exp(a_c - m_new[..., None])
        C_new = decay[..., None, None] * C + jnp.einsum('bhl,bhld,bhle->bhde', w, k_c, v_c)
        n_new = decay[..., None] * n + jnp.einsum('bhl,bhld->bhd', w, k_c)
        return (C_new, n_new, m_new), (C, n, m)

    init = (jnp.zeros((B, H, d, d), jnp.float32), jnp.zeros((B, H, d), jnp.float32),
            jnp.zeros((B, H), jnp.float32))
    xs = (k.transpose(2, 0, 1, 3, 4), v.transpose(2, 0, 1, 3, 4),
          a_end.transpose(2, 0, 1, 3), b_last.transpose(2, 0, 1))
    _, (C_prev, n_prev, m_prev) = lax.scan(step, init, xs)
    C_prev = C_prev.transpose(1, 2, 0, 3, 4)
    n_prev = n_prev.transpose(1, 2, 0, 3)
    m_prev = m_prev.transpose(1, 2, 0)

    causal = jnp.tril(jnp.ones((CHUNK, CHUNK), dtype=bool))
    logD = jnp.where(causal, b[..., :, None] - b[..., None, :] + ig[..., None, :], -jnp.inf)
    m_inter = b + m_prev[..., None]
    m_j = jnp.maximum(jnp.max(logD, axis=-1), m_inter)
    W = jnp.exp(logD - m_j[..., None])
    Sqk = jnp.einsum('bhcjd,bhcsd->bhcjs', q, k) * W
    inter = jnp.exp(m_inter - m_j)
    num = (jnp.einsum('bhcjs,bhcse->bhcje', Sqk, v)
           + inter[..., None] * jnp.einsum('bhcjd,bhcde->bhcje', q, C_prev))
    den = jnp.sum(Sqk, axis=-1) + inter * jnp.einsum('bhcjd,bhcd->bhcj', q, n_prev)
    h = num / jnp.maximum(jnp.abs(den), jnp.exp(-m_j))[..., None]
    return h.transpose(0, 2, 3, 1, 4).reshape(B, S, H, d)


def hybrid_mixer(h, w_in, conv_w, conv_b, b_igate, b_fgate, lam_q1, lam_k1, lam_q2, lam_k2,
                 diff_sub_g, mlstm_norm_g, w_out, rel_bias, lambda_init):
    B, S, _ = h.shape
    proj = h @ w_in
    sizes = [D_ATT, D_ATT, D_ATT, D_MLSTM, D_MLSTM, D_MLSTM, D_MLSTM, M_HEADS, M_HEADS]
    splits = np.cumsum(sizes)[:-1].tolist()
    dq, dk, dv, mq, mk, mv, mo, mi, mf = jnp.split(proj, splits, axis=-1)
    att = diff_attention(dq.reshape(B, S, ATT_HEADS, 2, ATT_QK_DIM),
                         dk.reshape(B, S, ATT_HEADS, 2, ATT_QK_DIM),
                         dv.reshape(B, S, ATT_HEADS, ATT_V_DIM),
                         rel_bias, lam_q1, lam_k1, lam_q2, lam_k2, diff_sub_g, lambda_init)
    qk = jax.nn.silu(causal_conv(jnp.concatenate([mq, mk], axis=-1), conv_w, conv_b))
    mq, mk = jnp.split(qk, 2, axis=-1)
    f32 = jnp.float32
    hm = mlstm_chunkwise(mq.reshape(B, S, M_HEADS, M_DIM).astype(f32),
                         (mk * (M_DIM ** -0.5)).reshape(B, S, M_HEADS, M_DIM).astype(f32),
                         mv.reshape(B, S, M_HEADS, M_DIM).astype(f32),
                         (mi + b_igate).astype(f32), (mf + b_fgate).astype(f32))
    hm = jax.nn.sigmoid(mo.astype(f32)).reshape(B, S, M_HEADS, M_DIM) * hm
    hm = _rms(hm).reshape(B, S, D_MLSTM).astype(h.dtype) * mlstm_norm_g
    mix = jnp.concatenate([att, hm], axis=-1)
    return mix @ w_out


def peer(h, w_q, sub_keys, u, v):
    B, S, D = h.shape
    T = B * S
    ht = h.reshape(T, D)
    q = (ht @ w_q).reshape(T, PEER_HEADS, 2, PEER_QDIM // 2)
    s = jnp.einsum('thpd,hpkd->thpk', q, sub_keys).astype(jnp.float32)
    sv, si = lax.top_k(s, PEER_TOPK)
    cand = sv[:, :, 0, :, None] + sv[:, :, 1, None, :]
    cand_idx = si[:, :, 0, :, None] * N_KEYS + si[:, :, 1, None, :]
    top_s, pos = lax.top_k(cand.reshape(T, PEER_HEADS, PEER_TOPK * PEER_TOPK), PEER_TOPK)
    idx = jnp.take_along_axis(cand_idx.reshape(T, PEER_HEADS, PEER_TOPK * PEER_TOPK), pos, axis=-1)
    g = jax.nn.softmax(top_s, axis=-1)
    nb = T // PEER_BLOCK

    def block(args):
        hb, ib, gb = args
        act = jax.nn.gelu(jnp.einsum('thkd,td->thk', u[ib], hb))
        return jnp.einsum('thk,thkd->td', (gb * act).astype(hb.dtype), v[ib])

    out = lax.map(block, (ht.reshape(nb, PEER_BLOCK, D),
                          idx.reshape(nb, PEER_BLOCK, PEER_HEADS, PEER_TOPK),
                          g.reshape(nb, PEER_BLOCK, PEER_HEADS, PEER_TOPK)))
    return out.reshape(B, S, D)


def setup_inputs(seed: int = 0) -> dict:
    key = jax.random.key(seed)
    ks = jax.random.split(key, 24)
    L, D = DEPTH, D_MODEL

    def nrm(k, shape, s):
        return jax.random.normal(k, shape, jnp.float32) * s

    return {
        'x': nrm(ks[0], (BATCH, SEQ, D), 1.0),
        'c': nrm(ks[1], (BATCH, D), 1.0),
        'w_ada': nrm(ks[2], (L, D, 6 * D), 0.5 * D ** -0.5),
        'b_ada': nrm(ks[3], (L, 6 * D), 0.02),
        'norm1_g': 1.0 + nrm(ks[4], (L, D), 0.02),
        'norm2_g': 1.0 + nrm(ks[5], (L, D), 0.02),
        'w_in': nrm(ks[6], (L, D, D_PROJ), D ** -0.5),
        'conv_w': nrm(ks[7], (L, CONV_W, 2 * D_MLSTM), CONV_W ** -0.5),
        'conv_b': nrm(ks[8], (L, 2 * D_MLSTM), 0.02),
        'b_igate': nrm(ks[9], (L, M_HEADS), 0.1),
        'b_fgate': jnp.linspace(3.0, 6.0, M_HEADS, dtype=jnp.float32)[None, :] + nrm(ks[10], (L, M_HEADS), 0.1),
        'lam_q1': nrm(ks[11], (L, ATT_QK_DIM), 0.1),
        'lam_k1': nrm(ks[12], (L, ATT_QK_DIM), 0.1),
        'lam_q2': nrm(ks[13], (L, ATT_QK_DIM), 0.1),
        'lam_k2': nrm(ks[14], (L, ATT_QK_DIM), 0.1),
        'diff_sub_g': 1.0 + nrm(ks[15], (L, ATT_V_DIM), 0.02),
        'mlstm_norm_g': 1.0 + nrm(ks[16], (L, D_MLSTM), 0.02),
        'w_out': nrm(ks[17], (L, D_MIX, D), D_MIX ** -0.5),
        'peer_w_q': nrm(ks[18], (L, D, PEER_HEADS * PEER_QDIM), D ** -0.5),
        'peer_sub_keys': nrm(ks[19], (L, PEER_HEADS, 2, N_KEYS, PEER_QDIM // 2), (PEER_QDIM // 2) ** -0.5),
        'peer_u': nrm(ks[20], (L, N_EXPERTS, D), D ** -0.5),
        'peer_v': nrm(ks[21], (L, N_EXPERTS, D), PEER_HEADS ** -0.5),
        'rel_bias': nrm(ks[22], (N_BUCKETS, ATT_HEADS), 0.5),
        'final_g': 1.0 + nrm(ks[23], (D,), 0.02),
    }


def reference(x, c, w_ada, b_ada, norm1_g, norm2_g, w_in, conv_w, conv_b, b_igate, b_fgate,
              lam_q1, lam_k1, lam_q2, lam_k2, diff_sub_g, mlstm_norm_g, w_out,
              peer_w_q, peer_sub_keys, peer_u, peer_v, rel_bias, final_g):
    cond = jax.nn.silu(c)
    for l in range(DEPTH):
        lambda_init = 0.8 - 0.6 * math.exp(-0.3 * l)
        mod = cond @ w_ada[l] + b_ada[l]
        sh1, sc1, g1, sh2, sc2, g2 = jnp.split(mod, 6, axis=-1)
        h = modulate(rmsnorm(x, norm1_g[l]), sh1, sc1)
        y = hybrid_mixer(h, w_in[l], conv_w[l], conv_b[l], b_igate[l], b_fgate[l],
                         lam_q1[l], lam_k1[l], lam_q2[l], lam_k2[l], diff_sub_g[l],
                         mlstm_norm_g[l], w_out[l], rel_bias, lambda_init)
        x = x + g1[:, None, :] * y
        h = modulate(rmsnorm(x, norm2_g[l]), sh2, sc2)
        x = x + g2[:, None, :] * peer(h, peer_w_q[l], peer_sub_keys[l], peer_u[l], peer_v[l])
    return rmsnorm(x, final_g)
```

```python
import math
import numpy as np
import concourse.bass as bass
import concourse.mybir as mybir
from concourse.bass_utils import run_bass_kernel_spmd

F32 = mybir.dt.float32
BF16 = mybir.dt.bfloat16
U32 = mybir.dt.uint32
I32 = mybir.dt.int32
ALU = mybir.AluOpType
AF = mybir.ActivationFunctionType
AX = mybir.AxisListType

ENGS = ("pe", "act", "dve", "pool", "sp")
SAME_ENGINE_SYNC = {"pe": False, "act": True, "dve": True, "pool": True, "sp": True}

S = 2048
D = 1024
NT = 16
EPS = 1e-6
NU = 3
NV = 3
ARENA_W = 24320
WREG = 6144
GS = 4
NB = 8


class Prog:
    def __init__(self, nc):
        self.nc = nc
        self.q = {e: [] for e in ENGS}
        self.cnt = {}
        self.semobj = {}
        self.lastw = {}
        self.readers = {}
        self.seen = {e: {} for e in ENGS}
        self.ownsem = {}
        for e in ENGS:
            s = nc.alloc_semaphore("es_" + e)
            self.ownsem[e] = s.name
            self.semobj[s.name] = s
            self.cnt[s.name] = 0
        self.nsem = 0

    def new_sem(self, name=None):
        self.nsem += 1
        s = self.nc.alloc_semaphore(name or ("ds_%d" % self.nsem))
        self.semobj[s.name] = s
        self.cnt[s.name] = 0
        return s.name

    def op(self, eng, fn, reads=(), writes=(), sem=None, inc=None):
        if sem is None:
            sem = self.ownsem[eng]
            inc = 1
        elif inc is None:
            inc = 16
        deps = {}

        def add(d):
            if d is None:
                return
            s, v = d
            if deps.get(s, 0) < v:
                deps[s] = v

        for k in reads:
            add(self.lastw.get(k))
        for k in writes:
            add(self.lastw.get(k))
            for s, v in self.readers.get(k, {}).items():
                add((s, v))
        waits = []
        for s, v in deps.items():
            if s == self.ownsem[eng] and not SAME_ENGINE_SYNC[eng]:
                continue
            if self.seen[eng].get(s, 0) >= v:
                continue
            self.seen[eng][s] = v
            waits.append((s, v))
        self.cnt[sem] += inc
        val = self.cnt[sem]
        for k in reads:
            r = self.readers.setdefault(k, {})
            if r.get(sem, 0) < val:
                r[sem] = val
        for k in writes:
            self.lastw[k] = (sem, val)
            self.readers[k] = {}
        self.q[eng].append((waits, fn, sem, inc))

    def barrier(self):
        snap = dict(self.cnt)
        for e in ENGS:
            waits = []
            for s, v in snap.items():
                if v > self.seen[e].get(s, 0):
                    self.seen[e][s] = v
                    waits.append((s, v))
            own = self.ownsem[e]
            self.cnt[own] += 1
            self.q[e].append((waits, (lambda eng: eng.nop()), own, 1))
        self.lastw.clear()
        self.readers.clear()

    def emit(self):
        nc = self.nc
        finals = [(s, v) for s, v in self.cnt.items() if v > 0]
        with nc.Block() as block:
            def run(eng_name, wait_all=False):
                def f(eng):
                    for waits, fn, sem, inc in self.q[eng_name]:
                        for s, v in waits:
                            eng.wait_ge(self.semobj[s], v)
                        ins = fn(eng)
                        ins.then_inc(self.semobj[sem], inc)
                    if wait_all:
                        for s, v in finals:
                            eng.wait_ge(self.semobj[s], v)
                return f
            block.tensor(run("pe"))
            block.scalar(run("act"))
            block.vector(run("dve"))
            block.gpsimd(run("pool"))
            block.sync(run("sp", wait_all=True))


def cap(t, off, dims):
    return bass.AP(t.tensor, t.offset + off, [list(t.ap[0])] + [list(d) for d in dims])


class _Stop(Exception):
    pass


class KB:
    def __init__(self, dbg=None, nseq=2, ntiles2=NT, stop=None, no_back=False):
        self.no_back = no_back
        self.dbg = dbg
        self.stop = stop
        self.nseq = nseq
        self.ntiles2 = ntiles2
        nc = self.nc = bass.Bass("TRN2", target_bir_lowering=False)
        self.p = Prog(nc)
        self._n = 0

    def mm(self, out, lhsT, rhs, start, stop, r, w):
        self.p.op("pe", lambda e: e.matmul(out, lhsT=lhsT, rhs=rhs, start=start, stop=stop), reads=r, writes=w)

    def tr(self, out, in_, ident, r, w):
        self.p.op("pe", lambda e: e.transpose(out=out, in_=in_, identity=ident), reads=r, writes=w)

    def act(self, out, in_, func, r, w, scale=None, bias=None, accum=None):
        kw = {}
        if scale is not None:
            kw["scale"] = scale
        if bias is not None:
            kw["bias"] = bias
        if accum is not None:
            kw["accum_out"] = accum
        self.p.op("act", lambda e: e.activation(out=out, in_=in_, func=func, **kw), reads=r, writes=w)

    def tt(self, eng, out, in0, in1, op, r, w):
        self.p.op(eng, lambda e: e.tensor_tensor(out=out, in0=in0, in1=in1, op=op), reads=r, writes=w)

    def ts(self, eng, out, in0, s1, op0, r, w, s2=None, op1=None):
        if op1 is None:
            self.p.op(eng, lambda e: e.tensor_scalar(out=out, in0=in0, scalar1=s1, scalar2=None, op0=op0), reads=r, writes=w)
        else:
            self.p.op(eng, lambda e: e.tensor_scalar(out=out, in0=in0, scalar1=s1, scalar2=s2, op0=op0, op1=op1), reads=r, writes=w)

    def stt(self, out, in0, scalar, in1, op0, op1, r, w):
        self.p.op("dve", lambda e: e.scalar_tensor_tensor(out=out, in0=in0, scalar=scalar, in1=in1, op0=op0, op1=op1), reads=r, writes=w)

    def cp(self, eng, out, in_, r, w):
        self.p.op(eng, lambda e: e.tensor_copy(out=out, in_=in_), reads=r, writes=w)

    def red(self, out, in_, op, r, w):
        self.p.op("dve", lambda e: e.tensor_reduce(out=out, in_=in_, axis=AX.X, op=op), reads=r, writes=w)

    def recip(self, out, in_, r, w):
        self.p.op("dve", lambda e: e.reciprocal(out=out, in_=in_), reads=r, writes=w)

    def memset(self, eng, ap, val, w):
        self.p.op(eng, lambda e: e.memset(ap, val), writes=w)

    def dma(self, eng, out, in_, r, w, sem):
        self.p.op(eng, lambda e: e.dma_start(out=out, in_=in_), reads=r, writes=w, sem=sem)

    def gather(self, out, table, idx, r, w, sem):
        self.p.op("pool", lambda e: e.indirect_dma_start(
            out=out, out_offset=None, in_=table,
            in_offset=bass.IndirectOffsetOnAxis(ap=idx, axis=0)), reads=r, writes=w, sem=sem)

    def sb(self, shape, dt, name=None):
        self._n += 1
        return self.nc.alloc_sbuf_tensor("sb_" + (name or ("t%d" % self._n)), shape, dt)

    def rstd(self, ss, tmp, out, inv_n, key_ss, key_tmp, key_out):
        self.ts("dve", tmp, ss, inv_n, ALU.mult, [key_ss], [key_tmp], s2=EPS, op1=ALU.add)
        self.act(tmp, tmp, AF.Sqrt, [key_tmp], [key_tmp])
        self.recip(out, tmp, [key_tmp], [key_out])

    def build(self):
        try:
            self._build()
        except _Stop:
            pass
        self.p.emit()
        return self.nc

    def chk(self, tag):
        if self.stop == tag:
            raise _Stop()

    def _build(self):
        nc, p = self.nc, self.p

        def din(name, shape, dt=F32):
            return nc.dram_tensor(name, shape, dt, kind="ExternalInput").ap()

        x_d = din("x", [2, S, D])
        cT_d = din("cT", [128, 8, 2])
        w_ada_d = din("w_ada", [D, 6 * D])
        b_ada_d = din("b_ada", [6 * D])
        b_adaT_d = din("b_adaT", [128, 48])
        n1gT_d = din("n1gT", [128, 8])
        n2gT_d = din("n2gT", [128, 8])
        n2g_d = din("n2g", [D])
        w_in_d = din("w_in", [D, 3592])
        convwT_d = din("convwT", [128, 8, 4])
        convbT_d = din("convbT", [128, 8])
        bgate_d = din("bgate", [8])
        lam_d = din("lam", [4, 64])
        gsub_d = din("gsub", [128])
        gm_d = din("gm", [512])
        w_out_d = din("w_out", [D, D])
        w_q_d = din("w_q", [D, 2048])
        skT_d = din("skT", [128, 2048])
        u_d = din("peer_u", [16384, D])
        v_d = din("peer_v", [16384, D])
        relb_d = din("rel_bias", [32, 4])
        biasT_d = din("biasT", [128, 4, 256])
        fin_d = din("final_g", [D])
        cst_d = din("cst", [128, 400])
        out_d = nc.dram_tensor("out", [2, S, D], F32, kind="ExternalOutput").ap()
        dbg_d = {}
        if self.dbg:
            for name, shape in self.dbg.items():
                dbg_d[name] = nc.dram_tensor("dbg_" + name, shape, F32, kind="ExternalOutput").ap()

        w_ada_v = w_ada_d.rearrange("(c p) n -> p c n", p=128)
        w_in_v = w_in_d.rearrange("(c p) n -> p c n", p=128)
        w_out_v = w_out_d.rearrange("(c p) n -> p c n", p=128)
        w_q_v = w_q_d.rearrange("(c p) n -> p c n", p=128)

        sb = self.sb
        cst = sb([128, 400], F32, "cst")
        ident_f = cst[:, 0:128]
        maskU_f = cst[:, 128:256]
        ones_f = cst[:, 256:384]
        iota16 = cst[:, 384:400]
        ident_b = sb([128, 128], BF16, "ident_b")
        biasT = sb([128, 4, 256], F32, "biasT")
        cfar = sb([128, 4], F32, "cfar")
        condT = sb([128, 8, 2], F32, "condT")
        modT = sb([128, 48, 2], F32, "modT")
        b_adaT = sb([128, 48], F32, "b_adaT")
        n1gT = sb([128, 8], F32, "n1gT")
        n2gT = sb([128, 8], F32, "n2gT")
        convwT = sb([128, 8, 4], F32, "convwT")
        convbT = sb([128, 8], F32, "convbT")
        A1 = sb([128, 2, 8], F32, "A1")
        A2 = sb([128, 2, 8], F32, "A2")
        fin_bc = sb([128, D], F32, "fin_bc")
        gsub_bc = sb([128, 128], F32, "gsub_bc")
        gm_bc = sb([128, 512], F32, "gm_bc")
        bg_bc = sb([128, 8], F32, "bg_bc")
        A2_bc = sb([128, D], F32, "A2_bc")
        B2_bc = sb([128, D], F32, "B2_bc")
        g2_bc = sb([128, D], F32, "g2_bc")
        lamv = sb([128, 4, 64], F32, "lamv")
        lams = sb([128, 8], F32, "lams")
        small = sb([128, 16], F32, "small")
        mixT = sb([128, 8, S], BF16, "mixT")
        arena1 = sb([128, 8, S], BF16, "arena1")
        hT = arena1
        wq = arena1
        woutp = sb([128, 8, D], BF16, "woutp")
        skT = sb([128, 16, 128], BF16, "skT")
        arena = sb([128, ARENA_W], F32, "arena")
        wst = [arena[:, i * 2048:(i + 1) * 2048].rearrange("p (c n) -> p c n", c=8) for i in range(2)]
        wbf = [arena[:, 4096 + i * 1024:4096 + (i + 1) * 1024].bitcast(BF16).rearrange("p (c n) -> p c n", c=8) for i in range(2)]

        pA = nc.alloc_psum_tensor("pA", [128, 2048], F32)
        pT = nc.alloc_psum_tensor("pT", [128, 1024], BF16)
        pB = nc.alloc_psum_tensor("pB", [128, 512], F32)
        pC = nc.alloc_psum_tensor("pC", [128, 512], F32)
        pD = nc.alloc_psum_tensor("pD", [128, 512], F32)

        s_c = p.new_sem("s_const")
        s_xt = p.new_sem("s_xt")
        s_w = [p.new_sem("s_w0"), p.new_sem("s_w1")]
        s_uv = [p.new_sem("s_uv%d" % i) for i in range(NB)]
        s_pu = [p.new_sem("s_pu%d" % i) for i in range(2)]
        s_pv = [p.new_sem("s_pv%d" % i) for i in range(2)]
        s_ps = [p.new_sem("s_ps%d" % i) for i in range(2)]
        s_out = p.new_sem("s_out")
        s_dbg = p.new_sem("s_dbg")

        ar = {"off": 0}

        def areset(off=WREG):
            ar["off"] = off

        def aF(n):
            o = ar["off"]
            ar["off"] += n
            assert ar["off"] <= ARENA_W, ar["off"]
            return arena[:, o:o + n]

        def aB(n):
            w = (n + 1) // 2
            return aF(w).bitcast(BF16)[:, 0:n]

        def aU(n):
            return aF(n).bitcast(U32)

        def aI(n):
            return aF(n).bitcast(I32)

        wstate = {"k": 0}

        def load_w(dram_v, c0, ncols, scale_bc=None, dst=None):
            k = wstate["k"]
            wstate["k"] ^= 1
            self.dma("sp", wst[k][:, :, 0:ncols], dram_v[:, :, c0:c0 + ncols], [], ["wst%d" % k], s_w[k])
            if dst is None:
                dst = wbf[k][:, :, 0:ncols]
                key = "wbf%d" % k
            else:
                dst, key = dst
            if scale_bc is None:
                self.cp("pool", dst, wst[k][:, :, 0:ncols], ["wst%d" % k], [key])
            else:
                bc, bkey = scale_bc
                self.tt("pool", dst, wst[k][:, :, 0:ncols], cap(bc, 0, [[0, 8], [1, ncols]]), ALU.mult,
                        ["wst%d" % k, bkey], [key])
            return dst, key

        def dbg_out(name, src, keys):
            if name in dbg_d:
                self.dma("sp", dbg_d[name], src, keys, [], s_dbg)

        cl = lambda o, i, w: self.dma("sp", o, i, [], [w], s_c)
        cl(cst[:], cst_d, "cst")
        cl(biasT[:], biasT_d, "biasT")
        cl(cfar[:], relb_d[31, :].partition_broadcast(128), "cfar")
        cl(condT[:], cT_d, "condT")
        cl(b_adaT[:], b_adaT_d, "b_adaT")
        cl(n1gT[:], n1gT_d, "n1gT")
        cl(n2gT[:], n2gT_d, "n2gT")
        cl(convwT[:], convwT_d, "convwT")
        cl(convbT[:], convbT_d, "convbT")
        cl(fin_bc[:], fin_d.partition_broadcast(128), "fin_bc")
        cl(gsub_bc[:], gsub_d.partition_broadcast(128), "gsub_bc")
        cl(gm_bc[:], gm_d.partition_broadcast(128), "gm_bc")
        cl(bg_bc[:], bgate_d.partition_broadcast(128), "bg_bc")
        for i in range(4):
            cl(lamv[:, i, :], lam_d[i, :].partition_broadcast(128), "lamv")
        uv_dram = nc.dram_tensor("uv_scratch", [16384, 2048], BF16).ap()
        areset()
        ust = [aF(D) for _ in range(2)]
        vst = [aF(D) for _ in range(2)]
        uvb = [aB(2048) for _ in range(2)]
        for blk in range(128):
            k = blk % 2
            rows = slice(blk * 128, (blk + 1) * 128)
            self.dma("sp", ust[k], u_d[rows, :], [], ["ust%d" % k], s_pu[k])
            self.dma("sp", vst[k], v_d[rows, :], [], ["vst%d" % k], s_pv[k])
            self.act(uvb[k][:, 0:1024], ust[k], AF.Copy, ["ust%d" % k], ["uvbA%d" % k])
            self.cp("dve", uvb[k][:, 1024:2048], vst[k], ["vst%d" % k], ["uvbB%d" % k])
            self.dma("pool", uv_dram[rows, :], uvb[k], ["uvbA%d" % k, "uvbB%d" % k], [], s_ps[k])
        p.barrier()
        areset()
        mod_rows = aF(6144)
        brow = aF(6144)
        cl(brow[0:2, :], b_ada_d.partition_broadcast(2), "brow")
        skst = wst[0].rearrange("p c n -> p (c n)")
        cl(skst, skT_d, "wst0")
        p.barrier()
        self.cp("pool", skT[:].rearrange("p c n -> p (c n)"), skst, ["wst0"], ["skT"])
        self.act(condT[:], condT[:], AF.Silu, ["condT"], ["condT"])
        self.cp("dve", ident_b[:], ident_f, ["cst"], ["ident_b"])
        self.ts("dve", gsub_bc[:], gsub_bc[:], 0.8, ALU.mult, ["gsub_bc"], ["gsub_bc"])
        junk64 = aF(64)
        for j in range(2):
            self.tt("dve", junk64, lamv[:, 2 * j, :], lamv[:, 2 * j + 1, :], ALU.mult, ["lamv"], ["junk64"])
            self.red(lams[:, j:j + 1], junk64, ALU.add, ["junk64"], ["lams"])
        self.act(lams[:, 2:4], lams[:, 0:2], AF.Exp, ["lams"], ["lams"])
        self.tt("dve", lams[:, 4:5], lams[:, 3:4], lams[:, 2:3], ALU.subtract, ["lams"], ["lams"])
        self.ts("dve", lams[:, 4:5], lams[:, 4:5], -0.2, ALU.add, ["lams"], ["lams"])
        neglam = lams[:, 4:5]
        for blk in range(24):
            k = wstate["k"]
            wstate["k"] ^= 1
            wk_ = "wst%d" % k
            self.dma("sp", wst[k], w_ada_v[:, :, blk * 256:(blk + 1) * 256], [], [wk_], s_w[k])
            for kc in range(8):
                self.mm(pB[0:2, 0:256], condT[:, kc, :], wst[k][:, kc, :], kc == 0, kc == 7, ["condT", wk_], ["pB"])
            self.tt("dve", mod_rows[0:2, blk * 256:(blk + 1) * 256], pB[0:2, 0:256], brow[0:2, blk * 256:(blk + 1) * 256],
                    ALU.add, ["pB", "brow"], ["mod_rows"])
            for jl in range(2):
                j = blk * 2 + jl
                for kc in range(8):
                    self.mm(pC[:, 2 * j:2 * j + 2], wst[k][:, kc, jl * 128:(jl + 1) * 128], condT[:, kc, :],
                            kc == 0, kc == 7, ["condT", wk_], ["pC"])
        self.tt("dve", modT[:], pC[:, 0:96].rearrange("p (j b) -> p j b", b=2), cap(b_adaT[:], 0, [[1, 48], [0, 2]]),
                ALU.add, ["pC", "b_adaT"], ["modT"])
        for b in range(2):
            self.stt(A1[:, b, :], modT[:, 8:16, b], 1.0, n1gT[:], ALU.add, ALU.mult, ["modT", "n1gT"], ["A1"])
            self.stt(A2[:, b, :], modT[:, 32:40, b], 1.0, n2gT[:], ALU.add, ALU.mult, ["modT", "n2gT"], ["A2"])
        if "modT" in dbg_d:
            dbg_out("modT", modT[:].rearrange("p j b -> p (j b)"), ["modT"])
        mod_dram = nc.dram_tensor("mod_scratch", [2, 6144], F32).ap()
        self.dma("sp", mod_dram, mod_rows[0:2, :], ["mod_rows"], [], s_c)
        p.barrier()

        self.chk("p0")
        for b in range(self.nseq):
            areset()
            g1_bc = aF(D)
            n2g_bc = aF(D)
            self.dma("sp", n2g_bc, n2g_d.partition_broadcast(128), [], ["n2g_bc"], s_c)
            for dst, key, col0 in ((g1_bc, "g1_bc", 2048), (B2_bc[:], "B2_bc", 3072), (A2_bc[:], "A2_bc", 4096), (g2_bc[:], "g2_bc", 5120)):
                self.dma("sp", dst, mod_dram[b, col0:col0 + 1024].partition_broadcast(128), [], [key], s_c)
            p.barrier()
            self.stt(A2_bc[:], A2_bc[:], 1.0, n2g_bc, ALU.add, ALU.mult, ["A2_bc", "n2g_bc"], ["A2_bc"])
            for blk in range(4):
                load_w(w_out_v, blk * 256, 256, scale_bc=(g1_bc[:, blk * 256:(blk + 1) * 256], "g1_bc"),
                       dst=(woutp[:, :, blk * 256:(blk + 1) * 256], "woutp"))
            p.barrier()

            areset()
            xt = aF(D)
            junk = aF(D)
            xn = aB(D)
            for i in range(NT):
                self.dma("sp", xt, x_d[b, i * 128:(i + 1) * 128, :], [], ["xt"], s_xt)
                self.act(junk, xt, AF.Square, ["xt"], ["junk", "ss"], accum=small[:, 0:1])
                self.rstd(small[:, 0:1], small[:, 1:2], small[:, 2:3], 1.0 / D, "ss", "tmpr", "rstd")
                self.ts("dve", xn, xt, small[:, 2:3], ALU.mult, ["xt", "rstd"], ["xn"])
                for c in range(8):
                    self.tr(pT[:, c * 128:(c + 1) * 128], xn[:, c * 128:(c + 1) * 128], ident_b[:], ["xn", "ident_b"], ["pT"])
                for c in range(8):
                    self.act(hT[:, c, i * 128:(i + 1) * 128], pT[:, c * 128:(c + 1) * 128], AF.Identity, ["pT", "A1", "modT"], ["hT"],
                             scale=A1[:, b, c:c + 1], bias=modT[:, c, b:b + 1])
            p.barrier()

            self.chk("p1a")

            def proj_fm(wcols, wkey, evac):
                for g in range(4):
                    for kc in range(8):
                        self.mm(pB[:, 0:512], wcols[:, kc, :], hT[:, kc, g * 512:(g + 1) * 512], kc == 0, kc == 7,
                                [wkey, "hT"], ["pB"])
                    evac(g)

            def proj_tm(wcols, wkey, ncols, evac):
                for i in range(NT):
                    for kc in range(8):
                        self.mm(pC[:, 0:ncols], hT[:, kc, i * 128:(i + 1) * 128], wcols[:, kc, 0:ncols], kc == 0, kc == 7,
                                [wkey, "hT"], ["pC"])
                    evac(i)

            areset()
            qT = aB(S)
            kT = aB(S)
            vh = aB(16 * 130).rearrange("p (i e) -> p i e", e=130)
            PT = aB(S)
            tmpS = aF(256)
            tmpF = aF(1792)
            o_sb = aF(128)
            o_bf = aB(128)
            junk128 = aF(128)
            self.memset("pool", vh[:, :, 128:129], 1.0, ["vh"])
            for h in range(4):
                wc, wkey = load_w(w_in_v, h * 128, 128)
                proj_fm(wc, wkey, lambda g: self.act(qT[:, g * 512:(g + 1) * 512], pB[:, 0:512], AF.Copy, ["pB"], ["qT"], scale=0.125))
                wc, wkey = load_w(w_in_v, 512 + h * 128, 128)
                proj_fm(wc, wkey, lambda g: self.act(kT[:, g * 512:(g + 1) * 512], pB[:, 0:512], AF.Copy, ["pB"], ["kT"]))
                wc, wkey = load_w(w_in_v, 1024 + h * 128, 128)
                proj_tm(wc, wkey, 128, lambda i: self.act(vh[:, i, 0:128], pC[:, 0:128], AF.Copy, ["pC"], ["vh"]))
                self.chk("p1b_proj")
                for qb in range(NT):
                    if qb == 1:
                        self.chk("p1b_qb0")
                    if qb == 2:
                        self.chk("p1b_qb1")
                    if qb == 3:
                        self.chk("p1b_qb2")
                    Os = (pC, pD)
                    for t in range(2):
                        ps = slice(t * 64, (t + 1) * 64)
                        for kb in range(qb + 1):
                            self.mm(pA[:, kb * 128:(kb + 1) * 128], kT[ps, kb * 128:(kb + 1) * 128], qT[ps, qb * 128:(qb + 1) * 128],
                                    True, True, ["kT", "qT"], ["pA"])
                        nfar = max(qb - 1, 0)
                        if nfar > 0:
                            self.ts("dve", tmpF[:, 0:nfar * 128], pA[:, 0:nfar * 128], cfar[:, h:h + 1], ALU.add, ["pA", "cfar"], ["tmpF"])
                            self.act(PT[:, 0:nfar * 128], tmpF[:, 0:nfar * 128], AF.Exp, ["tmpF"], ["PT"])
                        if qb >= 1:
                            c0 = (qb - 1) * 128
                            self.tt("dve", tmpS[:, 0:256], pA[:, c0:c0 + 256], biasT[:, h, 0:256], ALU.add, ["pA", "biasT"], ["tmpS"])
                            self.act(PT[:, c0:c0 + 256], tmpS[:, 0:256], AF.Exp, ["tmpS"], ["PT"])
                        else:
                            self.tt("dve", tmpS[:, 0:128], pA[:, 0:128], biasT[:, h, 128:256], ALU.add, ["pA", "biasT"], ["tmpS"])
                            self.act(PT[:, 0:128], tmpS[:, 0:128], AF.Exp, ["tmpS"], ["PT"])
                        okey = "pC" if t == 0 else "pD"
                        for kb in range(qb + 1):
                            self.mm(Os[t][:, 0:129], PT[:, kb * 128:(kb + 1) * 128], vh[:, kb, 0:129], kb == 0, kb == qb,
                                    ["PT", "vh"], [okey])
                    self.recip(small[:, 4:5], pC[:, 128:129], ["pC"], ["r1"])
                    self.recip(small[:, 5:6], pD[:, 128:129], ["pD"], ["r2"])
                    self.tt("dve", small[:, 5:6], small[:, 5:6], neglam, ALU.mult, ["r2", "lams"], ["r2"])
                    self.act(o_sb, pC[:, 0:128], AF.Identity, ["pC", "r1"], ["o_sb"], scale=small[:, 4:5])
                    self.stt(o_sb, pD[:, 0:128], small[:, 5:6], o_sb, ALU.mult, ALU.add, ["pD", "r2", "o_sb"], ["o_sb"])
                    self.act(junk128, o_sb, AF.Square, ["o_sb"], ["junk128", "ss"], accum=small[:, 0:1])
                    self.rstd(small[:, 0:1], small[:, 1:2], small[:, 2:3], 1.0 / 128, "ss", "tmpr", "rstd")
                    self.stt(o_bf, o_sb, small[:, 2:3], gsub_bc[:], ALU.mult, ALU.mult, ["o_sb", "rstd", "gsub_bc"], ["o_bf"])
                    self.tr(pT[:, 0:128], o_bf, ident_b[:], ["o_bf", "ident_b"], ["pT"])
                    self.act(mixT[:, h, qb * 128:(qb + 1) * 128], pT[:, 0:128], AF.Copy, ["pT"], ["mixT"])
            p.barrier()

            self.chk("p1b")
            areset()
            mraw = aF(2052)
            cacc = aF(S)
            mqT = aB(S)
            mkT = aB(S)
            mk_tm = aB(S).rearrange("p (i d) -> p i d", d=128)
            Vp = aB(16 * 130).rearrange("p (i e) -> p i e", e=130)
            sigo = aF(S).rearrange("p (i d) -> p i d", d=128)
            gates = aF(128).rearrange("p (i g) -> p i g", g=8)
            ef = aF(64).rearrange("p (i g) -> p i g", g=4)
            logf = aF(64)
            a_t = aF(64).rearrange("p (i g) -> p i g", g=4)
            e_t = aF(64).rearrange("p (i g) -> p i g", g=4)
            ebl_t = aF(64).rearrange("p (i g) -> p i g", g=4)
            tmp64 = aF(64).rearrange("p (i g) -> p i g", g=4)
            C_f = aF(130)
            tmpC = aF(130)
            C_b = aB(130)
            MS = aB(128)
            hm = aF(128)
            hm_bf = aB(128)
            junk128 = aF(128)
            sm2 = aF(8)
            self.memset("pool", mraw[:, 0:4], 0.0, ["mraw"])
            wc, wkey = load_w(w_in_v, 3584, 8)
            for i in range(NT):
                for kc in range(8):
                    self.mm(pB[:, i * 8:(i + 1) * 8], hT[:, kc, i * 128:(i + 1) * 128], wc[:, kc, 0:8], kc == 0, kc == 7, [wkey, "hT"], ["pB"])
            self.tt("dve", gates, pB[:, 0:128].rearrange("p (i g) -> p i g", g=8), cap(bg_bc[:], 0, [[0, 16], [1, 8]]), ALU.add,
                    ["pB", "bg_bc"], ["gates"])
            self.act(ef, gates[:, :, 4:8], AF.Exp, ["gates"], ["ef"], scale=-1.0)
            self.act(ef, ef, AF.Ln, ["ef"], ["ef"], bias=1.0, scale=1.0)
            self.ts("dve", logf, ef.rearrange("p i g -> p (i g)"), -1.0, ALU.mult, ["ef"], ["logf"])
            self.mm(pC[:, 0:64], maskU_f, logf, True, True, ["cst", "logf"], ["pC"])
            self.mm(pC[:, 64:128], ones_f, logf, True, True, ["cst", "logf"], ["pC"])
            self.tt("dve", tmp64, gates[:, :, 0:4], pC[:, 0:64].rearrange("p (i g) -> p i g", g=4), ALU.subtract, ["gates", "pC"], ["tmp64"])
            self.act(a_t, tmp64, AF.Exp, ["tmp64"], ["a_t"])
            self.act(e_t, pC[:, 0:64].rearrange("p (i g) -> p i g", g=4), AF.Exp, ["pC"], ["e_t"])
            self.act(ebl_t, pC[:, 64:128].rearrange("p (i g) -> p i g", g=4), AF.Exp, ["pC"], ["ebl_t"])
            for h in range(4):
                for which, dstT in ((0, mqT), (1, mkT)):
                    cc = which * 4 + h
                    wc, wkey = load_w(w_in_v, 1536 + which * 512 + h * 128, 128)
                    proj_fm(wc, wkey, lambda g: self.act(mraw[:, 3 + g * 512:3 + (g + 1) * 512], pB[:, 0:512], AF.Copy, ["pB"], ["mraw"]))
                    self.ts("dve", cacc, mraw[:, 0:S], convwT[:, cc, 0:1], ALU.mult, ["mraw", "convwT"], ["cacc"])
                    for j in range(1, 4):
                        self.stt(cacc, mraw[:, j:j + S], convwT[:, cc, j:j + 1], cacc, ALU.mult, ALU.add, ["mraw", "convwT", "cacc"], ["cacc"])
                    if which == 0:
                        self.act(mqT, cacc, AF.Silu, ["cacc", "convbT", "cst"], ["mqT"], bias=convbT[:, cc:cc + 1], scale=ones_f[:, 0:1])
                    else:
                        self.act(cacc, cacc, AF.Silu, ["cacc", "convbT", "cst"], ["cacc"], bias=convbT[:, cc:cc + 1], scale=ones_f[:, 0:1])
                        self.ts("dve", mkT, cacc, 128.0 ** -0.5, ALU.mult, ["cacc"], ["mkT"])
                for i0 in range(0, NT, 8):
                    for j in range(8):
                        self.tr(pT[:, j * 128:(j + 1) * 128], mkT[:, (i0 + j) * 128:(i0 + j + 1) * 128], ident_b[:], ["mkT", "ident_b"], ["pT"])
                    self.act(mk_tm[:, i0:i0 + 8, :], pT[:, 0:1024].rearrange("p (i d) -> p i d", d=128), AF.Copy, ["pT"], ["mk_tm"])
                wc, wkey = load_w(w_in_v, 2560 + h * 128, 128)
                proj_tm(wc, wkey, 128, lambda i: self.act(Vp[:, i, 0:128], pC[:, 0:128], AF.Identity, ["pC", "a_t"], ["Vp"], scale=a_t[:, i, h:h + 1]))
                self.cp("dve", Vp[:, :, 128:129], a_t[:, :, h:h + 1], ["a_t"], ["Vp"])
                wc, wkey = load_w(w_in_v, 3072 + h * 128, 128)
                proj_tm(wc, wkey, 128, lambda i: self.act(sigo[:, i, :], pC[:, 0:128], AF.Sigmoid, ["pC"], ["sigo"]))
                for i in range(NT):
                    tl = slice(i * 128, (i + 1) * 128)
                    self.mm(pA[:, 0:128], mkT[:, tl], mqT[:, tl], True, True, ["mkT", "mqT"], ["pA"])
                    self.tt("dve", MS, pA[:, 0:128], maskU_f, ALU.mult, ["pA", "cst"], ["MS"])
                    self.mm(pC[:, 0:129], MS, Vp[:, i, 0:129], True, i == 0, ["MS", "Vp"], ["pC"])
                    if i > 0:
                        self.mm(pC[:, 0:129], mqT[:, tl], C_b[:, 0:129], False, True, ["mqT", "C_b"], ["pC"])
                    if i < NT - 1:
                        self.mm(pD[:, 0:129], mk_tm[:, i, :], Vp[:, i, 0:129], True, True, ["mk_tm", "Vp"], ["pD"])
                        if i == 0:
                            self.cp("dve", tmpC[:, 0:129], pD[:, 0:129], ["pD"], ["tmpC"])
                        else:
                            self.tt("dve", tmpC[:, 0:129], pD[:, 0:129], C_f[:, 0:129], ALU.add, ["pD", "C_f"], ["tmpC"])
                        self.act(C_f[:, 0:129], tmpC[:, 0:129], AF.Identity, ["tmpC", "ebl_t"], ["C_f"], scale=ebl_t[:, i, h:h + 1])
                        self.cp("dve", C_b[:, 0:129], C_f[:, 0:129], ["C_f"], ["C_b"])
                    self.tt("dve", sm2[:, 0:1], pC[:, 128:129], e_t[:, i, h:h + 1], ALU.mult, ["pC", "e_t"], ["sm2"])
                    self.stt(sm2[:, 1:2], sm2[:, 0:1], -1.0, sm2[:, 0:1], ALU.mult, ALU.max, ["sm2"], ["sm2"])
                    self.ts("dve", sm2[:, 1:2], sm2[:, 1:2], 1.0, ALU.max, ["sm2"], ["sm2"])
                    self.recip(sm2[:, 2:3], sm2[:, 1:2], ["sm2"], ["sm2"])
                    self.tt("dve", sm2[:, 3:4], sm2[:, 2:3], e_t[:, i, h:h + 1], ALU.mult, ["sm2", "e_t"], ["sm2"])
                    self.stt(hm, pC[:, 0:128], sm2[:, 3:4], sigo[:, i, :], ALU.mult, ALU.mult, ["pC", "sm2", "sigo"], ["hm"])
                    self.act(junk128, hm, AF.Square, ["hm"], ["junk128", "ss"], accum=small[:, 0:1])
                    self.rstd(small[:, 0:1], small[:, 1:2], small[:, 2:3], 1.0 / 128, "ss", "tmpr", "rstd")
                    self.stt(hm_bf, hm, small[:, 2:3], gm_bc[:, h * 128:(h + 1) * 128], ALU.mult, ALU.mult, ["hm", "rstd", "gm_bc"], ["hm_bf"])
                    self.tr(pT[:, 0:128], hm_bf, ident_b[:], ["hm_bf", "ident_b"], ["pT"])
                    self.act(mixT[:, 4 + h, tl], pT[:, 0:128], AF.Copy, ["pT"], ["mixT"])
            p.barrier()

            self.chk("p1c")
            for blk in range(8):
                load_w(w_q_v, blk * 256, 256, dst=(wq[:, :, blk * 256:(blk + 1) * 256], "wq"))
            p.barrier()
            areset(0)
            pers = [dict(x1=aF(D), h2=aF(D), idx_i=aI(128), gte=aF(128)) for _ in range(2)]
            R = aF(2048)
            xt = R[:, 0:1024]
            qTp = R[:, 0:1024].bitcast(BF16).rearrange("p (c t) -> p c t", t=128)
            sc = R.rearrange("p (c k) -> p c k", k=128)
            cand = R
            eq = R
            wk = aF(2048)
            wk2 = wk
            xn = aB(D)
            h2T = aB(8 * 128).rearrange("p (c t) -> p c t", t=128)
            sv = aF(256)
            si = aU(256)
            sif = aF(256)
            tops = aF(128)
            pos = aU(128)
            pa_u = aU(128)
            pb_u = aU(128)
            paf = aF(128)
            pbf = aF(128)
            i1f = aF(128)
            i2f = aF(128)
            idxf = aF(128)
            tg = aF(128)
            zs = aF(8)
            uvbuf = [aB(2048) for _ in range(NB)]
            prod = [aF(D) for _ in range(2)]
            junkb = aB(D)
            diag = [aB(128) for _ in range(2)]
            dots = aF(128)
            t1 = aF(128)
            t2 = aF(128)
            wgt = aF(128)
            acc_sb = aF(D)
            V = lambda fn, r, w: p.op("dve", fn, reads=r, writes=w)

            def top16_multi(items):
                for (vals, idxs, src, scratch, tag) in items:
                    V(lambda e, o=vals, i_=src: e.max(out=o[:, 0:8], in_=i_), ["src" + tag], ["v" + tag])
                for (vals, idxs, src, scratch, tag) in items:
                    V(lambda e, o=idxs, m=vals, i_=src: e.max_index(out=o[:, 0:8], in_max=m[:, 0:8], in_values=i_), ["src" + tag, "v" + tag], ["i" + tag])
                for (vals, idxs, src, scratch, tag) in items:
                    V(lambda e, o=scratch, m=vals, i_=src: e.match_replace(out=o, in_to_replace=m[:, 0:8], in_values=i_, imm_value=-1e30), ["src" + tag, "v" + tag], ["w" + tag])
                for (vals, idxs, src, scratch, tag) in items:
                    V(lambda e, o=vals, i_=scratch: e.max(out=o[:, 8:16], in_=i_), ["w" + tag], ["v" + tag])
                for (vals, idxs, src, scratch, tag) in items:
                    V(lambda e, o=idxs, m=vals, i_=scratch: e.max_index(out=o[:, 8:16], in_max=m[:, 8:16], in_values=i_), ["w" + tag, "v" + tag], ["i" + tag])

            SA = ["srcA%d" % hp for hp in range(16)]
            VA = ["vA%d" % hp for hp in range(16)]
            IA = ["iA%d" % hp for hp in range(16)]
            SB_ = ["srcB%d" % h for h in range(8)]
            TOPS = ["vB%d" % h for h in range(8)]
            POS = ["iB%d" % h for h in range(8)]

            def front(i, par):
                tl = slice(i * 128, (i + 1) * 128)
                P = pers[par]
                x1, h2, idx_i, gte = P["x1"], P["h2"], P["idx_i"], P["gte"]
                kx1, kh2, kidx, kg = "x1_%d" % par, "h2_%d" % par, "idx_%d" % par, "gte_%d" % par
                self.dma("sp", xt, x_d[b, tl, :], [], ["R"], s_xt)
                for half in range(2):
                    for c in range(8):
                        self.mm(pA[:, half * 512:(half + 1) * 512], mixT[:, c, tl], woutp[:, c, half * 512:(half + 1) * 512],
                                c == 0, c == 7, ["mixT", "woutp"], ["pA"])
                yield
                self.tt("dve", x1, pA[:, 0:D], xt, ALU.add, ["pA", "R"], [kx1])
                self.act(junkb, x1, AF.Square, [kx1], ["junkb", "ss"], accum=small[:, 0:1])
                self.rstd(small[:, 0:1], small[:, 1:2], small[:, 2:3], 1.0 / D, "ss", "tmpr", "rstd")
                self.ts("dve", xn, x1, small[:, 2:3], ALU.mult, [kx1, "rstd"], ["xn"])
                yield
                self.stt(h2, x1, small[:, 2:3], A2_bc[:], ALU.mult, ALU.mult, [kx1, "rstd", "A2_bc"], [kh2])
                self.tt("dve", h2, h2, B2_bc[:], ALU.add, [kh2, "B2_bc"], [kh2])
                for c in range(8):
                    self.tr(pT[:, c * 128:(c + 1) * 128], xn[:, c * 128:(c + 1) * 128], ident_b[:], ["xn", "ident_b"], ["pT"])
                yield
                for c in range(8):
                    self.act(h2T[:, c, :], pT[:, c * 128:(c + 1) * 128], AF.Identity, ["pT", "A2", "modT"], ["h2T"],
                             scale=A2[:, b, c:c + 1], bias=modT[:, 24 + c, b:b + 1])
                yield
                for hp in range(16):
                    for kc in range(8):
                        self.mm(pA[:, hp * 128:(hp + 1) * 128], wq[:, kc, hp * 128:(hp + 1) * 128], h2T[:, kc, :], kc == 0, kc == 7,
                                ["wq", "h2T"], ["pA"])
                    if hp % 4 == 3:
                        yield
                self.act(qTp.rearrange("p c t -> p (c t)"), pA[:, 0:2048], AF.Copy, ["pA"], ["R"])
                for hp in range(16):
                    self.mm(pA[:, hp * 128:(hp + 1) * 128], qTp[:, hp, :], skT[:, hp, :], True, True, ["R", "skT"], ["pA"])
                yield
                self.act(sc.rearrange("p c k -> p (c k)"), pA[:, 0:2048], AF.Copy, ["pA"], ["R"] + SA)
                yield
                itemsA = [(sv[:, hp * 16:(hp + 1) * 16], si[:, hp * 16:(hp + 1) * 16], sc[:, hp, :], wk[:, hp * 128:(hp + 1) * 128], "A%d" % hp)
                          for hp in range(16)]
                top16_multi(itemsA[0:8])
                yield
                top16_multi(itemsA[8:16])
                yield
                self.cp("dve", sif, si, IA, ["sif"])
                cand4 = cand.rearrange("p (h a b) -> p h a b", h=8, a=16)
                self.tt("dve", cand4, cap(sv, 0, [[32, 8], [1, 16], [0, 16]]), cap(sv, 16, [[32, 8], [0, 16], [1, 16]]), ALU.add,
                        VA + IA, ["R"] + SA + SB_)
                yield
                top16_multi([(tops[:, h * 16:(h + 1) * 16], pos[:, h * 16:(h + 1) * 16], cand[:, h * 256:(h + 1) * 256], wk2[:, h * 256:(h + 1) * 256], "B%d" % h)
                             for h in range(8)])
                yield
                V(lambda e, o=pa_u, i_=pos: e.tensor_single_scalar(out=o, in_=i_, scalar=4, op=ALU.logical_shift_right), POS, ["pa_u"])
                V(lambda e, o=pb_u, i_=pos: e.tensor_single_scalar(out=o, in_=i_, scalar=15, op=ALU.bitwise_and), POS, ["pb_u"])
                self.cp("dve", paf, pa_u, ["pa_u"], ["paf"])
                self.cp("dve", pbf, pb_u, ["pb_u"], ["pbf"])
                yield
                eq4 = eq.rearrange("p (h k a) -> p h k a", h=8, k=16)
                for which, pf, outf, key in ((0, paf, i1f, "i1f"), (1, pbf, i2f, "i2f")):
                    self.tt("dve", eq4, cap(pf, 0, [[16, 8], [1, 16], [0, 16]]), cap(iota16, 0, [[0, 8], [0, 16], [1, 16]]), ALU.is_equal,
                            ["paf", "pbf", "cst"] + POS + TOPS, ["R"] + SB_)
                    self.tt("dve", eq4, eq4, cap(sif, which * 16, [[32, 8], [0, 16], [1, 16]]), ALU.mult, ["R", "sif"], ["R"])
                    self.red(outf.rearrange("p (h k) -> p h k", h=8), eq4, ALU.add, ["R"], [key])
                    yield
                self.stt(idxf, i1f, 128.0, i2f, ALU.mult, ALU.add, ["i1f", "i2f"], ["idxf"])
                self.cp("dve", idx_i, idxf, ["idxf"], [kidx])
                tops3 = tops.rearrange("p (h k) -> p h k", h=8)
                tg3 = tg.rearrange("p (h k) -> p h k", h=8)
                self.tt("dve", tg3, tops3, cap(tops, 0, [[16, 8], [0, 16]]), ALU.subtract, TOPS, ["tg"])
                self.act(tg, tg, AF.Exp, ["tg"], ["tg"])
                yield
                self.red(zs, tg3, ALU.add, ["tg"], ["zs"])
                self.recip(zs, zs, ["zs"], ["zs"])
                self.tt("dve", gte.rearrange("p (h k) -> p h k", h=8), tg3, cap(zs, 0, [[1, 8], [0, 16]]), ALU.mult, ["tg", "zs"], [kg])
                if b == 0 and i == 0:
                    dbg_out("x1", x1, [kx1])
                    dbg_out("idxf", idxf, ["idxf"])
                    dbg_out("gte", gte, [kg])
                    dbg_out("h2", h2, [kh2])
                if b == 0 and "x1full" in dbg_d:
                    self.dma("sp", dbg_d["x1full"][tl, :], x1, [kx1], [], s_dbg)
                    self.dma("sp", dbg_d["idxfull"][tl, :], idxf, ["idxf"], [], s_dbg)
                yield

            def back(i, par):
                tl = slice(i * 128, (i + 1) * 128)
                P = pers[par]
                x1, h2, idx_i, gte = P["x1"], P["h2"], P["idx_i"], P["gte"]
                kx1, kh2, kidx, kg = "x1_%d" % par, "h2_%d" % par, "idx_%d" % par, "gte_%d" % par
                NG = 128 // GS

                def stage_a(g):
                    for s in range(g * GS, (g + 1) * GS):
                        j = s % NB
                        uk = "uv%d" % j
                        self.gather(uvbuf[j], uv_dram, idx_i[:, s:s + 1], [kidx], [uk], s_uv[j])
                        pk = "prod%d" % (s % 2)
                        self.tt("dve", prod[s % 2], uvbuf[j][:, 0:1024], h2, ALU.mult, [uk, kh2], [pk])
                        self.act(junkb, prod[s % 2], AF.Identity, [pk], ["junkb", "dots%d" % (g % 4)], accum=dots[:, s:s + 1])

                def stage_b(g):
                    sl = slice(g * GS, (g + 1) * GS)
                    dk_, t1k, t2k, wk_ = "dots%d" % (g % 4), "t1_%d" % (g % 4), "t2_%d" % (g % 4), "wgt%d" % (g % 4)
                    self.tt("dve", t1[:, sl], dots[:, sl], dots[:, sl], ALU.mult, [dk_], [t1k])
                    self.tt("dve", t1[:, sl], t1[:, sl], dots[:, sl], ALU.mult, [t1k, dk_], [t1k])
                    self.stt(t1[:, sl], t1[:, sl], 0.044715, dots[:, sl], ALU.mult, ALU.add, [t1k, dk_], [t1k])
                    self.act(t2[:, sl], t1[:, sl], AF.Tanh, [t1k], [t2k], scale=0.7978845608028654)
                    self.ts("dve", t2[:, sl], t2[:, sl], 1.0, ALU.add, [t2k], [t2k], s2=0.5, op1=ALU.mult)
                    self.tt("dve", t2[:, sl], t2[:, sl], dots[:, sl], ALU.mult, [t2k, dk_], [t2k])
                    self.tt("dve", wgt[:, sl], t2[:, sl], gte[:, sl], ALU.mult, [t2k, kg], [wk_])
                    for s in range(g * GS, (g + 1) * GS):
                        j = s % NB
                        uk = "uv%d" % j
                        dk = "diag%d" % (s % 2)
                        self.act(diag[s % 2], ident_b[:], AF.Identity, ["ident_b", wk_], [dk], scale=wgt[:, s:s + 1])
                        self.mm(pB[:, 0:512], diag[s % 2], uvbuf[j][:, 1024:1536], s == 0, s == 127, [dk, uk], ["pB"])
                        self.mm(pC[:, 0:512], diag[s % 2], uvbuf[j][:, 1536:2048], s == 0, s == 127, [dk, uk], ["pC"])

                for g in range(NG + 1):
                    if g < NG:
                        stage_a(g)
                    if g >= 1:
                        stage_b(g - 1)
                    yield
                if b == 0 and i == 0:
                    dbg_out("dots", dots, ["dots%d" % k_ for k_ in range(4)])
                self.tt("dve", acc_sb[:, 0:512], pB[:, 0:512], g2_bc[:, 0:512], ALU.mult, ["pB", "g2_bc"], ["acc_sb"])
                self.tt("dve", acc_sb[:, 512:1024], pC[:, 0:512], g2_bc[:, 512:1024], ALU.mult, ["pC", "g2_bc"], ["acc_sb"])
                self.tt("dve", acc_sb, acc_sb, x1, ALU.add, ["acc_sb", kx1], ["acc_sb"])
                self.act(junkb, acc_sb, AF.Square, ["acc_sb"], ["junkb", "ssb"], accum=small[:, 8:9])
                self.rstd(small[:, 8:9], small[:, 9:10], small[:, 10:11], 1.0 / D, "ssb", "tmprb", "rstdb")
                self.stt(acc_sb, acc_sb, small[:, 10:11], fin_bc[:], ALU.mult, ALU.mult, ["acc_sb", "rstdb", "fin_bc"], ["acc_sb"])
                self.dma("sp", out_d[b, tl, :], acc_sb, ["acc_sb"], [], s_out)
                yield

            def run_all(gen):
                for _ in gen:
                    pass

            nt2 = self.ntiles2
            run_all(front(0, 0))
            self.chk("p2f")
            for i in range(nt2):
                par = i % 2
                fg = front(i + 1, par ^ 1) if i + 1 < nt2 else None
                if not self.no_back:
                    for _ in back(i, par):
                        if fg is not None:
                            next(fg, None)
                if fg is not None:
                    run_all(fg)
            p.barrier()


def _rel_bucket_np(n):
    n = np.asarray(n)
    max_exact = 16
    nf = np.maximum(n, 1).astype(np.float32)
    large = max_exact + (np.log(nf / np.float32(max_exact)) / np.float32(math.log(128 / max_exact))
                         * np.float32(32 - max_exact)).astype(np.int32)
    large = np.minimum(large, 31)
    return np.where(n < max_exact, n, large)


def _prep_inputs(inp):
    f = lambda a: np.ascontiguousarray(np.asarray(a, dtype=np.float32))
    x = f(inp["x"])
    c = f(inp["c"])
    shared = {}
    shared["w_ada"] = f(inp["w_ada"][0])
    shared["b_ada"] = f(inp["b_ada"][0])
    shared["b_adaT"] = f(inp["b_ada"][0].reshape(48, 128).T)
    shared["n1gT"] = f(inp["norm1_g"][0].reshape(8, 128).T)
    shared["n2gT"] = f(inp["norm2_g"][0].reshape(8, 128).T)
    shared["n2g"] = f(inp["norm2_g"][0])
    shared["w_in"] = f(inp["w_in"][0])
    shared["convwT"] = f(np.asarray(inp["conv_w"][0]).T.reshape(8, 128, 4).transpose(1, 0, 2))
    shared["convbT"] = f(np.asarray(inp["conv_b"][0]).reshape(8, 128).T)
    shared["bgate"] = f(np.concatenate([np.asarray(inp["b_igate"][0]), np.asarray(inp["b_fgate"][0])]))
    shared["lam"] = f(np.stack([np.asarray(inp[k][0]) for k in ("lam_q1", "lam_k1", "lam_q2", "lam_k2")]))
    shared["gsub"] = f(inp["diff_sub_g"][0])
    shared["gm"] = f(inp["mlstm_norm_g"][0])
    shared["w_out"] = f(inp["w_out"][0])
    shared["w_q"] = f(inp["peer_w_q"][0])
    sk = np.asarray(inp["peer_sub_keys"][0])
    shared["skT"] = f(sk.transpose(3, 0, 1, 2).reshape(128, 2048))
    shared["peer_u"] = f(inp["peer_u"][0])
    shared["peer_v"] = f(inp["peer_v"][0])
    rb = f(inp["rel_bias"])
    shared["rel_bias"] = rb
    kk = np.arange(128)[:, None]
    qq = np.arange(128)[None, :]
    rel1 = qq - kk + 128
    rel0 = qq - kk
    b1 = rb[_rel_bucket_np(rel1)]
    b0 = rb[_rel_bucket_np(np.maximum(rel0, 0))]
    biasT = np.empty((128, 4, 256), np.float32)
    biasT[:, :, 0:128] = b1.transpose(0, 2, 1)
    biasT[:, :, 128:256] = b0.transpose(0, 2, 1)
    biasT[:, :, 128:256][np.broadcast_to((rel0 < 0)[:, None, :], (128, 4, 128))] = -1e9
    shared["biasT"] = biasT
    shared["final_g"] = f(inp["final_g"])
    cst = np.zeros((128, 400), np.float32)
    cst[:, 0:128] = np.eye(128)
    cst[:, 128:256] = np.triu(np.ones((128, 128)))
    cst[:, 256:384] = 1.0
    cst[:, 384:400] = np.arange(16)[None, :]
    shared["cst"] = cst
    in_maps = []
    for core in range(8):
        m = dict(shared)
        m["x"] = np.ascontiguousarray(x[2 * core:2 * core + 2])
        cc = c[2 * core:2 * core + 2]
        m["cT"] = np.ascontiguousarray(cc.reshape(2, 8, 128).transpose(2, 1, 0))
        in_maps.append(m)
    return in_maps


_NC_CACHE = {}


def kernel(**inputs):
    in_maps = _prep_inputs(inputs)
    if "nc" not in _NC_CACHE:
        _NC_CACHE["nc"] = KB().build()
    nc = _NC_CACHE["nc"]
    res = run_bass_kernel_spmd(nc, in_maps, core_ids=list(range(8)))
    out = np.concatenate([np.asarray(r["out"]) for r in res.results], axis=0)
    return out.astype(np.float32)
```

```python
import math
import numpy as np
import concourse.bass as bass
import concourse.mybir as mybir
from concourse.bass_utils import run_bass_kernel_spmd

F32 = mybir.dt.float32
BF16 = mybir.dt.bfloat16
U32 = mybir.dt.uint32
I32 = mybir.dt.int32
ALU = mybir.AluOpType
AF = mybir.ActivationFunctionType
AX = mybir.AxisListType

ENGS = ("pe", "act", "dve", "pool", "sp")
SAME_ENGINE_SYNC = {"pe": False, "act": True, "dve": True, "pool": True, "sp": True}

S = 2048
D = 1024
NT = 16
EPS = 1e-6
NU = 3
NV = 3
ARENA_W = 24320
WREG = 6144
GS = 4
NB = 16


class Prog:
    def __init__(self, nc):
        self.nc = nc
        self.q = {e: [] for e in ENGS}
        self.cnt = {}
        self.semobj = {}
        self.lastw = {}
        self.readers = {}
        self.seen = {e: {} for e in ENGS}
        self.ownsem = {}
        for e in ENGS:
            s = nc.alloc_semaphore("es_" + e)
            self.ownsem[e] = s.name
            self.semobj[s.name] = s
            self.cnt[s.name] = 0
        self.nsem = 0

    def new_sem(self, name=None):
        self.nsem += 1
        s = self.nc.alloc_semaphore(name or ("ds_%d" % self.nsem))
        self.semobj[s.name] = s
        self.cnt[s.name] = 0
        return s.name

    def op(self, eng, fn, reads=(), writes=(), sem=None, inc=None):
        if sem is None:
            sem = self.ownsem[eng]
            inc = 1
        elif inc is None:
            inc = 16
        deps = {}

        def add(d):
            if d is None:
                return
            s, v = d
            if deps.get(s, 0) < v:
                deps[s] = v

        for k in reads:
            add(self.lastw.get(k))
        for k in writes:
            add(self.lastw.get(k))
            for s, v in self.readers.get(k, {}).items():
                add((s, v))
        waits = []
        for s, v in deps.items():
            if s == self.ownsem[eng] and not SAME_ENGINE_SYNC[eng]:
                continue
            if self.seen[eng].get(s, 0) >= v:
                continue
            self.seen[eng][s] = v
            waits.append((s, v))
        self.cnt[sem] += inc
        val = self.cnt[sem]
        for k in reads:
            r = self.readers.setdefault(k, {})
            if r.get(sem, 0) < val:
                r[sem] = val
        for k in writes:
            self.lastw[k] = (sem, val)
            self.readers[k] = {}
        self.q[eng].append((waits, fn, sem, inc))

    def barrier(self):
        snap = dict(self.cnt)
        for e in ENGS:
            waits = []
            for s, v in snap.items():
                if v > self.seen[e].get(s, 0):
                    self.seen[e][s] = v
                    waits.append((s, v))
            own = self.ownsem[e]
            self.cnt[own] += 1
            self.q[e].append((waits, (lambda eng: eng.nop()), own, 1))
        self.lastw.clear()
        self.readers.clear()

    def emit(self):
        nc = self.nc
        finals = [(s, v) for s, v in self.cnt.items() if v > 0]
        with nc.Block() as block:
            def run(eng_name, wait_all=False):
                def f(eng):
                    for waits, fn, sem, inc in self.q[eng_name]:
                        for s, v in waits:
                            eng.wait_ge(self.semobj[s], v)
                        ins = fn(eng)
                        ins.then_inc(self.semobj[sem], inc)
                    if wait_all:
                        for s, v in finals:
                            eng.wait_ge(self.semobj[s], v)
                return f
            block.tensor(run("pe"))
            block.scalar(run("act"))
            block.vector(run("dve"))
            block.gpsimd(run("pool"))
            block.sync(run("sp", wait_all=True))


def cap(t, off, dims):
    return bass.AP(t.tensor, t.offset + off, [list(t.ap[0])] + [list(d) for d in dims])


class _Stop(Exception):
    pass


class KB:
    def __init__(self, dbg=None, nseq=2, ntiles2=NT, stop=None, no_back=False):
        self.no_back = no_back
        self.dbg = dbg
        self.stop = stop
        self.nseq = nseq
        self.ntiles2 = ntiles2
        nc = self.nc = bass.Bass("TRN2", target_bir_lowering=False)
        self.p = Prog(nc)
        self._n = 0

    def mm(self, out, lhsT, rhs, start, stop, r, w):
        self.p.op("pe", lambda e: e.matmul(out, lhsT=lhsT, rhs=rhs, start=start, stop=stop), reads=r, writes=w)

    def tr(self, out, in_, ident, r, w):
        self.p.op("pe", lambda e: e.transpose(out=out, in_=in_, identity=ident), reads=r, writes=w)

    def act(self, out, in_, func, r, w, scale=None, bias=None, accum=None):
        kw = {}
        if scale is not None:
            kw["scale"] = scale
        if bias is not None:
            kw["bias"] = bias
        if accum is not None:
            kw["accum_out"] = accum
        self.p.op("act", lambda e: e.activation(out=out, in_=in_, func=func, **kw), reads=r, writes=w)

    def tt(self, eng, out, in0, in1, op, r, w):
        self.p.op(eng, lambda e: e.tensor_tensor(out=out, in0=in0, in1=in1, op=op), reads=r, writes=w)

    def ts(self, eng, out, in0, s1, op0, r, w, s2=None, op1=None):
        if op1 is None:
            self.p.op(eng, lambda e: e.tensor_scalar(out=out, in0=in0, scalar1=s1, scalar2=None, op0=op0), reads=r, writes=w)
        else:
            self.p.op(eng, lambda e: e.tensor_scalar(out=out, in0=in0, scalar1=s1, scalar2=s2, op0=op0, op1=op1), reads=r, writes=w)

    def stt(self, out, in0, scalar, in1, op0, op1, r, w):
        self.p.op("dve", lambda e: e.scalar_tensor_tensor(out=out, in0=in0, scalar=scalar, in1=in1, op0=op0, op1=op1), reads=r, writes=w)

    def cp(self, eng, out, in_, r, w):
        self.p.op(eng, lambda e: e.tensor_copy(out=out, in_=in_), reads=r, writes=w)

    def red(self, out, in_, op, r, w):
        self.p.op("dve", lambda e: e.tensor_reduce(out=out, in_=in_, axis=AX.X, op=op), reads=r, writes=w)

    def recip(self, out, in_, r, w):
        self.p.op("dve", lambda e: e.reciprocal(out=out, in_=in_), reads=r, writes=w)

    def memset(self, eng, ap, val, w):
        self.p.op(eng, lambda e: e.memset(ap, val), writes=w)

    def dma(self, eng, out, in_, r, w, sem):
        self.p.op(eng, lambda e: e.dma_start(out=out, in_=in_), reads=r, writes=w, sem=sem)

    def gather(self, out, table, idx, r, w, sem):
        self.p.op("pool", lambda e: e.indirect_dma_start(
            out=out, out_offset=None, in_=table,
            in_offset=bass.IndirectOffsetOnAxis(ap=idx, axis=0)), reads=r, writes=w, sem=sem)

    def sb(self, shape, dt, name=None):
        self._n += 1
        return self.nc.alloc_sbuf_tensor("sb_" + (name or ("t%d" % self._n)), shape, dt)

    def rstd(self, ss, tmp, out, inv_n, key_ss, key_tmp, key_out):
        self.ts("dve", tmp, ss, inv_n, ALU.mult, [key_ss], [key_tmp], s2=EPS, op1=ALU.add)
        self.act(tmp, tmp, AF.Sqrt, [key_tmp], [key_tmp])
        self.recip(out, tmp, [key_tmp], [key_out])

    def build(self):
        try:
            self._build()
        except _Stop:
            pass
        self.p.emit()
        return self.nc

    def chk(self, tag):
        if self.stop == tag:
            raise _Stop()

    def _build(self):
        nc, p = self.nc, self.p

        def din(name, shape, dt=F32):
            return nc.dram_tensor(name, shape, dt, kind="ExternalInput").ap()

        x_d = din("x", [2, S, D])
        cT_d = din("cT", [128, 8, 2])
        w_ada_d = din("w_ada", [D, 6 * D])
        b_ada_d = din("b_ada", [6 * D])
        b_adaT_d = din("b_adaT", [128, 48])
        n1gT_d = din("n1gT", [128, 8])
        n2gT_d = din("n2gT", [128, 8])
        n2g_d = din("n2g", [D])
        w_in_d = din("w_in", [D, 3592])
        convwT_d = din("convwT", [128, 8, 4])
        convbT_d = din("convbT", [128, 8])
        bgate_d = din("bgate", [8])
        lam_d = din("lam", [4, 64])
        gsub_d = din("gsub", [128])
        gm_d = din("gm", [512])
        w_out_d = din("w_out", [D, D])
        w_q_d = din("w_q", [D, 2048])
        skT_d = din("skT", [128, 2048])
        u_d = din("peer_u", [16384, D])
        v_d = din("peer_v", [16384, D])
        relb_d = din("rel_bias", [32, 4])
        biasT_d = din("biasT", [128, 4, 256])
        fin_d = din("final_g", [D])
        cst_d = din("cst", [128, 400])
        out_d = nc.dram_tensor("out", [2, S, D], F32, kind="ExternalOutput").ap()
        dbg_d = {}
        if self.dbg:
            for name, shape in self.dbg.items():
                dbg_d[name] = nc.dram_tensor("dbg_" + name, shape, F32, kind="ExternalOutput").ap()

        w_ada_v = w_ada_d.rearrange("(c p) n -> p c n", p=128)
        w_in_v = w_in_d.rearrange("(c p) n -> p c n", p=128)
        w_out_v = w_out_d.rearrange("(c p) n -> p c n", p=128)
        w_q_v = w_q_d.rearrange("(c p) n -> p c n", p=128)

        sb = self.sb
        cst = sb([128, 400], F32, "cst")
        ident_f = cst[:, 0:128]
        maskU_f = cst[:, 128:256]
        ones_f = cst[:, 256:384]
        iota16 = cst[:, 384:400]
        ident_b = sb([128, 128], BF16, "ident_b")
        biasT = sb([128, 4, 256], F32, "biasT")
        cfar = sb([128, 4], F32, "cfar")
        condT = sb([128, 8, 2], F32, "condT")
        modT = sb([128, 48, 2], F32, "modT")
        b_adaT = sb([128, 48], F32, "b_adaT")
        n1gT = sb([128, 8], F32, "n1gT")
        n2gT = sb([128, 8], F32, "n2gT")
        convwT = sb([128, 8, 4], F32, "convwT")
        convbT = sb([128, 8], F32, "convbT")
        A1 = sb([128, 2, 8], F32, "A1")
        A2 = sb([128, 2, 8], F32, "A2")
        fin_bc = sb([128, D], F32, "fin_bc")
        gsub_bc = sb([128, 128], F32, "gsub_bc")
        gm_bc = sb([128, 512], F32, "gm_bc")
        bg_bc = sb([128, 8], F32, "bg_bc")
        A2_bc = sb([128, D], F32, "A2_bc")
        B2_bc = sb([128, D], F32, "B2_bc")
        g2_bc = sb([128, D], F32, "g2_bc")
        lamv = sb([128, 4, 64], F32, "lamv")
        lams = sb([128, 8], F32, "lams")
        small = sb([128, 16], F32, "small")
        mixT = sb([128, 8, S], BF16, "mixT")
        arena1 = sb([128, 8, S], BF16, "arena1")
        hT = arena1
        wq = arena1
        woutp = sb([128, 8, D], BF16, "woutp")
        skT = sb([128, 16, 128], BF16, "skT")
        arena = sb([128, ARENA_W], F32, "arena")
        wst = [arena[:, i * 2048:(i + 1) * 2048].rearrange("p (c n) -> p c n", c=8) for i in range(2)]
        wbf = [arena[:, 4096 + i * 1024:4096 + (i + 1) * 1024].bitcast(BF16).rearrange("p (c n) -> p c n", c=8) for i in range(2)]

        pA = nc.alloc_psum_tensor("pA", [128, 2048], F32)
        pT = nc.alloc_psum_tensor("pT", [128, 1024], BF16)
        pB = nc.alloc_psum_tensor("pB", [128, 512], F32)
        pC = nc.alloc_psum_tensor("pC", [128, 512], F32)
        pD = nc.alloc_psum_tensor("pD", [128, 512], F32)

        s_c = p.new_sem("s_const")
        s_xt = p.new_sem("s_xt")
        s_w = [p.new_sem("s_w0"), p.new_sem("s_w1")]
        s_uv = [p.new_sem("s_uv%d" % i) for i in range(NB)]
        s_pu = [p.new_sem("s_pu%d" % i) for i in range(2)]
        s_pv = [p.new_sem("s_pv%d" % i) for i in range(2)]
        s_ps = [p.new_sem("s_ps%d" % i) for i in range(2)]
        s_xts = [p.new_sem("s_xts%d" % i) for i in range(2)]
        s_x1s = [p.new_sem("s_x1s%d" % i) for i in range(2)]
        s_x1 = [p.new_sem("s_x1_%d" % i) for i in range(2)]
        x1_dram = nc.dram_tensor("x1_scratch", [S, D], F32).ap()
        s_out = p.new_sem("s_out")
        s_dbg = p.new_sem("s_dbg")

        ar = {"off": 0}

        def areset(off=WREG):
            ar["off"] = off

        def aF(n):
            o = ar["off"]
            ar["off"] += n
            assert ar["off"] <= ARENA_W, ar["off"]
            return arena[:, o:o + n]

        def aB(n):
            w = (n + 1) // 2
            return aF(w).bitcast(BF16)[:, 0:n]

        def aU(n):
            return aF(n).bitcast(U32)

        def aI(n):
            return aF(n).bitcast(I32)

        wstate = {"k": 0}

        def load_w(dram_v, c0, ncols, scale_bc=None, dst=None):
            k = wstate["k"]
            wstate["k"] ^= 1
            self.dma("sp", wst[k][:, :, 0:ncols], dram_v[:, :, c0:c0 + ncols], [], ["wst%d" % k], s_w[k])
            if dst is None:
                dst = wbf[k][:, :, 0:ncols]
                key = "wbf%d" % k
            else:
                dst, key = dst
            if scale_bc is None:
                self.cp("pool", dst, wst[k][:, :, 0:ncols], ["wst%d" % k], [key])
            else:
                bc, bkey = scale_bc
                self.tt("pool", dst, wst[k][:, :, 0:ncols], cap(bc, 0, [[0, 8], [1, ncols]]), ALU.mult,
                        ["wst%d" % k, bkey], [key])
            return dst, key

        def dbg_out(name, src, keys):
            if name in dbg_d:
                self.dma("sp", dbg_d[name], src, keys, [], s_dbg)

        cl = lambda o, i, w: self.dma("sp", o, i, [], [w], s_c)
        cl(cst[:], cst_d, "cst")
        cl(biasT[:], biasT_d, "biasT")
        cl(cfar[:], relb_d[31, :].partition_broadcast(128), "cfar")
        cl(condT[:], cT_d, "condT")
        cl(b_adaT[:], b_adaT_d, "b_adaT")
        cl(n1gT[:], n1gT_d, "n1gT")
        cl(n2gT[:], n2gT_d, "n2gT")
        cl(convwT[:], convwT_d, "convwT")
        cl(convbT[:], convbT_d, "convbT")
        cl(fin_bc[:], fin_d.partition_broadcast(128), "fin_bc")
        cl(gsub_bc[:], gsub_d.partition_broadcast(128), "gsub_bc")
        cl(gm_bc[:], gm_d.partition_broadcast(128), "gm_bc")
        cl(bg_bc[:], bgate_d.partition_broadcast(128), "bg_bc")
        for i in range(4):
            cl(lamv[:, i, :], lam_d[i, :].partition_broadcast(128), "lamv")
        uv_dram = nc.dram_tensor("uv_scratch", [16384, 2048], BF16).ap()
        areset()
        ust = [aF(D) for _ in range(2)]
        vst = [aF(D) for _ in range(2)]
        uvb = [aB(2048) for _ in range(2)]
        for blk in range(128):
            k = blk % 2
            rows = slice(blk * 128, (blk + 1) * 128)
            self.dma("sp", ust[k], u_d[rows, :], [], ["ust%d" % k], s_pu[k])
            self.dma("sp", vst[k], v_d[rows, :], [], ["vst%d" % k], s_pv[k])
            self.act(uvb[k][:, 0:1024], ust[k], AF.Copy, ["ust%d" % k], ["uvbA%d" % k])
            self.cp("dve", uvb[k][:, 1024:2048], vst[k], ["vst%d" % k], ["uvbB%d" % k])
            self.dma("pool", uv_dram[rows, :], uvb[k], ["uvbA%d" % k, "uvbB%d" % k], [], s_ps[k])
        p.barrier()
        areset()
        mod_rows = aF(6144)
        brow = aF(6144)
        cl(brow[0:2, :], b_ada_d.partition_broadcast(2), "brow")
        skst = wst[0].rearrange("p c n -> p (c n)")
        cl(skst, skT_d, "wst0")
        p.barrier()
        self.cp("pool", skT[:].rearrange("p c n -> p (c n)"), skst, ["wst0"], ["skT"])
        self.act(condT[:], condT[:], AF.Silu, ["condT"], ["condT"])
        self.cp("dve", ident_b[:], ident_f, ["cst"], ["ident_b"])
        self.ts("dve", gsub_bc[:], gsub_bc[:], 0.8, ALU.mult, ["gsub_bc"], ["gsub_bc"])
        junk64 = aF(64)
        for j in range(2):
            self.tt("dve", junk64, lamv[:, 2 * j, :], lamv[:, 2 * j + 1, :], ALU.mult, ["lamv"], ["junk64"])
            self.red(lams[:, j:j + 1], junk64, ALU.add, ["junk64"], ["lams"])
        self.act(lams[:, 2:4], lams[:, 0:2], AF.Exp, ["lams"], ["lams"])
        self.tt("dve", lams[:, 4:5], lams[:, 3:4], lams[:, 2:3], ALU.subtract, ["lams"], ["lams"])
        self.ts("dve", lams[:, 4:5], lams[:, 4:5], -0.2, ALU.add, ["lams"], ["lams"])
        neglam = lams[:, 4:5]
        for blk in range(24):
            k = wstate["k"]
            wstate["k"] ^= 1
            wk_ = "wst%d" % k
            self.dma("sp", wst[k], w_ada_v[:, :, blk * 256:(blk + 1) * 256], [], [wk_], s_w[k])
            for kc in range(8):
                self.mm(pB[0:2, 0:256], condT[:, kc, :], wst[k][:, kc, :], kc == 0, kc == 7, ["condT", wk_], ["pB"])
            self.tt("dve", mod_rows[0:2, blk * 256:(blk + 1) * 256], pB[0:2, 0:256], brow[0:2, blk * 256:(blk + 1) * 256],
                    ALU.add, ["pB", "brow"], ["mod_rows"])
            for jl in range(2):
                j = blk * 2 + jl
                for kc in range(8):
                    self.mm(pC[:, 2 * j:2 * j + 2], wst[k][:, kc, jl * 128:(jl + 1) * 128], condT[:, kc, :],
                            kc == 0, kc == 7, ["condT", wk_], ["pC"])
        self.tt("dve", modT[:], pC[:, 0:96].rearrange("p (j b) -> p j b", b=2), cap(b_adaT[:], 0, [[1, 48], [0, 2]]),
                ALU.add, ["pC", "b_adaT"], ["modT"])
        for b in range(2):
            self.stt(A1[:, b, :], modT[:, 8:16, b], 1.0, n1gT[:], ALU.add, ALU.mult, ["modT", "n1gT"], ["A1"])
            self.stt(A2[:, b, :], modT[:, 32:40, b], 1.0, n2gT[:], ALU.add, ALU.mult, ["modT", "n2gT"], ["A2"])
        if "modT" in dbg_d:
            dbg_out("modT", modT[:].rearrange("p j b -> p (j b)"), ["modT"])
        mod_dram = nc.dram_tensor("mod_scratch", [2, 6144], F32).ap()
        self.dma("sp", mod_dram, mod_rows[0:2, :], ["mod_rows"], [], s_c)
        p.barrier()

        self.chk("p0")
        for b in range(self.nseq):
            areset()
            g1_bc = aF(D)
            n2g_bc = aF(D)
            self.dma("sp", n2g_bc, n2g_d.partition_broadcast(128), [], ["n2g_bc"], s_c)
            for dst, key, col0 in ((g1_bc, "g1_bc", 2048), (B2_bc[:], "B2_bc", 3072), (A2_bc[:], "A2_bc", 4096), (g2_bc[:], "g2_bc", 5120)):
                self.dma("sp", dst, mod_dram[b, col0:col0 + 1024].partition_broadcast(128), [], [key], s_c)
            p.barrier()
            self.stt(A2_bc[:], A2_bc[:], 1.0, n2g_bc, ALU.add, ALU.mult, ["A2_bc", "n2g_bc"], ["A2_bc"])
            for blk in range(4):
                load_w(w_out_v, blk * 256, 256, scale_bc=(g1_bc[:, blk * 256:(blk + 1) * 256], "g1_bc"),
                       dst=(woutp[:, :, blk * 256:(blk + 1) * 256], "woutp"))
            p.barrier()

            areset()
            xt = aF(D)
            junk = aF(D)
            xn = aB(D)
            for i in range(NT):
                self.dma("sp", xt, x_d[b, i * 128:(i + 1) * 128, :], [], ["xt"], s_xt)
                self.act(junk, xt, AF.Square, ["xt"], ["junk", "ss"], accum=small[:, 0:1])
                self.rstd(small[:, 0:1], small[:, 1:2], small[:, 2:3], 1.0 / D, "ss", "tmpr", "rstd")
                self.ts("dve", xn, xt, small[:, 2:3], ALU.mult, ["xt", "rstd"], ["xn"])
                for c in range(8):
                    self.tr(pT[:, c * 128:(c + 1) * 128], xn[:, c * 128:(c + 1) * 128], ident_b[:], ["xn", "ident_b"], ["pT"])
                for c in range(8):
                    self.act(hT[:, c, i * 128:(i + 1) * 128], pT[:, c * 128:(c + 1) * 128], AF.Identity, ["pT", "A1", "modT"], ["hT"],
                             scale=A1[:, b, c:c + 1], bias=modT[:, c, b:b + 1])
            p.barrier()

            self.chk("p1a")

            def proj_fm(wcols, wkey, evac):
                for g in range(4):
                    for kc in range(8):
                        self.mm(pB[:, 0:512], wcols[:, kc, :], hT[:, kc, g * 512:(g + 1) * 512], kc == 0, kc == 7,
                                [wkey, "hT"], ["pB"])
                    evac(g)

            def proj_tm(wcols, wkey, ncols, evac):
                for i in range(NT):
                    for kc in range(8):
                        self.mm(pC[:, 0:ncols], hT[:, kc, i * 128:(i + 1) * 128], wcols[:, kc, 0:ncols], kc == 0, kc == 7,
                                [wkey, "hT"], ["pC"])
                    evac(i)

            areset()
            qT = aB(S)
            kT = aB(S)
            vh = aB(16 * 130).rearrange("p (i e) -> p i e", e=130)
            PT = aB(S)
            tmpS = aF(256)
            tmpF = aF(1792)
            o_sb = aF(128)
            o_bf = aB(128)
            junk128 = aF(128)
            self.memset("pool", vh[:, :, 128:129], 1.0, ["vh"])
            for h in range(4):
                wc, wkey = load_w(w_in_v, h * 128, 128)
                proj_fm(wc, wkey, lambda g: self.act(qT[:, g * 512:(g + 1) * 512], pB[:, 0:512], AF.Copy, ["pB"], ["qT"], scale=0.125))
                wc, wkey = load_w(w_in_v, 512 + h * 128, 128)
                proj_fm(wc, wkey, lambda g: self.act(kT[:, g * 512:(g + 1) * 512], pB[:, 0:512], AF.Copy, ["pB"], ["kT"]))
                wc, wkey = load_w(w_in_v, 1024 + h * 128, 128)
                proj_tm(wc, wkey, 128, lambda i: self.act(vh[:, i, 0:128], pC[:, 0:128], AF.Copy, ["pC"], ["vh"]))
                self.chk("p1b_proj")
                for qb in range(NT):
                    if qb == 1:
                        self.chk("p1b_qb0")
                    if qb == 2:
                        self.chk("p1b_qb1")
                    if qb == 3:
                        self.chk("p1b_qb2")
                    Os = (pC, pD)
                    for t in range(2):
                        ps = slice(t * 64, (t + 1) * 64)
                        for kb in range(qb + 1):
                            self.mm(pA[:, kb * 128:(kb + 1) * 128], kT[ps, kb * 128:(kb + 1) * 128], qT[ps, qb * 128:(qb + 1) * 128],
                                    True, True, ["kT", "qT"], ["pA"])
                        nfar = max(qb - 1, 0)
                        if nfar > 0:
                            self.ts("dve", tmpF[:, 0:nfar * 128], pA[:, 0:nfar * 128], cfar[:, h:h + 1], ALU.add, ["pA", "cfar"], ["tmpF"])
                            self.act(PT[:, 0:nfar * 128], tmpF[:, 0:nfar * 128], AF.Exp, ["tmpF"], ["PT"])
                        if qb >= 1:
                            c0 = (qb - 1) * 128
                            self.tt("dve", tmpS[:, 0:256], pA[:, c0:c0 + 256], biasT[:, h, 0:256], ALU.add, ["pA", "biasT"], ["tmpS"])
                            self.act(PT[:, c0:c0 + 256], tmpS[:, 0:256], AF.Exp, ["tmpS"], ["PT"])
                        else:
                            self.tt("dve", tmpS[:, 0:128], pA[:, 0:128], biasT[:, h, 128:256], ALU.add, ["pA", "biasT"], ["tmpS"])
                            self.act(PT[:, 0:128], tmpS[:, 0:128], AF.Exp, ["tmpS"], ["PT"])
                        okey = "pC" if t == 0 else "pD"
                        for kb in range(qb + 1):
                            self.mm(Os[t][:, 0:129], PT[:, kb * 128:(kb + 1) * 128], vh[:, kb, 0:129], kb == 0, kb == qb,
                                    ["PT", "vh"], [okey])
                    self.recip(small[:, 4:5], pC[:, 128:129], ["pC"], ["r1"])
                    self.recip(small[:, 5:6], pD[:, 128:129], ["pD"], ["r2"])
                    self.tt("dve", small[:, 5:6], small[:, 5:6], neglam, ALU.mult, ["r2", "lams"], ["r2"])
                    self.act(o_sb, pC[:, 0:128], AF.Identity, ["pC", "r1"], ["o_sb"], scale=small[:, 4:5])
                    self.stt(o_sb, pD[:, 0:128], small[:, 5:6], o_sb, ALU.mult, ALU.add, ["pD", "r2", "o_sb"], ["o_sb"])
                    self.act(junk128, o_sb, AF.Square, ["o_sb"], ["junk128", "ss"], accum=small[:, 0:1])
                    self.rstd(small[:, 0:1], small[:, 1:2], small[:, 2:3], 1.0 / 128, "ss", "tmpr", "rstd")
                    self.stt(o_bf, o_sb, small[:, 2:3], gsub_bc[:], ALU.mult, ALU.mult, ["o_sb", "rstd", "gsub_bc"], ["o_bf"])
                    self.tr(pT[:, 0:128], o_bf, ident_b[:], ["o_bf", "ident_b"], ["pT"])
                    self.act(mixT[:, h, qb * 128:(qb + 1) * 128], pT[:, 0:128], AF.Copy, ["pT"], ["mixT"])
            p.barrier()

            self.chk("p1b")
            areset()
            mraw = aF(2052)
            cacc = aF(S)
            mqT = aB(S)
            mkT = aB(S)
            mk_tm = aB(S).rearrange("p (i d) -> p i d", d=128)
            Vp = aB(16 * 130).rearrange("p (i e) -> p i e", e=130)
            sigo = aF(S).rearrange("p (i d) -> p i d", d=128)
            gates = aF(128).rearrange("p (i g) -> p i g", g=8)
            ef = aF(64).rearrange("p (i g) -> p i g", g=4)
            logf = aF(64)
            a_t = aF(64).rearrange("p (i g) -> p i g", g=4)
            e_t = aF(64).rearrange("p (i g) -> p i g", g=4)
            ebl_t = aF(64).rearrange("p (i g) -> p i g", g=4)
            tmp64 = aF(64).rearrange("p (i g) -> p i g", g=4)
            C_f = aF(130)
            tmpC = aF(130)
            C_b = aB(130)
            MS = aB(128)
            hm = aF(128)
            hm_bf = aB(128)
            junk128 = aF(128)
            sm2 = aF(8)
            self.memset("pool", mraw[:, 0:4], 0.0, ["mraw"])
            wc, wkey = load_w(w_in_v, 3584, 8)
            for i in range(NT):
                for kc in range(8):
                    self.mm(pB[:, i * 8:(i + 1) * 8], hT[:, kc, i * 128:(i + 1) * 128], wc[:, kc, 0:8], kc == 0, kc == 7, [wkey, "hT"], ["pB"])
            self.tt("dve", gates, pB[:, 0:128].rearrange("p (i g) -> p i g", g=8), cap(bg_bc[:], 0, [[0, 16], [1, 8]]), ALU.add,
                    ["pB", "bg_bc"], ["gates"])
            self.act(ef, gates[:, :, 4:8], AF.Exp, ["gates"], ["ef"], scale=-1.0)
            self.act(ef, ef, AF.Ln, ["ef"], ["ef"], bias=1.0, scale=1.0)
            self.ts("dve", logf, ef.rearrange("p i g -> p (i g)"), -1.0, ALU.mult, ["ef"], ["logf"])
            self.mm(pC[:, 0:64], maskU_f, logf, True, True, ["cst", "logf"], ["pC"])
            self.mm(pC[:, 64:128], ones_f, logf, True, True, ["cst", "logf"], ["pC"])
            self.tt("dve", tmp64, gates[:, :, 0:4], pC[:, 0:64].rearrange("p (i g) -> p i g", g=4), ALU.subtract, ["gates", "pC"], ["tmp64"])
            self.act(a_t, tmp64, AF.Exp, ["tmp64"], ["a_t"])
            self.act(e_t, pC[:, 0:64].rearrange("p (i g) -> p i g", g=4), AF.Exp, ["pC"], ["e_t"])
            self.act(ebl_t, pC[:, 64:128].rearrange("p (i g) -> p i g", g=4), AF.Exp, ["pC"], ["ebl_t"])
            for h in range(4):
                for which, dstT in ((0, mqT), (1, mkT)):
                    cc = which * 4 + h
                    wc, wkey = load_w(w_in_v, 1536 + which * 512 + h * 128, 128)
                    proj_fm(wc, wkey, lambda g: self.act(mraw[:, 3 + g * 512:3 + (g + 1) * 512], pB[:, 0:512], AF.Copy, ["pB"], ["mraw"]))
                    self.ts("dve", cacc, mraw[:, 0:S], convwT[:, cc, 0:1], ALU.mult, ["mraw", "convwT"], ["cacc"])
                    for j in range(1, 4):
                        self.stt(cacc, mraw[:, j:j + S], convwT[:, cc, j:j + 1], cacc, ALU.mult, ALU.add, ["mraw", "convwT", "cacc"], ["cacc"])
                    if which == 0:
                        self.act(mqT, cacc, AF.Silu, ["cacc", "convbT", "cst"], ["mqT"], bias=convbT[:, cc:cc + 1], scale=ones_f[:, 0:1])
                    else:
                        self.act(cacc, cacc, AF.Silu, ["cacc", "convbT", "cst"], ["cacc"], bias=convbT[:, cc:cc + 1], scale=ones_f[:, 0:1])
                        self.ts("dve", mkT, cacc, 128.0 ** -0.5, ALU.mult, ["cacc"], ["mkT"])
                for i0 in range(0, NT, 8):
                    for j in range(8):
                        self.tr(pT[:, j * 128:(j + 1) * 128], mkT[:, (i0 + j) * 128:(i0 + j + 1) * 128], ident_b[:], ["mkT", "ident_b"], ["pT"])
                    self.act(mk_tm[:, i0:i0 + 8, :], pT[:, 0:1024].rearrange("p (i d) -> p i d", d=128), AF.Copy, ["pT"], ["mk_tm"])
                wc, wkey = load_w(w_in_v, 2560 + h * 128, 128)
                proj_tm(wc, wkey, 128, lambda i: self.act(Vp[:, i, 0:128], pC[:, 0:128], AF.Identity, ["pC", "a_t"], ["Vp"], scale=a_t[:, i, h:h + 1]))
                self.cp("dve", Vp[:, :, 128:129], a_t[:, :, h:h + 1], ["a_t"], ["Vp"])
                wc, wkey = load_w(w_in_v, 3072 + h * 128, 128)
                proj_tm(wc, wkey, 128, lambda i: self.act(sigo[:, i, :], pC[:, 0:128], AF.Sigmoid, ["pC"], ["sigo"]))
                for i in range(NT):
                    tl = slice(i * 128, (i + 1) * 128)
                    self.mm(pA[:, 0:128], mkT[:, tl], mqT[:, tl], True, True, ["mkT", "mqT"], ["pA"])
                    self.tt("dve", MS, pA[:, 0:128], maskU_f, ALU.mult, ["pA", "cst"], ["MS"])
                    self.mm(pC[:, 0:129], MS, Vp[:, i, 0:129], True, i == 0, ["MS", "Vp"], ["pC"])
                    if i > 0:
                        self.mm(pC[:, 0:129], mqT[:, tl], C_b[:, 0:129], False, True, ["mqT", "C_b"], ["pC"])
                    if i < NT - 1:
                        self.mm(pD[:, 0:129], mk_tm[:, i, :], Vp[:, i, 0:129], True, True, ["mk_tm", "Vp"], ["pD"])
                        if i == 0:
                            self.cp("dve", tmpC[:, 0:129], pD[:, 0:129], ["pD"], ["tmpC"])
                        else:
                            self.tt("dve", tmpC[:, 0:129], pD[:, 0:129], C_f[:, 0:129], ALU.add, ["pD", "C_f"], ["tmpC"])
                        self.act(C_f[:, 0:129], tmpC[:, 0:129], AF.Identity, ["tmpC", "ebl_t"], ["C_f"], scale=ebl_t[:, i, h:h + 1])
                        self.cp("dve", C_b[:, 0:129], C_f[:, 0:129], ["C_f"], ["C_b"])
                    self.tt("dve", sm2[:, 0:1], pC[:, 128:129], e_t[:, i, h:h + 1], ALU.mult, ["pC", "e_t"], ["sm2"])
                    self.stt(sm2[:, 1:2], sm2[:, 0:1], -1.0, sm2[:, 0:1], ALU.mult, ALU.max, ["sm2"], ["sm2"])
                    self.ts("dve", sm2[:, 1:2], sm2[:, 1:2], 1.0, ALU.max, ["sm2"], ["sm2"])
                    self.recip(sm2[:, 2:3], sm2[:, 1:2], ["sm2"], ["sm2"])
                    self.tt("dve", sm2[:, 3:4], sm2[:, 2:3], e_t[:, i, h:h + 1], ALU.mult, ["sm2", "e_t"], ["sm2"])
                    self.stt(hm, pC[:, 0:128], sm2[:, 3:4], sigo[:, i, :], ALU.mult, ALU.mult, ["pC", "sm2", "sigo"], ["hm"])
                    self.act(junk128, hm, AF.Square, ["hm"], ["junk128", "ss"], accum=small[:, 0:1])
                    self.rstd(small[:, 0:1], small[:, 1:2], small[:, 2:3], 1.0 / 128, "ss", "tmpr", "rstd")
                    self.stt(hm_bf, hm, small[:, 2:3], gm_bc[:, h * 128:(h + 1) * 128], ALU.mult, ALU.mult, ["hm", "rstd", "gm_bc"], ["hm_bf"])
                    self.tr(pT[:, 0:128], hm_bf, ident_b[:], ["hm_bf", "ident_b"], ["pT"])
                    self.act(mixT[:, 4 + h, tl], pT[:, 0:128], AF.Copy, ["pT"], ["mixT"])
            p.barrier()

            self.chk("p1c")
            areset()
            xts = [aF(D) for _ in range(2)]
            x1s = [aF(D) for _ in range(2)]
            for i in range(NT):
                k = i % 2
                tl = slice(i * 128, (i + 1) * 128)
                pcol = k * 1024
                self.dma("sp", xts[k], x_d[b, tl, :], [], ["xts%d" % k], s_xts[k])
                for half in range(2):
                    for c in range(8):
                        self.mm(pA[:, pcol + half * 512:pcol + (half + 1) * 512], mixT[:, c, tl], woutp[:, c, half * 512:(half + 1) * 512],
                                c == 0, c == 7, ["mixT", "woutp"], ["pA%d" % k])
                self.tt("dve", x1s[k], pA[:, pcol:pcol + D], xts[k], ALU.add, ["pA%d" % k, "xts%d" % k], ["x1s%d" % k])
                self.dma("sp", x1_dram[tl, :], x1s[k], ["x1s%d" % k], [], s_x1s[k])
            p.barrier()
            for blk in range(8):
                load_w(w_q_v, blk * 256, 256, dst=(wq[:, :, blk * 256:(blk + 1) * 256], "wq"))
            p.barrier()
            areset(0)
            pers = [dict(x1=aF(D), h2=aF(D), idx_i=aI(128), gte=aF(128)) for _ in range(2)]
            R = aF(2048)
            xt = R[:, 0:1024]
            qTp = R[:, 0:1024].bitcast(BF16).rearrange("p (c t) -> p c t", t=128)
            sc = R.rearrange("p (c k) -> p c k", k=128)
            cand = R
            eq = R
            wk = aF(2048)
            wk2 = wk
            xn = aB(D)
            h2T = aB(8 * 128).rearrange("p (c t) -> p c t", t=128)
            sv = aF(256)
            si = aU(256)
            sif = aF(256)
            tops = aF(128)
            pos = aU(128)
            pa_u = aU(128)
            pb_u = aU(128)
            paf = aF(128)
            pbf = aF(128)
            i1f = aF(128)
            i2f = aF(128)
            idxf = aF(128)
            tg = aF(128)
            zs = aF(8)
            uvbuf = [mixT[:, j, :] for j in range(8)] + \
                    [woutp[:, 2 * j:2 * j + 2, :].rearrange("p c n -> p (c n)") for j in range(4)] + \
                    [aB(2048) for _ in range(NB - 12)]
            prod = [aF(D) for _ in range(2)]
            junkb = aB(D)
            diag = [aB(128) for _ in range(2)]
            dots = aF(128)
            t1 = aF(128)
            t2 = aF(128)
            wgt = aF(128)
            acc_sb = aF(D)
            V = lambda fn, r, w: p.op("dve", fn, reads=r, writes=w)

            def top16_multi(items):
                for (vals, idxs, src, scratch, tag) in items:
                    V(lambda e, o=vals, i_=src: e.max(out=o[:, 0:8], in_=i_), ["src" + tag], ["v" + tag])
                for (vals, idxs, src, scratch, tag) in items:
                    V(lambda e, o=idxs, m=vals, i_=src: e.max_index(out=o[:, 0:8], in_max=m[:, 0:8], in_values=i_), ["src" + tag, "v" + tag], ["i" + tag])
                for (vals, idxs, src, scratch, tag) in items:
                    V(lambda e, o=scratch, m=vals, i_=src: e.match_replace(out=o, in_to_replace=m[:, 0:8], in_values=i_, imm_value=-1e30), ["src" + tag, "v" + tag], ["w" + tag])
                for (vals, idxs, src, scratch, tag) in items:
                    V(lambda e, o=vals, i_=scratch: e.max(out=o[:, 8:16], in_=i_), ["w" + tag], ["v" + tag])
                for (vals, idxs, src, scratch, tag) in items:
                    V(lambda e, o=idxs, m=vals, i_=scratch: e.max_index(out=o[:, 8:16], in_max=m[:, 8:16], in_values=i_), ["w" + tag, "v" + tag], ["i" + tag])

            SA = ["srcA%d" % hp for hp in range(16)]
            VA = ["vA%d" % hp for hp in range(16)]
            IA = ["iA%d" % hp for hp in range(16)]
            SB_ = ["srcB%d" % h for h in range(8)]
            TOPS = ["vB%d" % h for h in range(8)]
            POS = ["iB%d" % h for h in range(8)]

            def front(i, par):
                tl = slice(i * 128, (i + 1) * 128)
                P = pers[par]
                x1, h2, idx_i, gte = P["x1"], P["h2"], P["idx_i"], P["gte"]
                kx1, kh2, kidx, kg = "x1_%d" % par, "h2_%d" % par, "idx_%d" % par, "gte_%d" % par
                self.dma("sp", x1, x1_dram[tl, :], [], [kx1], s_x1[par])
                yield
                self.act(junkb, x1, AF.Square, [kx1], ["junkb", "ss"], accum=small[:, 0:1])
                yield
                self.ts("dve", small[:, 1:2], small[:, 0:1], 1.0 / D, ALU.mult, ["ss"], ["tmpr"], s2=EPS, op1=ALU.add)
                yield
                self.act(small[:, 1:2], small[:, 1:2], AF.Sqrt, ["tmpr"], ["tmpr"])
                yield
                self.recip(small[:, 2:3], small[:, 1:2], ["tmpr"], ["rstd"])
                self.ts("dve", xn, x1, small[:, 2:3], ALU.mult, [kx1, "rstd"], ["xn"])
                self.stt(h2, x1, small[:, 2:3], A2_bc[:], ALU.mult, ALU.mult, [kx1, "rstd", "A2_bc"], [kh2])
                self.tt("dve", h2, h2, B2_bc[:], ALU.add, [kh2, "B2_bc"], [kh2])
                yield
                for c in range(8):
                    self.tr(pT[:, c * 128:(c + 1) * 128], xn[:, c * 128:(c + 1) * 128], ident_b[:], ["xn", "ident_b"], ["pT"])
                yield
                for c in range(8):
                    self.act(h2T[:, c, :], pT[:, c * 128:(c + 1) * 128], AF.Identity, ["pT", "A2", "modT"], ["h2T"],
                             scale=A2[:, b, c:c + 1], bias=modT[:, 24 + c, b:b + 1])
                yield
                for hp in range(16):
                    for kc in range(8):
                        self.mm(pA[:, hp * 128:(hp + 1) * 128], wq[:, kc, hp * 128:(hp + 1) * 128], h2T[:, kc, :], kc == 0, kc == 7,
                                ["wq", "h2T"], ["pA"])
                    if hp % 4 == 3:
                        yield
                self.act(qTp.rearrange("p c t -> p (c t)"), pA[:, 0:2048], AF.Copy, ["pA"], ["R"])
                yield
                for hp in range(16):
                    self.mm(pA[:, hp * 128:(hp + 1) * 128], qTp[:, hp, :], skT[:, hp, :], True, True, ["R", "skT"], ["pA"])
                yield
                self.act(sc.rearrange("p c k -> p (c k)"), pA[:, 0:2048], AF.Copy, ["pA"], ["R"] + SA)
                yield
                itemsA = [(sv[:, hp * 16:(hp + 1) * 16], si[:, hp * 16:(hp + 1) * 16], sc[:, hp, :], wk[:, hp * 128:(hp + 1) * 128], "A%d" % hp)
                          for hp in range(16)]
                top16_multi(itemsA[0:8])
                yield
                top16_multi(itemsA[8:16])
                yield
                self.cp("dve", sif, si, IA, ["sif"])
                cand4 = cand.rearrange("p (h a b) -> p h a b", h=8, a=16)
                self.tt("dve", cand4, cap(sv, 0, [[32, 8], [1, 16], [0, 16]]), cap(sv, 16, [[32, 8], [0, 16], [1, 16]]), ALU.add,
                        VA + IA, ["R"] + SA + SB_)
                yield
                top16_multi([(tops[:, h * 16:(h + 1) * 16], pos[:, h * 16:(h + 1) * 16], cand[:, h * 256:(h + 1) * 256], wk2[:, h * 256:(h + 1) * 256], "B%d" % h)
                             for h in range(8)])
                yield
                V(lambda e, o=pa_u, i_=pos: e.tensor_single_scalar(out=o, in_=i_, scalar=4, op=ALU.logical_shift_right), POS, ["pa_u"])
                V(lambda e, o=pb_u, i_=pos: e.tensor_single_scalar(out=o, in_=i_, scalar=15, op=ALU.bitwise_and), POS, ["pb_u"])
                self.cp("dve", paf, pa_u, ["pa_u"], ["paf"])
                self.cp("dve", pbf, pb_u, ["pb_u"], ["pbf"])
                yield
                eq4 = eq.rearrange("p (h k a) -> p h k a", h=8, k=16)
                for which, pf, outf, key in ((0, paf, i1f, "i1f"), (1, pbf, i2f, "i2f")):
                    self.tt("dve", eq4, cap(pf, 0, [[16, 8], [1, 16], [0, 16]]), cap(iota16, 0, [[0, 8], [0, 16], [1, 16]]), ALU.is_equal,
                            ["paf", "pbf", "cst"] + POS + TOPS, ["R"] + SB_)
                    self.tt("dve", eq4, eq4, cap(sif, which * 16, [[32, 8], [0, 16], [1, 16]]), ALU.mult, ["R", "sif"], ["R"])
                    self.red(outf.rearrange("p (h k) -> p h k", h=8), eq4, ALU.add, ["R"], [key])
                    yield
                self.stt(idxf, i1f, 128.0, i2f, ALU.mult, ALU.add, ["i1f", "i2f"], ["idxf"])
                self.cp("dve", idx_i, idxf, ["idxf"], [kidx])
                tops3 = tops.rearrange("p (h k) -> p h k", h=8)
                tg3 = tg.rearrange("p (h k) -> p h k", h=8)
                self.tt("dve", tg3, tops3, cap(tops, 0, [[16, 8], [0, 16]]), ALU.subtract, TOPS, ["tg"])
                yield
                self.act(tg, tg, AF.Exp, ["tg"], ["tg"])
                yield
                self.red(zs, tg3, ALU.add, ["tg"], ["zs"])
                self.recip(zs, zs, ["zs"], ["zs"])
                self.tt("dve", gte.rearrange("p (h k) -> p h k", h=8), tg3, cap(zs, 0, [[1, 8], [0, 16]]), ALU.mult, ["tg", "zs"], [kg])
                if b == 0 and i == 0:
                    dbg_out("x1", x1, [kx1])
                    dbg_out("idxf", idxf, ["idxf"])
                    dbg_out("gte", gte, [kg])
                    dbg_out("h2", h2, [kh2])
                if b == 0 and "x1full" in dbg_d:
                    self.dma("sp", dbg_d["x1full"][tl, :], x1, [kx1], [], s_dbg)
                    self.dma("sp", dbg_d["idxfull"][tl, :], idxf, ["idxf"], [], s_dbg)
                yield

            def back(i, par):
                tl = slice(i * 128, (i + 1) * 128)
                P = pers[par]
                x1, h2, idx_i, gte = P["x1"], P["h2"], P["idx_i"], P["gte"]
                kx1, kh2, kidx, kg = "x1_%d" % par, "h2_%d" % par, "idx_%d" % par, "gte_%d" % par
                NG = 128 // GS

                def stage_a(g):
                    for s in range(g * GS, (g + 1) * GS):
                        j = s % NB
                        uk = "uv%d" % j
                        self.gather(uvbuf[j], uv_dram, idx_i[:, s:s + 1], [kidx], [uk], s_uv[j])
                        pk = "prod%d" % (s % 2)
                        self.tt("dve", prod[s % 2], uvbuf[j][:, 0:1024], h2, ALU.mult, [uk, kh2], [pk])
                        self.act(junkb, prod[s % 2], AF.Identity, [pk], ["junkb", "dots%d" % (g % 4)], accum=dots[:, s:s + 1])

                def stage_b(g):
                    sl = slice(g * GS, (g + 1) * GS)
                    dk_, t1k, t2k, wk_ = "dots%d" % (g % 4), "t1_%d" % (g % 4), "t2_%d" % (g % 4), "wgt%d" % (g % 4)
                    self.tt("dve", t1[:, sl], dots[:, sl], dots[:, sl], ALU.mult, [dk_], [t1k])
                    self.tt("dve", t1[:, sl], t1[:, sl], dots[:, sl], ALU.mult, [t1k, dk_], [t1k])
                    self.stt(t1[:, sl], t1[:, sl], 0.044715, dots[:, sl], ALU.mult, ALU.add, [t1k, dk_], [t1k])
                    self.act(t2[:, sl], t1[:, sl], AF.Tanh, [t1k], [t2k], scale=0.7978845608028654)
                    self.ts("dve", t2[:, sl], t2[:, sl], 1.0, ALU.add, [t2k], [t2k], s2=0.5, op1=ALU.mult)
                    self.tt("dve", t2[:, sl], t2[:, sl], dots[:, sl], ALU.mult, [t2k, dk_], [t2k])
                    self.tt("dve", wgt[:, sl], t2[:, sl], gte[:, sl], ALU.mult, [t2k, kg], [wk_])
                    for s in range(g * GS, (g + 1) * GS):
                        j = s % NB
                        uk = "uv%d" % j
                        dk = "diag%d" % (s % 2)
                        self.act(diag[s % 2], ident_b[:], AF.Identity, ["ident_b", wk_], [dk], scale=wgt[:, s:s + 1])
                        self.mm(pB[:, 0:512], diag[s % 2], uvbuf[j][:, 1024:1536], s == 0, s == 127, [dk, uk], ["pB"])
                        self.mm(pC[:, 0:512], diag[s % 2], uvbuf[j][:, 1536:2048], s == 0, s == 127, [dk, uk], ["pC"])

                for g in range(NG + 1):
                    if g < NG:
                        stage_a(g)
                    if g >= 1:
                        stage_b(g - 1)
                    yield
                if b == 0 and i == 0:
                    dbg_out("dots", dots, ["dots%d" % k_ for k_ in range(4)])
                self.tt("dve", acc_sb[:, 0:512], pB[:, 0:512], g2_bc[:, 0:512], ALU.mult, ["pB", "g2_bc"], ["acc_sb"])
                self.tt("dve", acc_sb[:, 512:1024], pC[:, 0:512], g2_bc[:, 512:1024], ALU.mult, ["pC", "g2_bc"], ["acc_sb"])
                self.tt("dve", acc_sb, acc_sb, x1, ALU.add, ["acc_sb", kx1], ["acc_sb"])
                self.act(junkb, acc_sb, AF.Square, ["acc_sb"], ["junkb", "ssb"], accum=small[:, 8:9])
                self.rstd(small[:, 8:9], small[:, 9:10], small[:, 10:11], 1.0 / D, "ssb", "tmprb", "rstdb")
                self.stt(acc_sb, acc_sb, small[:, 10:11], fin_bc[:], ALU.mult, ALU.mult, ["acc_sb", "rstdb", "fin_bc"], ["acc_sb"])
                self.dma("sp", out_d[b, tl, :], acc_sb, ["acc_sb"], [], s_out)
                yield

            def run_all(gen):
                for _ in gen:
                    pass

            nt2 = self.ntiles2
            run_all(front(0, 0))
            self.chk("p2f")
            for i in range(nt2):
                par = i % 2
                fg = front(i + 1, par ^ 1) if i + 1 < nt2 else None
                if not self.no_back:
                    for _ in back(i, par):
                        if fg is not None:
                            next(fg, None)
                if fg is not None:
                    run_all(fg)
            p.barrier()


def _rel_bucket_np(n):
    n = np.asarray(n)
    max_exact = 16
    nf = np.maximum(n, 1).astype(np.float32)
    large = max_exact + (np.log(nf / np.float32(max_exact)) / np.float32(math.log(128 / max_exact))
                         * np.float32(32 - max_exact)).astype(np.int32)
    large = np.minimum(large, 31)
    return np.where(n < max_exact, n, large)


def _prep_inputs(inp):
    f = lambda a: np.ascontiguousarray(np.asarray(a, dtype=np.float32))
    x = f(inp["x"])
    c = f(inp["c"])
    shared = {}
    shared["w_ada"] = f(inp["w_ada"][0])
    shared["b_ada"] = f(inp["b_ada"][0])
    shared["b_adaT"] = f(inp["b_ada"][0].reshape(48, 128).T)
    shared["n1gT"] = f(inp["norm1_g"][0].reshape(8, 128).T)
    shared["n2gT"] = f(inp["norm2_g"][0].reshape(8, 128).T)
    shared["n2g"] = f(inp["norm2_g"][0])
    shared["w_in"] = f(inp["w_in"][0])
    shared["convwT"] = f(np.asarray(inp["conv_w"][0]).T.reshape(8, 128, 4).transpose(1, 0, 2))
    shared["convbT"] = f(np.asarray(inp["conv_b"][0]).reshape(8, 128).T)
    shared["bgate"] = f(np.concatenate([np.asarray(inp["b_igate"][0]), np.asarray(inp["b_fgate"][0])]))
    shared["lam"] = f(np.stack([np.asarray(inp[k][0]) for k in ("lam_q1", "lam_k1", "lam_q2", "lam_k2")]))
    shared["gsub"] = f(inp["diff_sub_g"][0])
    shared["gm"] = f(inp["mlstm_norm_g"][0])
    shared["w_out"] = f(inp["w_out"][0])
    shared["w_q"] = f(inp["peer_w_q"][0])
    sk = np.asarray(inp["peer_sub_keys"][0])
    shared["skT"] = f(sk.transpose(3, 0, 1, 2).reshape(128, 2048))
    shared["peer_u"] = f(inp["peer_u"][0])
    shared["peer_v"] = f(inp["peer_v"][0])
    rb = f(inp["rel_bias"])
    shared["rel_bias"] = rb
    kk = np.arange(128)[:, None]
    qq = np.arange(128)[None, :]
    rel1 = qq - kk + 128
    rel0 = qq - kk
    b1 = rb[_rel_bucket_np(rel1)]
    b0 = rb[_rel_bucket_np(np.maximum(rel0, 0))]
    biasT = np.empty((128, 4, 256), np.float32)
    biasT[:, :, 0:128] = b1.transpose(0, 2, 1)
    biasT[:, :, 128:256] = b0.transpose(0, 2, 1)
    biasT[:, :, 128:256][np.broadcast_to((rel0 < 0)[:, None, :], (128, 4, 128))] = -1e9
    shared["biasT"] = biasT
    shared["final_g"] = f(inp["final_g"])
    cst = np.zeros((128, 400), np.float32)
    cst[:, 0:128] = np.eye(128)
    cst[:, 128:256] = np.triu(np.ones((128, 128)))
    cst[:, 256:384] = 1.0
    cst[:, 384:400] = np.arange(16)[None, :]
    shared["cst"] = cst
    in_maps = []
    for core in range(8):
        m = dict(shared)
        m["x"] = np.ascontiguousarray(x[2 * core:2 * core + 2])
        cc = c[2 * core:2 * core + 2]
        m["cT"] = np.ascontiguousarray(cc.reshape(2, 8, 128).transpose(2, 1, 0))
        in_maps.append(m)
    return in_maps


_NC_CACHE = {}


def kernel(**inputs):
    in_maps = _prep_inputs(inputs)
    if "nc" not in _NC_CACHE:
        _NC_CACHE["nc"] = KB().build()
    nc = _NC_CACHE["nc"]
    res = run_bass_kernel_spmd(nc, in_maps, core_ids=list(range(8)))
    out = np.concatenate([np.asarray(r["out"]) for r in res.results], axis=0)
    return out.astype(np.float32)
```

```python
import math
import numpy as np
import concourse.bass as bass
import concourse.mybir as mybir
from concourse.bass_utils import run_bass_kernel_spmd

F32 = mybir.dt.float32
BF16 = mybir.dt.bfloat16
U32 = mybir.dt.uint32
I32 = mybir.dt.int32
ALU = mybir.AluOpType
AF = mybir.ActivationFunctionType
AX = mybir.AxisListType

ENGS = ("pe", "act", "dve", "pool", "sp")
SAME_ENGINE_SYNC = {"pe": False, "act": True, "dve": True, "pool": True, "sp": True}

S = 2048
D = 1024
NT = 16
EPS = 1e-6
NU = 3
NV = 3
ARENA_W = 24320
WREG = 6144
GS = 4
NB = 16


class Prog:
    def __init__(self, nc):
        self.nc = nc
        self.q = {e: [] for e in ENGS}
        self.cnt = {}
        self.semobj = {}
        self.lastw = {}
        self.readers = {}
        self.seen = {e: {} for e in ENGS}
        self.ownsem = {}
        for e in ENGS:
            s = nc.alloc_semaphore("es_" + e)
            self.ownsem[e] = s.name
            self.semobj[s.name] = s
            self.cnt[s.name] = 0
        self.nsem = 0

    def new_sem(self, name=None):
        self.nsem += 1
        s = self.nc.alloc_semaphore(name or ("ds_%d" % self.nsem))
        self.semobj[s.name] = s
        self.cnt[s.name] = 0
        return s.name

    def op(self, eng, fn, reads=(), writes=(), sem=None, inc=None):
        if sem is None:
            sem = self.ownsem[eng]
            inc = 1
        elif inc is None:
            inc = 16
        deps = {}

        def add(d):
            if d is None:
                return
            s, v = d
            if deps.get(s, 0) < v:
                deps[s] = v

        for k in reads:
            add(self.lastw.get(k))
        for k in writes:
            add(self.lastw.get(k))
            for s, v in self.readers.get(k, {}).items():
                add((s, v))
        waits = []
        for s, v in deps.items():
            if s == self.ownsem[eng] and not SAME_ENGINE_SYNC[eng]:
                continue
            if self.seen[eng].get(s, 0) >= v:
                continue
            self.seen[eng][s] = v
            waits.append((s, v))
        self.cnt[sem] += inc
        val = self.cnt[sem]
        for k in reads:
            r = self.readers.setdefault(k, {})
            if r.get(sem, 0) < val:
                r[sem] = val
        for k in writes:
            self.lastw[k] = (sem, val)
            self.readers[k] = {}
        self.q[eng].append((waits, fn, sem, inc))

    def barrier(self):
        snap = dict(self.cnt)
        for e in ENGS:
            waits = []
            for s, v in snap.items():
                if v > self.seen[e].get(s, 0):
                    self.seen[e][s] = v
                    waits.append((s, v))
            own = self.ownsem[e]
            self.cnt[own] += 1
            self.q[e].append((waits, (lambda eng: eng.nop()), own, 1))
        self.lastw.clear()
        self.readers.clear()

    def emit(self):
        nc = self.nc
        finals = [(s, v) for s, v in self.cnt.items() if v > 0]
        with nc.Block() as block:
            def run(eng_name, wait_all=False):
                def f(eng):
                    for waits, fn, sem, inc in self.q[eng_name]:
                        for s, v in waits:
                            eng.wait_ge(self.semobj[s], v)
                        ins = fn(eng)
                        ins.then_inc(self.semobj[sem], inc)
                    if wait_all:
                        for s, v in finals:
                            eng.wait_ge(self.semobj[s], v)
                return f
            block.tensor(run("pe"))
            block.scalar(run("act"))
            block.vector(run("dve"))
            block.gpsimd(run("pool"))
            block.sync(run("sp", wait_all=True))


def cap(t, off, dims):
    return bass.AP(t.tensor, t.offset + off, [list(t.ap[0])] + [list(d) for d in dims])


class _Stop(Exception):
    pass


class KB:
    def __init__(self, dbg=None, nseq=2, ntiles2=NT, stop=None, no_back=False):
        self.no_back = no_back
        self.dbg = dbg
        self.stop = stop
        self.nseq = nseq
        self.ntiles2 = ntiles2
        nc = self.nc = bass.Bass("TRN2", target_bir_lowering=False)
        self.p = Prog(nc)
        self._n = 0

    def mm(self, out, lhsT, rhs, start, stop, r, w):
        self.p.op("pe", lambda e: e.matmul(out, lhsT=lhsT, rhs=rhs, start=start, stop=stop), reads=r, writes=w)

    def tr(self, out, in_, ident, r, w):
        self.p.op("pe", lambda e: e.transpose(out=out, in_=in_, identity=ident), reads=r, writes=w)

    def act(self, out, in_, func, r, w, scale=None, bias=None, accum=None):
        kw = {}
        if scale is not None:
            kw["scale"] = scale
        if bias is not None:
            kw["bias"] = bias
        if accum is not None:
            kw["accum_out"] = accum
        self.p.op("act", lambda e: e.activation(out=out, in_=in_, func=func, **kw), reads=r, writes=w)

    def tt(self, eng, out, in0, in1, op, r, w):
        self.p.op(eng, lambda e: e.tensor_tensor(out=out, in0=in0, in1=in1, op=op), reads=r, writes=w)

    def ts(self, eng, out, in0, s1, op0, r, w, s2=None, op1=None):
        if op1 is None:
            self.p.op(eng, lambda e: e.tensor_scalar(out=out, in0=in0, scalar1=s1, scalar2=None, op0=op0), reads=r, writes=w)
        else:
            self.p.op(eng, lambda e: e.tensor_scalar(out=out, in0=in0, scalar1=s1, scalar2=s2, op0=op0, op1=op1), reads=r, writes=w)

    def stt(self, out, in0, scalar, in1, op0, op1, r, w):
        self.p.op("dve", lambda e: e.scalar_tensor_tensor(out=out, in0=in0, scalar=scalar, in1=in1, op0=op0, op1=op1), reads=r, writes=w)

    def cp(self, eng, out, in_, r, w):
        self.p.op(eng, lambda e: e.tensor_copy(out=out, in_=in_), reads=r, writes=w)

    def red(self, out, in_, op, r, w):
        self.p.op("dve", lambda e: e.tensor_reduce(out=out, in_=in_, axis=AX.X, op=op), reads=r, writes=w)

    def recip(self, out, in_, r, w):
        self.p.op("dve", lambda e: e.reciprocal(out=out, in_=in_), reads=r, writes=w)

    def memset(self, eng, ap, val, w):
        self.p.op(eng, lambda e: e.memset(ap, val), writes=w)

    def dma(self, eng, out, in_, r, w, sem):
        self.p.op(eng, lambda e: e.dma_start(out=out, in_=in_), reads=r, writes=w, sem=sem)

    def gather(self, out, table, idx, r, w, sem):
        self.p.op("pool", lambda e: e.indirect_dma_start(
            out=out, out_offset=None, in_=table,
            in_offset=bass.IndirectOffsetOnAxis(ap=idx, axis=0)), reads=r, writes=w, sem=sem)

    def sb(self, shape, dt, name=None):
        self._n += 1
        return self.nc.alloc_sbuf_tensor("sb_" + (name or ("t%d" % self._n)), shape, dt)

    def rstd(self, ss, tmp, out, inv_n, key_ss, key_tmp, key_out):
        self.ts("dve", tmp, ss, inv_n, ALU.mult, [key_ss], [key_tmp], s2=EPS, op1=ALU.add)
        self.act(tmp, tmp, AF.Sqrt, [key_tmp], [key_tmp])
        self.recip(out, tmp, [key_tmp], [key_out])

    def build(self):
        try:
            self._build()
        except _Stop:
            pass
        self.p.emit()
        return self.nc

    def chk(self, tag):
        if self.stop == tag:
            raise _Stop()

    def _build(self):
        nc, p = self.nc, self.p

        def din(name, shape, dt=F32):
            return nc.dram_tensor(name, shape, dt, kind="ExternalInput").ap()

        x_d = din("x", [2, S, D])
        cT_d = din("cT", [128, 8, 2])
        w_ada_d = din("w_ada", [D, 6 * D])
        b_ada_d = din("b_ada", [6 * D])
        b_adaT_d = din("b_adaT", [128, 48])
        n1gT_d = din("n1gT", [128, 8])
        n2gT_d = din("n2gT", [128, 8])
        n2g_d = din("n2g", [D])
        w_in_d = din("w_in", [D, 3592])
        convwT_d = din("convwT", [128, 8, 4])
        convbT_d = din("convbT", [128, 8])
        bgate_d = din("bgate", [8])
        lam_d = din("lam", [4, 64])
        gsub_d = din("gsub", [128])
        gm_d = din("gm", [512])
        w_out_d = din("w_out", [D, D])
        w_q_d = din("w_q", [D, 2048])
        skT_d = din("skT", [128, 2048])
        u_d = din("peer_u", [16384, D])
        v_d = din("peer_v", [16384, D])
        relb_d = din("rel_bias", [32, 4])
        biasT_d = din("biasT", [128, 4, 256])
        fin_d = din("final_g", [D])
        cst_d = din("cst", [128, 400])
        out_d = nc.dram_tensor("out", [2, S, D], F32, kind="ExternalOutput").ap()
        dbg_d = {}
        if self.dbg:
            for name, shape in self.dbg.items():
                dbg_d[name] = nc.dram_tensor("dbg_" + name, shape, F32, kind="ExternalOutput").ap()

        w_ada_v = w_ada_d.rearrange("(c p) n -> p c n", p=128)
        w_in_v = w_in_d.rearrange("(c p) n -> p c n", p=128)
        w_out_v = w_out_d.rearrange("(c p) n -> p c n", p=128)
        w_q_v = w_q_d.rearrange("(c p) n -> p c n", p=128)

        sb = self.sb
        cst = sb([128, 400], F32, "cst")
        ident_f = cst[:, 0:128]
        maskU_f = cst[:, 128:256]
        ones_f = cst[:, 256:384]
        iota16 = cst[:, 384:400]
        ident_b = sb([128, 128], BF16, "ident_b")
        biasT = sb([128, 4, 256], F32, "biasT")
        cfar = sb([128, 4], F32, "cfar")
        condT = sb([128, 8, 2], F32, "condT")
        modT = sb([128, 48, 2], F32, "modT")
        b_adaT = sb([128, 48], F32, "b_adaT")
        n1gT = sb([128, 8], F32, "n1gT")
        n2gT = sb([128, 8], F32, "n2gT")
        convwT = sb([128, 8, 4], F32, "convwT")
        convbT = sb([128, 8], F32, "convbT")
        A1 = sb([128, 2, 8], F32, "A1")
        A2 = sb([128, 2, 8], F32, "A2")
        fin_bc = sb([128, D], F32, "fin_bc")
        gsub_bc = sb([128, 128], F32, "gsub_bc")
        gm_bc = sb([128, 512], F32, "gm_bc")
        bg_bc = sb([128, 8], F32, "bg_bc")
        A2_bc = sb([128, D], F32, "A2_bc")
        B2_bc = sb([128, D], F32, "B2_bc")
        g2_bc = sb([128, D], F32, "g2_bc")
        lamv = sb([128, 4, 64], F32, "lamv")
        lams = sb([128, 8], F32, "lams")
        small = sb([128, 16], F32, "small")
        mixT = sb([128, 8, S], BF16, "mixT")
        arena1 = sb([128, 8, S], BF16, "arena1")
        hT = arena1
        wq = arena1
        woutp = sb([128, 8, D], BF16, "woutp")
        skT = sb([128, 16, 128], BF16, "skT")
        arena = sb([128, ARENA_W], F32, "arena")
        wst = [arena[:, i * 2048:(i + 1) * 2048].rearrange("p (c n) -> p c n", c=8) for i in range(2)]
        wbf = [arena[:, 4096 + i * 1024:4096 + (i + 1) * 1024].bitcast(BF16).rearrange("p (c n) -> p c n", c=8) for i in range(2)]

        pA = nc.alloc_psum_tensor("pA", [128, 2048], F32)
        pT = nc.alloc_psum_tensor("pT", [128, 1024], BF16)
        pB = nc.alloc_psum_tensor("pB", [128, 512], F32)
        pC = nc.alloc_psum_tensor("pC", [128, 512], F32)
        pD = nc.alloc_psum_tensor("pD", [128, 512], F32)

        s_c = p.new_sem("s_const")
        s_xt = p.new_sem("s_xt")
        s_w = [p.new_sem("s_w0"), p.new_sem("s_w1")]
        s_uv = [p.new_sem("s_uv%d" % i) for i in range(NB)]
        s_pu = [p.new_sem("s_pu%d" % i) for i in range(2)]
        s_pv = [p.new_sem("s_pv%d" % i) for i in range(2)]
        s_ps = [p.new_sem("s_ps%d" % i) for i in range(2)]
        s_xts = [p.new_sem("s_xts%d" % i) for i in range(2)]
        s_x1s = [p.new_sem("s_x1s%d" % i) for i in range(2)]
        s_x1 = [p.new_sem("s_x1_%d" % i) for i in range(2)]
        x1_dram = nc.dram_tensor("x1_scratch", [S, D], F32).ap()
        s_out = p.new_sem("s_out")
        s_dbg = p.new_sem("s_dbg")

        ar = {"off": 0}

        def areset(off=WREG):
            ar["off"] = off

        def aF(n):
            o = ar["off"]
            ar["off"] += n
            assert ar["off"] <= ARENA_W, ar["off"]
            return arena[:, o:o + n]

        def aB(n):
            w = (n + 1) // 2
            return aF(w).bitcast(BF16)[:, 0:n]

        def aU(n):
            return aF(n).bitcast(U32)

        def aI(n):
            return aF(n).bitcast(I32)

        wstate = {"k": 0}

        def load_w(dram_v, c0, ncols, scale_bc=None, dst=None):
            k = wstate["k"]
            wstate["k"] ^= 1
            self.dma("sp", wst[k][:, :, 0:ncols], dram_v[:, :, c0:c0 + ncols], [], ["wst%d" % k], s_w[k])
            if dst is None:
                dst = wbf[k][:, :, 0:ncols]
                key = "wbf%d" % k
            else:
                dst, key = dst
            if scale_bc is None:
                self.cp("pool", dst, wst[k][:, :, 0:ncols], ["wst%d" % k], [key])
            else:
                bc, bkey = scale_bc
                self.tt("pool", dst, wst[k][:, :, 0:ncols], cap(bc, 0, [[0, 8], [1, ncols]]), ALU.mult,
                        ["wst%d" % k, bkey], [key])
            return dst, key

        def dbg_out(name, src, keys):
            if name in dbg_d:
                self.dma("sp", dbg_d[name], src, keys, [], s_dbg)

        cl = lambda o, i, w: self.dma("sp", o, i, [], [w], s_c)
        cl(cst[:], cst_d, "cst")
        cl(biasT[:], biasT_d, "biasT")
        cl(cfar[:], relb_d[31, :].partition_broadcast(128), "cfar")
        cl(condT[:], cT_d, "condT")
        cl(b_adaT[:], b_adaT_d, "b_adaT")
        cl(n1gT[:], n1gT_d, "n1gT")
        cl(n2gT[:], n2gT_d, "n2gT")
        cl(convwT[:], convwT_d, "convwT")
        cl(convbT[:], convbT_d, "convbT")
        cl(fin_bc[:], fin_d.partition_broadcast(128), "fin_bc")
        cl(gsub_bc[:], gsub_d.partition_broadcast(128), "gsub_bc")
        cl(gm_bc[:], gm_d.partition_broadcast(128), "gm_bc")
        cl(bg_bc[:], bgate_d.partition_broadcast(128), "bg_bc")
        for i in range(4):
            cl(lamv[:, i, :], lam_d[i, :].partition_broadcast(128), "lamv")
        uv_dram = nc.dram_tensor("uv_scratch", [16384, 2048], BF16).ap()
        areset()
        ust = [aF(D) for _ in range(2)]
        vst = [aF(D) for _ in range(2)]
        uvb = [aB(2048) for _ in range(2)]
        for blk in range(128):
            k = blk % 2
            rows = slice(blk * 128, (blk + 1) * 128)
            self.dma("sp", ust[k], u_d[rows, :], [], ["ust%d" % k], s_pu[k])
            self.dma("sp", vst[k], v_d[rows, :], [], ["vst%d" % k], s_pv[k])
            self.act(uvb[k][:, 0:1024], ust[k], AF.Copy, ["ust%d" % k], ["uvbA%d" % k])
            self.cp("dve", uvb[k][:, 1024:2048], vst[k], ["vst%d" % k], ["uvbB%d" % k])
            self.dma("pool", uv_dram[rows, :], uvb[k], ["uvbA%d" % k, "uvbB%d" % k], [], s_ps[k])
        p.barrier()
        areset()
        mod_rows = aF(6144)
        brow = aF(6144)
        cl(brow[0:2, :], b_ada_d.partition_broadcast(2), "brow")
        skst = wst[0].rearrange("p c n -> p (c n)")
        cl(skst, skT_d, "wst0")
        p.barrier()
        self.cp("pool", skT[:].rearrange("p c n -> p (c n)"), skst, ["wst0"], ["skT"])
        self.act(condT[:], condT[:], AF.Silu, ["condT"], ["condT"])
        self.cp("dve", ident_b[:], ident_f, ["cst"], ["ident_b"])
        for h_ in range(4):
            self.ts("dve", biasT[:, h_, :], biasT[:, h_, :], cfar[:, h_:h_ + 1], ALU.subtract, ["biasT", "cfar"], ["biasT"])
        self.ts("dve", gsub_bc[:], gsub_bc[:], 0.8, ALU.mult, ["gsub_bc"], ["gsub_bc"])
        junk64 = aF(64)
        for j in range(2):
            self.tt("dve", junk64, lamv[:, 2 * j, :], lamv[:, 2 * j + 1, :], ALU.mult, ["lamv"], ["junk64"])
            self.red(lams[:, j:j + 1], junk64, ALU.add, ["junk64"], ["lams"])
        self.act(lams[:, 2:4], lams[:, 0:2], AF.Exp, ["lams"], ["lams"])
        self.tt("dve", lams[:, 4:5], lams[:, 3:4], lams[:, 2:3], ALU.subtract, ["lams"], ["lams"])
        self.ts("dve", lams[:, 4:5], lams[:, 4:5], -0.2, ALU.add, ["lams"], ["lams"])
        neglam = lams[:, 4:5]
        for blk in range(24):
            k = wstate["k"]
            wstate["k"] ^= 1
            wk_ = "wst%d" % k
            self.dma("sp", wst[k], w_ada_v[:, :, blk * 256:(blk + 1) * 256], [], [wk_], s_w[k])
            for kc in range(8):
                self.mm(pB[0:2, 0:256], condT[:, kc, :], wst[k][:, kc, :], kc == 0, kc == 7, ["condT", wk_], ["pB"])
            self.tt("dve", mod_rows[0:2, blk * 256:(blk + 1) * 256], pB[0:2, 0:256], brow[0:2, blk * 256:(blk + 1) * 256],
                    ALU.add, ["pB", "brow"], ["mod_rows"])
            for jl in range(2):
                j = blk * 2 + jl
                for kc in range(8):
                    self.mm(pC[:, 2 * j:2 * j + 2], wst[k][:, kc, jl * 128:(jl + 1) * 128], condT[:, kc, :],
                            kc == 0, kc == 7, ["condT", wk_], ["pC"])
        self.tt("dve", modT[:], pC[:, 0:96].rearrange("p (j b) -> p j b", b=2), cap(b_adaT[:], 0, [[1, 48], [0, 2]]),
                ALU.add, ["pC", "b_adaT"], ["modT"])
        for b in range(2):
            self.stt(A1[:, b, :], modT[:, 8:16, b], 1.0, n1gT[:], ALU.add, ALU.mult, ["modT", "n1gT"], ["A1"])
            self.stt(A2[:, b, :], modT[:, 32:40, b], 1.0, n2gT[:], ALU.add, ALU.mult, ["modT", "n2gT"], ["A2"])
        if "modT" in dbg_d:
            dbg_out("modT", modT[:].rearrange("p j b -> p (j b)"), ["modT"])
        mod_dram = nc.dram_tensor("mod_scratch", [2, 6144], F32).ap()
        self.dma("sp", mod_dram, mod_rows[0:2, :], ["mod_rows"], [], s_c)
        p.barrier()

        self.chk("p0")
        for b in range(self.nseq):
            areset()
            g1_bc = aF(D)
            n2g_bc = aF(D)
            self.dma("sp", n2g_bc, n2g_d.partition_broadcast(128), [], ["n2g_bc"], s_c)
            for dst, key, col0 in ((g1_bc, "g1_bc", 2048), (B2_bc[:], "B2_bc", 3072), (A2_bc[:], "A2_bc", 4096), (g2_bc[:], "g2_bc", 5120)):
                self.dma("sp", dst, mod_dram[b, col0:col0 + 1024].partition_broadcast(128), [], [key], s_c)
            p.barrier()
            self.stt(A2_bc[:], A2_bc[:], 1.0, n2g_bc, ALU.add, ALU.mult, ["A2_bc", "n2g_bc"], ["A2_bc"])
            for blk in range(4):
                load_w(w_out_v, blk * 256, 256, scale_bc=(g1_bc[:, blk * 256:(blk + 1) * 256], "g1_bc"),
                       dst=(woutp[:, :, blk * 256:(blk + 1) * 256], "woutp"))
            p.barrier()

            areset()
            xt = aF(D)
            junk = aF(D)
            xn = aB(D)
            for i in range(NT):
                self.dma("sp", xt, x_d[b, i * 128:(i + 1) * 128, :], [], ["xt"], s_xt)
                self.act(junk, xt, AF.Square, ["xt"], ["junk", "ss"], accum=small[:, 0:1])
                self.rstd(small[:, 0:1], small[:, 1:2], small[:, 2:3], 1.0 / D, "ss", "tmpr", "rstd")
                self.ts("dve", xn, xt, small[:, 2:3], ALU.mult, ["xt", "rstd"], ["xn"])
                for c in range(8):
                    self.tr(pT[:, c * 128:(c + 1) * 128], xn[:, c * 128:(c + 1) * 128], ident_b[:], ["xn", "ident_b"], ["pT"])
                for c in range(8):
                    self.act(hT[:, c, i * 128:(i + 1) * 128], pT[:, c * 128:(c + 1) * 128], AF.Identity, ["pT", "A1", "modT"], ["hT"],
                             scale=A1[:, b, c:c + 1], bias=modT[:, c, b:b + 1])
            p.barrier()

            self.chk("p1a")

            def proj_fm(wcols, wkey, evac):
                for g in range(4):
                    for kc in range(8):
                        self.mm(pB[:, 0:512], wcols[:, kc, :], hT[:, kc, g * 512:(g + 1) * 512], kc == 0, kc == 7,
                                [wkey, "hT"], ["pB"])
                    evac(g)

            def proj_tm(wcols, wkey, ncols, evac):
                for i in range(NT):
                    for kc in range(8):
                        self.mm(pC[:, 0:ncols], hT[:, kc, i * 128:(i + 1) * 128], wcols[:, kc, 0:ncols], kc == 0, kc == 7,
                                [wkey, "hT"], ["pC"])
                    evac(i)

            areset()
            qT = aB(S)
            kT = aB(S)
            vh = aB(16 * 130).rearrange("p (i e) -> p i e", e=130)
            PTs = [aB(1024) for _ in range(2)]
            tmpSs = [aF(256) for _ in range(2)]
            o_sb = aF(128)
            o_bf = aB(128)
            junk128 = aF(128)
            self.memset("pool", vh[:, :, 128:129], 1.0, ["vh"])
            for h in range(4):
                wc, wkey = load_w(w_in_v, h * 128, 128)
                proj_fm(wc, wkey, lambda g: self.act(qT[:, g * 512:(g + 1) * 512], pB[:, 0:512], AF.Copy, ["pB"], ["qT"], scale=0.125))
                wc, wkey = load_w(w_in_v, 512 + h * 128, 128)
                proj_fm(wc, wkey, lambda g: self.act(kT[:, g * 512:(g + 1) * 512], pB[:, 0:512], AF.Copy, ["pB"], ["kT"]))
                wc, wkey = load_w(w_in_v, 1024 + h * 128, 128)
                proj_tm(wc, wkey, 128, lambda i: self.act(vh[:, i, 0:128], pC[:, 0:128], AF.Copy, ["pC"], ["vh"]))
                self.chk("p1b_proj")
                units = []
                for qb in range(NT):
                    for t in range(2):
                        if qb < 8:
                            units.append((qb, t, 0, qb))
                        else:
                            units.append((qb, t, 0, 7))
                            units.append((qb, t, 8, qb))
                Os = (pC, pD)

                def s1(n):
                    qb, t, kb0, kb1 = units[n]
                    reg = n % 2
                    ps = slice(t * 64, (t + 1) * 64)
                    for kb in range(kb0, kb1 + 1):
                        col = reg * 1024 + (kb - kb0) * 128
                        self.mm(pA[:, col:col + 128], kT[ps, kb * 128:(kb + 1) * 128], qT[ps, qb * 128:(qb + 1) * 128],
                                True, True, ["kT", "qT"], ["pA%d" % reg])

                def s2(n):
                    qb, t, kb0, kb1 = units[n]
                    reg = n % 2
                    rk, pk, tk = "pA%d" % reg, "PT%d" % reg, "tmpS%d" % reg
                    base = reg * 1024
                    far_hi = min(kb1, qb - 2)
                    nears = []
                    for kb, boff in ((qb - 1, 0), (qb, 128)):
                        if kb < kb0 or kb > kb1 or kb < 0:
                            continue
                        lc = (kb - kb0) * 128
                        self.tt("dve", tmpSs[reg][:, boff:boff + 128], pA[:, base + lc:base + lc + 128], biasT[:, h, boff:boff + 128], ALU.add,
                                [rk, "biasT"], [tk])
                        nears.append((lc, boff))
                    if far_hi >= kb0:
                        ncol = (far_hi - kb0 + 1) * 128
                        self.act(PTs[reg][:, 0:ncol], pA[:, base:base + ncol], AF.Exp, [rk, tk], [pk])
                    for lc, boff in nears:
                        self.act(PTs[reg][:, lc:lc + 128], tmpSs[reg][:, boff:boff + 128], AF.Exp, [tk], [pk])

                def s3(n):
                    qb, t, kb0, kb1 = units[n]
                    reg = n % 2
                    okey = "pC" if t == 0 else "pD"
                    for kb in range(kb0, kb1 + 1):
                        lc = (kb - kb0) * 128
                        self.mm(Os[t][:, 0:129], PTs[reg][:, lc:lc + 128], vh[:, kb, 0:129], kb == 0, kb == qb,
                                ["PT%d" % reg, "vh"], [okey])
                    if t == 1 and kb1 == qb:
                        self.recip(small[:, 4:5], pC[:, 128:129], ["pC"], ["r1"])
                        self.recip(small[:, 5:6], pD[:, 128:129], ["pD"], ["r2"])
                        self.tt("dve", small[:, 5:6], small[:, 5:6], neglam, ALU.mult, ["r2", "lams"], ["r2"])
                        self.act(o_sb, pC[:, 0:128], AF.Identity, ["pC", "r1"], ["o_sb"], scale=small[:, 4:5])
                        self.stt(o_sb, pD[:, 0:128], small[:, 5:6], o_sb, ALU.mult, ALU.add, ["pD", "r2", "o_sb"], ["o_sb"])
                        self.act(junk128, o_sb, AF.Square, ["o_sb"], ["junk128", "ss"], accum=small[:, 0:1])
                        self.rstd(small[:, 0:1], small[:, 1:2], small[:, 2:3], 1.0 / 128, "ss", "tmpr", "rstd")
                        self.stt(o_bf, o_sb, small[:, 2:3], gsub_bc[:], ALU.mult, ALU.mult, ["o_sb", "rstd", "gsub_bc"], ["o_bf"])
                        self.tr(pT[:, 0:128], o_bf, ident_b[:], ["o_bf", "ident_b"], ["pT"])
                        self.act(mixT[:, h, qb * 128:(qb + 1) * 128], pT[:, 0:128], AF.Copy, ["pT"], ["mixT"])

                for n in range(len(units) + 1):
                    if n < len(units):
                        s1(n)
                        s2(n)
                    if n >= 1:
                        s3(n - 1)
            p.barrier()

            self.chk("p1b")
            areset()
            mraw = aF(2052)
            cacc = aF(S)
            mqT = aB(S)
            mkT = aB(S)
            mk_tm = aB(S).rearrange("p (i d) -> p i d", d=128)
            Vp = aB(16 * 130).rearrange("p (i e) -> p i e", e=130)
            sigo = aF(S).rearrange("p (i d) -> p i d", d=128)
            gates = aF(128).rearrange("p (i g) -> p i g", g=8)
            ef = aF(64).rearrange("p (i g) -> p i g", g=4)
            logf = aF(64)
            a_t = aF(64).rearrange("p (i g) -> p i g", g=4)
            e_t = aF(64).rearrange("p (i g) -> p i g", g=4)
            ebl_t = aF(64).rearrange("p (i g) -> p i g", g=4)
            tmp64 = aF(64).rearrange("p (i g) -> p i g", g=4)
            C_f = aF(130)
            tmpC = aF(130)
            C_b = aB(130)
            MS = aB(128)
            hm = aF(128)
            hm_bf = aB(128)
            junk128 = aF(128)
            sm2 = aF(8)
            self.memset("pool", mraw[:, 0:4], 0.0, ["mraw"])
            wc, wkey = load_w(w_in_v, 3584, 8)
            for i in range(NT):
                for kc in range(8):
                    self.mm(pB[:, i * 8:(i + 1) * 8], hT[:, kc, i * 128:(i + 1) * 128], wc[:, kc, 0:8], kc == 0, kc == 7, [wkey, "hT"], ["pB"])
            self.tt("dve", gates, pB[:, 0:128].rearrange("p (i g) -> p i g", g=8), cap(bg_bc[:], 0, [[0, 16], [1, 8]]), ALU.add,
                    ["pB", "bg_bc"], ["gates"])
            self.act(ef, gates[:, :, 4:8], AF.Exp, ["gates"], ["ef"], scale=-1.0)
            self.act(ef, ef, AF.Ln, ["ef"], ["ef"], bias=1.0, scale=1.0)
            self.ts("dve", logf, ef.rearrange("p i g -> p (i g)"), -1.0, ALU.mult, ["ef"], ["logf"])
            self.mm(pC[:, 0:64], maskU_f, logf, True, True, ["cst", "logf"], ["pC"])
            self.mm(pC[:, 64:128], ones_f, logf, True, True, ["cst", "logf"], ["pC"])
            self.tt("dve", tmp64, gates[:, :, 0:4], pC[:, 0:64].rearrange("p (i g) -> p i g", g=4), ALU.subtract, ["gates", "pC"], ["tmp64"])
            self.act(a_t, tmp64, AF.Exp, ["tmp64"], ["a_t"])
            self.act(e_t, pC[:, 0:64].rearrange("p (i g) -> p i g", g=4), AF.Exp, ["pC", "tmp64"], ["e_t"])
            self.act(ebl_t, pC[:, 64:128].rearrange("p (i g) -> p i g", g=4), AF.Exp, ["pC", "tmp64"], ["ebl_t"])
            for h in range(4):
                for which, dstT in ((0, mqT), (1, mkT)):
                    cc = which * 4 + h
                    wc, wkey = load_w(w_in_v, 1536 + which * 512 + h * 128, 128)
                    proj_fm(wc, wkey, lambda g: self.act(mraw[:, 3 + g * 512:3 + (g + 1) * 512], pB[:, 0:512], AF.Copy, ["pB"], ["mraw"]))
                    self.ts("dve", cacc, mraw[:, 0:S], convwT[:, cc, 0:1], ALU.mult, ["mraw", "convwT"], ["cacc"])
                    for j in range(1, 4):
                        self.stt(cacc, mraw[:, j:j + S], convwT[:, cc, j:j + 1], cacc, ALU.mult, ALU.add, ["mraw", "convwT", "cacc"], ["cacc"])
                    if which == 0:
                        self.act(mqT, cacc, AF.Silu, ["cacc", "convbT", "cst"], ["mqT"], bias=convbT[:, cc:cc + 1], scale=ones_f[:, 0:1])
                    else:
                        self.act(cacc, cacc, AF.Silu, ["cacc", "convbT", "cst"], ["cacc"], bias=convbT[:, cc:cc + 1], scale=ones_f[:, 0:1])
                        self.ts("dve", mkT, cacc, 128.0 ** -0.5, ALU.mult, ["cacc"], ["mkT"])
                for i0 in range(0, NT, 8):
                    for j in range(8):
                        self.tr(pT[:, j * 128:(j + 1) * 128], mkT[:, (i0 + j) * 128:(i0 + j + 1) * 128], ident_b[:], ["mkT", "ident_b"], ["pT"])
                    self.act(mk_tm[:, i0:i0 + 8, :], pT[:, 0:1024].rearrange("p (i d) -> p i d", d=128), AF.Copy, ["pT"], ["mk_tm"])
                wc, wkey = load_w(w_in_v, 2560 + h * 128, 128)
                proj_tm(wc, wkey, 128, lambda i: self.act(Vp[:, i, 0:128], pC[:, 0:128], AF.Identity, ["pC", "a_t"], ["Vp"], scale=a_t[:, i, h:h + 1]))
                self.cp("dve", Vp[:, :, 128:129], a_t[:, :, h:h + 1], ["a_t"], ["Vp"])
                wc, wkey = load_w(w_in_v, 3072 + h * 128, 128)
                proj_tm(wc, wkey, 128, lambda i: self.act(sigo[:, i, :], pC[:, 0:128], AF.Sigmoid, ["pC"], ["sigo"]))
                for i in range(NT):
                    tl = slice(i * 128, (i + 1) * 128)
                    self.mm(pA[:, 0:128], mkT[:, tl], mqT[:, tl], True, True, ["mkT", "mqT"], ["pA"])
                    self.tt("dve", MS, pA[:, 0:128], maskU_f, ALU.mult, ["pA", "cst"], ["MS"])
                    self.mm(pC[:, 0:129], MS, Vp[:, i, 0:129], True, i == 0, ["MS", "Vp"], ["pC"])
                    if i > 0:
                        self.mm(pC[:, 0:129], mqT[:, tl], C_b[:, 0:129], False, True, ["mqT", "C_b"], ["pC"])
                    if i < NT - 1:
                        self.mm(pD[:, 0:129], mk_tm[:, i, :], Vp[:, i, 0:129], True, True, ["mk_tm", "Vp"], ["pD"])
                        if i == 0:
                            self.cp("dve", tmpC[:, 0:129], pD[:, 0:129], ["pD"], ["tmpC"])
                        else:
                            self.tt("dve", tmpC[:, 0:129], pD[:, 0:129], C_f[:, 0:129], ALU.add, ["pD", "C_f"], ["tmpC"])
                        self.act(C_f[:, 0:129], tmpC[:, 0:129], AF.Identity, ["tmpC", "ebl_t"], ["C_f"], scale=ebl_t[:, i, h:h + 1])
                        self.cp("dve", C_b[:, 0:129], C_f[:, 0:129], ["C_f"], ["C_b"])
                    self.tt("dve", sm2[:, 0:1], pC[:, 128:129], e_t[:, i, h:h + 1], ALU.mult, ["pC", "e_t"], ["sm2"])
                    self.stt(sm2[:, 1:2], sm2[:, 0:1], -1.0, sm2[:, 0:1], ALU.mult, ALU.max, ["sm2"], ["sm2"])
                    self.ts("dve", sm2[:, 1:2], sm2[:, 1:2], 1.0, ALU.max, ["sm2"], ["sm2"])
                    self.recip(sm2[:, 2:3], sm2[:, 1:2], ["sm2"], ["sm2"])
                    self.tt("dve", sm2[:, 3:4], sm2[:, 2:3], e_t[:, i, h:h + 1], ALU.mult, ["sm2", "e_t"], ["sm2"])
                    self.stt(hm, pC[:, 0:128], sm2[:, 3:4], sigo[:, i, :], ALU.mult, ALU.mult, ["pC", "sm2", "sigo"], ["hm"])
                    self.act(junk128, hm, AF.Square, ["hm"], ["junk128", "ss"], accum=small[:, 0:1])
                    self.rstd(small[:, 0:1], small[:, 1:2], small[:, 2:3], 1.0 / 128, "ss", "tmpr", "rstd")
                    self.stt(hm_bf, hm, small[:, 2:3], gm_bc[:, h * 128:(h + 1) * 128], ALU.mult, ALU.mult, ["hm", "rstd", "gm_bc"], ["hm_bf"])
                    self.tr(pT[:, 0:128], hm_bf, ident_b[:], ["hm_bf", "ident_b"], ["pT"])
                    self.act(mixT[:, 4 + h, tl], pT[:, 0:128], AF.Copy, ["pT"], ["mixT"])
            p.barrier()

            self.chk("p1c")
            areset()
            xts = [aF(D) for _ in range(2)]
            x1s = [aF(D) for _ in range(2)]
            for i in range(NT):
                k = i % 2
                tl = slice(i * 128, (i + 1) * 128)
                pcol = k * 1024
                self.dma("sp", xts[k], x_d[b, tl, :], [], ["xts%d" % k], s_xts[k])
                for half in range(2):
                    for c in range(8):
                        self.mm(pA[:, pcol + half * 512:pcol + (half + 1) * 512], mixT[:, c, tl], woutp[:, c, half * 512:(half + 1) * 512],
                                c == 0, c == 7, ["mixT", "woutp"], ["pA%d" % k])
                self.tt("dve", x1s[k], pA[:, pcol:pcol + D], xts[k], ALU.add, ["pA%d" % k, "xts%d" % k], ["x1s%d" % k])
                self.dma("sp", x1_dram[tl, :], x1s[k], ["x1s%d" % k], [], s_x1s[k])
            p.barrier()
            for blk in range(8):
                load_w(w_q_v, blk * 256, 256, dst=(wq[:, :, blk * 256:(blk + 1) * 256], "wq"))
            p.barrier()
            areset(0)
            pers = [dict(x1=aF(D), h2=aF(D), idx_i=aI(128), gte=aF(128)) for _ in range(2)]
            R = aF(2048)
            xt = R[:, 0:1024]
            qTp = R[:, 0:1024].bitcast(BF16).rearrange("p (c t) -> p c t", t=128)
            sc = R.rearrange("p (c k) -> p c k", k=128)
            cand = R
            eq = R
            wk = aF(2048)
            wk2 = wk
            xn = aB(D)
            h2T = aB(8 * 128).rearrange("p (c t) -> p c t", t=128)
            sv = aF(256)
            si = aU(256)
            sif = aF(256)
            tops = aF(128)
            pos = aU(128)
            pa_u = aU(128)
            pb_u = aU(128)
            paf = aF(128)
            pbf = aF(128)
            i1f = aF(128)
            i2f = aF(128)
            idxf = aF(128)
            tg = aF(128)
            zs = aF(8)
            uvbuf = [mixT[:, j, :] for j in range(8)] + \
                    [woutp[:, 2 * j:2 * j + 2, :].rearrange("p c n -> p (c n)") for j in range(4)] + \
                    [aB(2048) for _ in range(NB - 12)]
            prod = [aF(D) for _ in range(2)]
            junkb = aB(D)
            diag = [aB(128) for _ in range(2)]
            dots = aF(128)
            t1 = aF(128)
            t2 = aF(128)
            wgt = aF(128)
            acc_sb = aF(D)
            V = lambda fn, r, w: p.op("dve", fn, reads=r, writes=w)

            def top16_multi(items):
                for (vals, idxs, src, scratch, tag) in items:
                    V(lambda e, o=vals, i_=src: e.max(out=o[:, 0:8], in_=i_), ["src" + tag], ["v" + tag])
                for (vals, idxs, src, scratch, tag) in items:
                    V(lambda e, o=idxs, m=vals, i_=src: e.max_index(out=o[:, 0:8], in_max=m[:, 0:8], in_values=i_), ["src" + tag, "v" + tag], ["i" + tag])
                for (vals, idxs, src, scratch, tag) in items:
                    V(lambda e, o=scratch, m=vals, i_=src: e.match_replace(out=o, in_to_replace=m[:, 0:8], in_values=i_, imm_value=-1e30), ["src" + tag, "v" + tag], ["w" + tag])
                for (vals, idxs, src, scratch, tag) in items:
                    V(lambda e, o=vals, i_=scratch: e.max(out=o[:, 8:16], in_=i_), ["w" + tag], ["v" + tag])
                for (vals, idxs, src, scratch, tag) in items:
                    V(lambda e, o=idxs, m=vals, i_=scratch: e.max_index(out=o[:, 8:16], in_max=m[:, 8:16], in_values=i_), ["w" + tag, "v" + tag], ["i" + tag])

            SA = ["srcA%d" % hp for hp in range(16)]
            VA = ["vA%d" % hp for hp in range(16)]
            IA = ["iA%d" % hp for hp in range(16)]
            SB_ = ["srcB%d" % h for h in range(8)]
            TOPS = ["vB%d" % h for h in range(8)]
            POS = ["iB%d" % h for h in range(8)]

            def front(i, par):
                tl = slice(i * 128, (i + 1) * 128)
                P = pers[par]
                x1, h2, idx_i, gte = P["x1"], P["h2"], P["idx_i"], P["gte"]
                kx1, kh2, kidx, kg = "x1_%d" % par, "h2_%d" % par, "idx_%d" % par, "gte_%d" % par
                self.dma("sp", x1, x1_dram[tl, :], [], [kx1], s_x1[par])
                yield
                self.act(junkb, x1, AF.Square, [kx1], ["junkb", "ss"], accum=small[:, 0:1])
                yield
                self.ts("dve", small[:, 1:2], small[:, 0:1], 1.0 / D, ALU.mult, ["ss"], ["tmpr"], s2=EPS, op1=ALU.add)
                yield
                self.act(small[:, 1:2], small[:, 1:2], AF.Sqrt, ["tmpr"], ["tmpr"])
                yield
                self.recip(small[:, 2:3], small[:, 1:2], ["tmpr"], ["rstd"])
                self.ts("dve", xn, x1, small[:, 2:3], ALU.mult, [kx1, "rstd"], ["xn"])
                self.stt(h2, x1, small[:, 2:3], A2_bc[:], ALU.mult, ALU.mult, [kx1, "rstd", "A2_bc"], [kh2])
                self.tt("dve", h2, h2, B2_bc[:], ALU.add, [kh2, "B2_bc"], [kh2])
                yield
                for c in range(8):
                    self.tr(pT[:, c * 128:(c + 1) * 128], xn[:, c * 128:(c + 1) * 128], ident_b[:], ["xn", "ident_b"], ["pT"])
                yield
                for c in range(8):
                    self.act(h2T[:, c, :], pT[:, c * 128:(c + 1) * 128], AF.Identity, ["pT", "A2", "modT"], ["h2T"],
                             scale=A2[:, b, c:c + 1], bias=modT[:, 24 + c, b:b + 1])
                yield
                for hp in range(16):
                    for kc in range(8):
                        self.mm(pA[:, hp * 128:(hp + 1) * 128], wq[:, kc, hp * 128:(hp + 1) * 128], h2T[:, kc, :], kc == 0, kc == 7,
                                ["wq", "h2T"], ["pA"])
                    if hp % 4 == 3:
                        yield
                self.act(qTp.rearrange("p c t -> p (c t)"), pA[:, 0:2048], AF.Copy, ["pA"], ["R"])
                yield
                for hp in range(16):
                    self.mm(pA[:, hp * 128:(hp + 1) * 128], qTp[:, hp, :], skT[:, hp, :], True, True, ["R", "skT"], ["pA"])
                yield
                self.act(sc.rearrange("p c k -> p (c k)"), pA[:, 0:2048], AF.Copy, ["pA"], ["R"] + SA)
                yield
                itemsA = [(sv[:, hp * 16:(hp + 1) * 16], si[:, hp * 16:(hp + 1) * 16], sc[:, hp, :], wk[:, hp * 128:(hp + 1) * 128], "A%d" % hp)
                          for hp in range(16)]
                top16_multi(itemsA[0:8])
                yield
                top16_multi(itemsA[8:16])
                yield
                self.cp("dve", sif, si, IA, ["sif"])
                cand4 = cand.rearrange("p (h a b) -> p h a b", h=8, a=16)
                self.tt("dve", cand4, cap(sv, 0, [[32, 8], [1, 16], [0, 16]]), cap(sv, 16, [[32, 8], [0, 16], [1, 16]]), ALU.add,
                        VA + IA, ["R"] + SA + SB_)
                yield
                top16_multi([(tops[:, h * 16:(h + 1) * 16], pos[:, h * 16:(h + 1) * 16], cand[:, h * 256:(h + 1) * 256], wk2[:, h * 256:(h + 1) * 256], "B%d" % h)
                             for h in range(8)])
                yield
                V(lambda e, o=pa_u, i_=pos: e.tensor_single_scalar(out=o, in_=i_, scalar=4, op=ALU.logical_shift_right), POS, ["pa_u"])
                V(lambda e, o=pb_u, i_=pos: e.tensor_single_scalar(out=o, in_=i_, scalar=15, op=ALU.bitwise_and), POS, ["pb_u"])
                self.cp("dve", paf, pa_u, ["pa_u"], ["paf"])
                self.cp("dve", pbf, pb_u, ["pb_u"], ["pbf"])
                yield
                eq4 = eq.rearrange("p (h k a) -> p h k a", h=8, k=16)
                for which, pf, outf, key in ((0, paf, i1f, "i1f"), (1, pbf, i2f, "i2f")):
                    self.tt("dve", eq4, cap(pf, 0, [[16, 8], [1, 16], [0, 16]]), cap(iota16, 0, [[0, 8], [0, 16], [1, 16]]), ALU.is_equal,
                            ["paf", "pbf", "cst"] + POS + TOPS, ["R"] + SB_)
                    self.tt("dve", eq4, eq4, cap(sif, which * 16, [[32, 8], [0, 16], [1, 16]]), ALU.mult, ["R", "sif"], ["R"])
                    self.red(outf.rearrange("p (h k) -> p h k", h=8), eq4, ALU.add, ["R"], [key])
                    yield
                self.stt(idxf, i1f, 128.0, i2f, ALU.mult, ALU.add, ["i1f", "i2f"], ["idxf"])
                self.cp("dve", idx_i, idxf, ["idxf"], [kidx])
                tops3 = tops.rearrange("p (h k) -> p h k", h=8)
                tg3 = tg.rearrange("p (h k) -> p h k", h=8)
                self.tt("dve", tg3, tops3, cap(tops, 0, [[16, 8], [0, 16]]), ALU.subtract, TOPS, ["tg"])
                yield
                self.act(tg, tg, AF.Exp, ["tg"], ["tg"])
                yield
                self.red(zs, tg3, ALU.add, ["tg"], ["zs"])
                self.recip(zs, zs, ["zs"], ["zs"])
                self.tt("dve", gte.rearrange("p (h k) -> p h k", h=8), tg3, cap(zs, 0, [[1, 8], [0, 16]]), ALU.mult, ["tg", "zs"], [kg])
                if b == 0 and i == 0:
                    dbg_out("x1", x1, [kx1])
                    dbg_out("idxf", idxf, ["idxf"])
                    dbg_out("gte", gte, [kg])
                    dbg_out("h2", h2, [kh2])
                if b == 0 and "x1full" in dbg_d:
                    self.dma("sp", dbg_d["x1full"][tl, :], x1, [kx1], [], s_dbg)
                    self.dma("sp", dbg_d["idxfull"][tl, :], idxf, ["idxf"], [], s_dbg)
                yield

            def back(i, par):
                tl = slice(i * 128, (i + 1) * 128)
                P = pers[par]
                x1, h2, idx_i, gte = P["x1"], P["h2"], P["idx_i"], P["gte"]
                kx1, kh2, kidx, kg = "x1_%d" % par, "h2_%d" % par, "idx_%d" % par, "gte_%d" % par
                NG = 128 // GS

                def stage_a(g):
                    for s in range(g * GS, (g + 1) * GS):
                        j = s % NB
                        uk = "uv%d" % j
                        self.gather(uvbuf[j], uv_dram, idx_i[:, s:s + 1], [kidx], [uk], s_uv[j])
                        pk = "prod%d" % (s % 2)
                        self.tt("dve", prod[s % 2], uvbuf[j][:, 0:1024], h2, ALU.mult, [uk, kh2], [pk])
                        self.act(junkb, prod[s % 2], AF.Identity, [pk], ["junkb", "dots%d" % (g % 4)], accum=dots[:, s:s + 1])

                def stage_b(g):
                    sl = slice(g * GS, (g + 1) * GS)
                    dk_, t1k, t2k, wk_ = "dots%d" % (g % 4), "t1_%d" % (g % 4), "t2_%d" % (g % 4), "wgt%d" % (g % 4)
                    self.tt("dve", t1[:, sl], dots[:, sl], dots[:, sl], ALU.mult, [dk_], [t1k])
                    self.tt("dve", t1[:, sl], t1[:, sl], dots[:, sl], ALU.mult, [t1k, dk_], [t1k])
                    self.stt(t1[:, sl], t1[:, sl], 0.044715, dots[:, sl], ALU.mult, ALU.add, [t1k, dk_], [t1k])
                    self.act(t2[:, sl], t1[:, sl], AF.Tanh, [t1k], [t2k], scale=0.7978845608028654)
                    self.ts("dve", t2[:, sl], t2[:, sl], 1.0, ALU.add, [t2k], [t2k], s2=0.5, op1=ALU.mult)
                    self.tt("dve", t2[:, sl], t2[:, sl], dots[:, sl], ALU.mult, [t2k, dk_], [t2k])
                    self.tt("dve", wgt[:, sl], t2[:, sl], gte[:, sl], ALU.mult, [t2k, kg], [wk_])
                    for s in range(g * GS, (g + 1) * GS):
                        j = s % NB
                        uk = "uv%d" % j
                        dk = "diag%d" % (s % 2)
                        self.act(diag[s % 2], ident_b[:], AF.Identity, ["ident_b", wk_], [dk], scale=wgt[:, s:s + 1])
                        self.mm(pB[:, 0:512], diag[s % 2], uvbuf[j][:, 1024:1536], s == 0, s == 127, [dk, uk], ["pB"])
                        self.mm(pC[:, 0:512], diag[s % 2], uvbuf[j][:, 1536:2048], s == 0, s == 127, [dk, uk], ["pC"])

                for g in range(NG + 1):
                    if g < NG:
                        stage_a(g)
                    if g >= 1:
                        stage_b(g - 1)
                    yield
                if b == 0 and i == 0:
                    dbg_out("dots", dots, ["dots%d" % k_ for k_ in range(4)])
                self.tt("dve", acc_sb[:, 0:512], pB[:, 0:512], g2_bc[:, 0:512], ALU.mult, ["pB", "g2_bc"], ["acc_sb"])
                self.tt("dve", acc_sb[:, 512:1024], pC[:, 0:512], g2_bc[:, 512:1024], ALU.mult, ["pC", "g2_bc"], ["acc_sb"])
                self.tt("dve", acc_sb, acc_sb, x1, ALU.add, ["acc_sb", kx1], ["acc_sb"])
                self.act(junkb, acc_sb, AF.Square, ["acc_sb"], ["junkb", "ssb"], accum=small[:, 8:9])
                self.rstd(small[:, 8:9], small[:, 9:10], small[:, 10:11], 1.0 / D, "ssb", "tmprb", "rstdb")
                self.stt(acc_sb, acc_sb, small[:, 10:11], fin_bc[:], ALU.mult, ALU.mult, ["acc_sb", "rstdb", "fin_bc"], ["acc_sb"])
                self.dma("sp", out_d[b, tl, :], acc_sb, ["acc_sb"], [], s_out)
                yield

            def run_all(gen):
                for _ in gen:
                    pass

            nt2 = self.ntiles2
            run_all(front(0, 0))
            self.chk("p2f")
            for i in range(nt2):
                par = i % 2
                fg = front(i + 1, par ^ 1) if i + 1 < nt2 else None
                if not self.no_back:
                    for _ in back(i, par):
                        if fg is not None:
                            next(fg, None)
                if fg is not None:
                    run_all(fg)
            p.barrier()


def _rel_bucket_np(n):
    n = np.asarray(n)
    max_exact = 16
    nf = np.maximum(n, 1).astype(np.float32)
    large = max_exact + (np.log(nf / np.float32(max_exact)) / np.float32(math.log(128 / max_exact))
                         * np.float32(32 - max_exact)).astype(np.int32)
    large = np.minimum(large, 31)
    return np.where(n < max_exact, n, large)


def _prep_inputs(inp):
    f = lambda a: np.ascontiguousarray(np.asarray(a, dtype=np.float32))
    x = f(inp["x"])
    c = f(inp["c"])
    shared = {}
    shared["w_ada"] = f(inp["w_ada"][0])
    shared["b_ada"] = f(inp["b_ada"][0])
    shared["b_adaT"] = f(inp["b_ada"][0].reshape(48, 128).T)
    shared["n1gT"] = f(inp["norm1_g"][0].reshape(8, 128).T)
    shared["n2gT"] = f(inp["norm2_g"][0].reshape(8, 128).T)
    shared["n2g"] = f(inp["norm2_g"][0])
    shared["w_in"] = f(inp["w_in"][0])
    shared["convwT"] = f(np.asarray(inp["conv_w"][0]).T.reshape(8, 128, 4).transpose(1, 0, 2))
    shared["convbT"] = f(np.asarray(inp["conv_b"][0]).reshape(8, 128).T)
    shared["bgate"] = f(np.concatenate([np.asarray(inp["b_igate"][0]), np.asarray(inp["b_fgate"][0])]))
    shared["lam"] = f(np.stack([np.asarray(inp[k][0]) for k in ("lam_q1", "lam_k1", "lam_q2", "lam_k2")]))
    shared["gsub"] = f(inp["diff_sub_g"][0])
    shared["gm"] = f(inp["mlstm_norm_g"][0])
    shared["w_out"] = f(inp["w_out"][0])
    shared["w_q"] = f(inp["peer_w_q"][0])
    sk = np.asarray(inp["peer_sub_keys"][0])
    shared["skT"] = f(sk.transpose(3, 0, 1, 2).reshape(128, 2048))
    shared["peer_u"] = f(inp["peer_u"][0])
    shared["peer_v"] = f(inp["peer_v"][0])
    rb = f(inp["rel_bias"])
    shared["rel_bias"] = rb
    kk = np.arange(128)[:, None]
    qq = np.arange(128)[None, :]
    rel1 = qq - kk + 128
    rel0 = qq - kk
    b1 = rb[_rel_bucket_np(rel1)]
    b0 = rb[_rel_bucket_np(np.maximum(rel0, 0))]
    biasT = np.empty((128, 4, 256), np.float32)
    biasT[:, :, 0:128] = b1.transpose(0, 2, 1)
    biasT[:, :, 128:256] = b0.transpose(0, 2, 1)
    biasT[:, :, 128:256][np.broadcast_to((rel0 < 0)[:, None, :], (128, 4, 128))] = -1e9
    shared["biasT"] = biasT
    shared["final_g"] = f(inp["final_g"])
    cst = np.zeros((128, 400), np.float32)
    cst[:, 0:128] = np.eye(128)
    cst[:, 128:256] = np.triu(np.ones((128, 128)))
    cst[:, 256:384] = 1.0
    cst[:, 384:400] = np.arange(16)[None, :]
    shared["cst"] = cst
    in_maps = []
    for core in range(8):
        m = dict(shared)
        m["x"] = np.ascontiguousarray(x[2 * core:2 * core + 2])
        cc = c[2 * core:2 * core + 2]
        m["cT"] = np.ascontiguousarray(cc.reshape(2, 8, 128).transpose(2, 1, 0))
        in_maps.append(m)
    return in_maps


_NC_CACHE = {}


def kernel(**inputs):
    in_maps = _prep_inputs(inputs)
    if "nc" not in _NC_CACHE:
        _NC_CACHE["nc"] = KB().build()
    nc = _NC_CACHE["nc"]
    res = run_bass_kernel_spmd(nc, in_maps, core_ids=list(range(8)))
    out = np.concatenate([np.asarray(r["out"]) for r in res.results], axis=0)
    return out.astype(np.float32)
```

```python
import math
import numpy as np
import concourse.bass as bass
import concourse.mybir as mybir
from concourse.bass_utils import run_bass_kernel_spmd

F32 = mybir.dt.float32
BF16 = mybir.dt.bfloat16
U32 = mybir.dt.uint32
I32 = mybir.dt.int32
ALU = mybir.AluOpType
AF = mybir.ActivationFunctionType
AX = mybir.AxisListType

ENGS = ("pe", "act", "dve", "pool", "sp")
SAME_ENGINE_SYNC = {"pe": False, "act": True, "dve": True, "pool": True, "sp": True}

S = 2048
D = 1024
NT = 16
EPS = 1e-6
NU = 3
NV = 3
ARENA_W = 24320
WREG = 6144
GS = 4
NB = 16


class Prog:
    def __init__(self, nc):
        self.nc = nc
        self.q = {e: [] for e in ENGS}
        self.cnt = {}
        self.semobj = {}
        self.lastw = {}
        self.readers = {}
        self.seen = {e: {} for e in ENGS}
        self.ownsem = {}
        for e in ENGS:
            s = nc.alloc_semaphore("es_" + e)
            self.ownsem[e] = s.name
            self.semobj[s.name] = s
            self.cnt[s.name] = 0
        self.nsem = 0

    def new_sem(self, name=None):
        self.nsem += 1
        s = self.nc.alloc_semaphore(name or ("ds_%d" % self.nsem))
        self.semobj[s.name] = s
        self.cnt[s.name] = 0
        return s.name

    def op(self, eng, fn, reads=(), writes=(), sem=None, inc=None):
        if sem is None:
            sem = self.ownsem[eng]
            inc = 1
        elif inc is None:
            inc = 16
        deps = {}

        def add(d):
            if d is None:
                return
            s, v = d
            if deps.get(s, 0) < v:
                deps[s] = v

        for k in reads:
            add(self.lastw.get(k))
        for k in writes:
            add(self.lastw.get(k))
            for s, v in self.readers.get(k, {}).items():
                add((s, v))
        waits = []
        for s, v in deps.items():
            if s == self.ownsem[eng] and not SAME_ENGINE_SYNC[eng]:
                continue
            if self.seen[eng].get(s, 0) >= v:
                continue
            self.seen[eng][s] = v
            waits.append((s, v))
        self.cnt[sem] += inc
        val = self.cnt[sem]
        for k in reads:
            r = self.readers.setdefault(k, {})
            if r.get(sem, 0) < val:
                r[sem] = val
        for k in writes:
            self.lastw[k] = (sem, val)
            self.readers[k] = {}
        self.q[eng].append((waits, fn, sem, inc))

    def barrier(self):
        snap = dict(self.cnt)
        for e in ENGS:
            waits = []
            for s, v in snap.items():
                if v > self.seen[e].get(s, 0):
                    self.seen[e][s] = v
                    waits.append((s, v))
            own = self.ownsem[e]
            self.cnt[own] += 1
            self.q[e].append((waits, (lambda eng: eng.nop()), own, 1))
        self.lastw.clear()
        self.readers.clear()

    def emit(self):
        nc = self.nc
        finals = [(s, v) for s, v in self.cnt.items() if v > 0]
        with nc.Block() as block:
            def run(eng_name, wait_all=False):
                def f(eng):
                    for waits, fn, sem, inc in self.q[eng_name]:
                        for s, v in waits:
                            eng.wait_ge(self.semobj[s], v)
                        ins = fn(eng)
                        ins.then_inc(self.semobj[sem], inc)
                    if wait_all:
                        for s, v in finals:
                            eng.wait_ge(self.semobj[s], v)
                return f
            block.tensor(run("pe"))
            block.scalar(run("act"))
            block.vector(run("dve"))
            block.gpsimd(run("pool"))
            block.sync(run("sp", wait_all=True))


def cap(t, off, dims):
    return bass.AP(t.tensor, t.offset + off, [list(t.ap[0])] + [list(d) for d in dims])


class _Stop(Exception):
    pass


class KB:
    def __init__(self, dbg=None, nseq=2, ntiles2=NT, stop=None, no_back=False):
        self.no_back = no_back
        self.dbg = dbg
        self.stop = stop
        self.nseq = nseq
        self.ntiles2 = ntiles2
        nc = self.nc = bass.Bass("TRN2", target_bir_lowering=False)
        self.p = Prog(nc)
        self._n = 0

    def mm(self, out, lhsT, rhs, start, stop, r, w):
        self.p.op("pe", lambda e: e.matmul(out, lhsT=lhsT, rhs=rhs, start=start, stop=stop), reads=r, writes=w)

    def tr(self, out, in_, ident, r, w):
        self.p.op("pe", lambda e: e.transpose(out=out, in_=in_, identity=ident), reads=r, writes=w)

    def act(self, out, in_, func, r, w, scale=None, bias=None, accum=None):
        kw = {}
        if scale is not None:
            kw["scale"] = scale
        if bias is not None:
            kw["bias"] = bias
        if accum is not None:
            kw["accum_out"] = accum
        self.p.op("act", lambda e: e.activation(out=out, in_=in_, func=func, **kw), reads=r, writes=w)

    def tt(self, eng, out, in0, in1, op, r, w):
        self.p.op(eng, lambda e: e.tensor_tensor(out=out, in0=in0, in1=in1, op=op), reads=r, writes=w)

    def ts(self, eng, out, in0, s1, op0, r, w, s2=None, op1=None):
        if op1 is None:
            self.p.op(eng, lambda e: e.tensor_scalar(out=out, in0=in0, scalar1=s1, scalar2=None, op0=op0), reads=r, writes=w)
        else:
            self.p.op(eng, lambda e: e.tensor_scalar(out=out, in0=in0, scalar1=s1, scalar2=s2, op0=op0, op1=op1), reads=r, writes=w)

    def stt(self, out, in0, scalar, in1, op0, op1, r, w):
        self.p.op("dve", lambda e: e.scalar_tensor_tensor(out=out, in0=in0, scalar=scalar, in1=in1, op0=op0, op1=op1), reads=r, writes=w)

    def cp(self, eng, out, in_, r, w):
        self.p.op(eng, lambda e: e.tensor_copy(out=out, in_=in_), reads=r, writes=w)

    def red(self, out, in_, op, r, w):
        self.p.op("dve", lambda e: e.tensor_reduce(out=out, in_=in_, axis=AX.X, op=op), reads=r, writes=w)

    def recip(self, out, in_, r, w):
        self.p.op("dve", lambda e: e.reciprocal(out=out, in_=in_), reads=r, writes=w)

    def memset(self, eng, ap, val, w):
        self.p.op(eng, lambda e: e.memset(ap, val), writes=w)

    def dma(self, eng, out, in_, r, w, sem):
        self.p.op(eng, lambda e: e.dma_start(out=out, in_=in_), reads=r, writes=w, sem=sem)

    def gather(self, out, table, idx, r, w, sem):
        self.p.op("pool", lambda e: e.indirect_dma_start(
            out=out, out_offset=None, in_=table,
            in_offset=bass.IndirectOffsetOnAxis(ap=idx, axis=0)), reads=r, writes=w, sem=sem)

    def sb(self, shape, dt, name=None):
        self._n += 1
        return self.nc.alloc_sbuf_tensor("sb_" + (name or ("t%d" % self._n)), shape, dt)

    def rstd(self, ss, tmp, out, inv_n, key_ss, key_tmp, key_out):
        self.ts("dve", tmp, ss, inv_n, ALU.mult, [key_ss], [key_tmp], s2=EPS, op1=ALU.add)
        self.act(tmp, tmp, AF.Sqrt, [key_tmp], [key_tmp])
        self.recip(out, tmp, [key_tmp], [key_out])

    def build(self):
        try:
            self._build()
        except _Stop:
            pass
        self.p.emit()
        return self.nc

    def chk(self, tag):
        if self.stop == tag:
            raise _Stop()

    def _build(self):
        nc, p = self.nc, self.p

        def din(name, shape, dt=F32):
            return nc.dram_tensor(name, shape, dt, kind="ExternalInput").ap()

        x_d = din("x", [2, S, D])
        cT_d = din("cT", [128, 8, 2])
        w_ada_d = din("w_ada", [D, 6 * D])
        b_ada_d = din("b_ada", [6 * D])
        b_adaT_d = din("b_adaT", [128, 48])
        n1gT_d = din("n1gT", [128, 8])
        n2gT_d = din("n2gT", [128, 8])
        n2g_d = din("n2g", [D])
        w_in_d = din("w_in", [D, 3592])
        convwT_d = din("convwT", [128, 8, 4])
        convbT_d = din("convbT", [128, 8])
        bgate_d = din("bgate", [8])
        lam_d = din("lam", [4, 64])
        gsub_d = din("gsub", [128])
        gm_d = din("gm", [512])
        w_out_d = din("w_out", [D, D])
        w_q_d = din("w_q", [D, 2048])
        skT_d = din("skT", [128, 2048])
        u_d = din("peer_u", [16384, D])
        v_d = din("peer_v", [16384, D])
        relb_d = din("rel_bias", [32, 4])
        biasT_d = din("biasT", [128, 4, 256])
        fin_d = din("final_g", [D])
        cst_d = din("cst", [128, 400])
        out_d = nc.dram_tensor("out", [2, S, D], F32, kind="ExternalOutput").ap()
        dbg_d = {}
        if self.dbg:
            for name, shape in self.dbg.items():
                dbg_d[name] = nc.dram_tensor("dbg_" + name, shape, F32, kind="ExternalOutput").ap()

        w_ada_v = w_ada_d.rearrange("(c p) n -> p c n", p=128)
        w_in_v = w_in_d.rearrange("(c p) n -> p c n", p=128)
        w_out_v = w_out_d.rearrange("(c p) n -> p c n", p=128)
        w_q_v = w_q_d.rearrange("(c p) n -> p c n", p=128)

        sb = self.sb
        cst = sb([128, 400], F32, "cst")
        ident_f = cst[:, 0:128]
        maskU_f = cst[:, 128:256]
        ones_f = cst[:, 256:384]
        iota16 = cst[:, 384:400]
        ident_b = sb([128, 128], BF16, "ident_b")
        biasT = sb([128, 4, 256], F32, "biasT")
        cfar = sb([128, 4], F32, "cfar")
        condT = sb([128, 8, 2], F32, "condT")
        modT = sb([128, 48, 2], F32, "modT")
        b_adaT = sb([128, 48], F32, "b_adaT")
        n1gT = sb([128, 8], F32, "n1gT")
        n2gT = sb([128, 8], F32, "n2gT")
        convwT = sb([128, 8, 4], F32, "convwT")
        convbT = sb([128, 8], F32, "convbT")
        A1 = sb([128, 2, 8], F32, "A1")
        A2 = sb([128, 2, 8], F32, "A2")
        fin_bc = sb([128, D], F32, "fin_bc")
        gsub_bc = sb([128, 128], F32, "gsub_bc")
        gm_bc = sb([128, 512], F32, "gm_bc")
        bg_bc = sb([128, 8], F32, "bg_bc")
        A2_bc = sb([128, D], F32, "A2_bc")
        B2_bc = sb([128, D], F32, "B2_bc")
        g2_bc = sb([128, D], F32, "g2_bc")
        lamv = sb([128, 4, 64], F32, "lamv")
        lams = sb([128, 8], F32, "lams")
        small = sb([128, 16], F32, "small")
        mixT = sb([128, 8, S], BF16, "mixT")
        arena1 = sb([128, 8, S], BF16, "arena1")
        hT = arena1
        wq = arena1
        woutp = sb([128, 8, D], BF16, "woutp")
        skT = sb([128, 16, 128], BF16, "skT")
        arena = sb([128, ARENA_W], F32, "arena")
        wst = [arena[:, i * 2048:(i + 1) * 2048].rearrange("p (c n) -> p c n", c=8) for i in range(2)]
        wbf = [arena[:, 4096 + i * 1024:4096 + (i + 1) * 1024].bitcast(BF16).rearrange("p (c n) -> p c n", c=8) for i in range(2)]

        pA = nc.alloc_psum_tensor("pA", [128, 2048], F32)
        pT = nc.alloc_psum_tensor("pT", [128, 1024], BF16)
        pB = nc.alloc_psum_tensor("pB", [128, 512], F32)
        pC = nc.alloc_psum_tensor("pC", [128, 512], F32)
        pD = nc.alloc_psum_tensor("pD", [128, 512], F32)

        s_c = p.new_sem("s_const")
        s_xt = p.new_sem("s_xt")
        s_w = [p.new_sem("s_w0"), p.new_sem("s_w1")]
        s_uv = [p.new_sem("s_uv%d" % i) for i in range(NB)]
        s_pu = [p.new_sem("s_pu%d" % i) for i in range(2)]
        s_pv = [p.new_sem("s_pv%d" % i) for i in range(2)]
        s_ps = [p.new_sem("s_ps%d" % i) for i in range(2)]
        s_xts = [p.new_sem("s_xts%d" % i) for i in range(2)]
        s_x1s = [p.new_sem("s_x1s%d" % i) for i in range(2)]
        s_x1 = [p.new_sem("s_x1_%d" % i) for i in range(2)]
        x1_dram = nc.dram_tensor("x1_scratch", [S, D], F32).ap()
        s_out = p.new_sem("s_out")
        s_dbg = p.new_sem("s_dbg")

        ar = {"off": 0}

        def areset(off=WREG):
            ar["off"] = off

        def aF(n):
            o = ar["off"]
            ar["off"] += n
            assert ar["off"] <= ARENA_W, ar["off"]
            return arena[:, o:o + n]

        def aB(n):
            w = (n + 1) // 2
            return aF(w).bitcast(BF16)[:, 0:n]

        def aU(n):
            return aF(n).bitcast(U32)

        def aI(n):
            return aF(n).bitcast(I32)

        wstate = {"k": 0}

        def load_w(dram_v, c0, ncols, scale_bc=None, dst=None):
            k = wstate["k"]
            wstate["k"] ^= 1
            self.dma("sp", wst[k][:, :, 0:ncols], dram_v[:, :, c0:c0 + ncols], [], ["wst%d" % k], s_w[k])
            if dst is None:
                dst = wbf[k][:, :, 0:ncols]
                key = "wbf%d" % k
            else:
                dst, key = dst
            if scale_bc is None:
                self.cp("pool", dst, wst[k][:, :, 0:ncols], ["wst%d" % k], [key])
            else:
                bc, bkey = scale_bc
                self.tt("pool", dst, wst[k][:, :, 0:ncols], cap(bc, 0, [[0, 8], [1, ncols]]), ALU.mult,
                        ["wst%d" % k, bkey], [key])
            return dst, key

        def dbg_out(name, src, keys):
            if name in dbg_d:
                self.dma("sp", dbg_d[name], src, keys, [], s_dbg)

        cl = lambda o, i, w: self.dma("sp", o, i, [], [w], s_c)
        cl(cst[:], cst_d, "cst")
        cl(biasT[:], biasT_d, "biasT")
        cl(cfar[:], relb_d[31, :].partition_broadcast(128), "cfar")
        cl(condT[:], cT_d, "condT")
        cl(b_adaT[:], b_adaT_d, "b_adaT")
        cl(n1gT[:], n1gT_d, "n1gT")
        cl(n2gT[:], n2gT_d, "n2gT")
        cl(convwT[:], convwT_d, "convwT")
        cl(convbT[:], convbT_d, "convbT")
        cl(fin_bc[:], fin_d.partition_broadcast(128), "fin_bc")
        cl(gsub_bc[:], gsub_d.partition_broadcast(128), "gsub_bc")
        cl(gm_bc[:], gm_d.partition_broadcast(128), "gm_bc")
        cl(bg_bc[:], bgate_d.partition_broadcast(128), "bg_bc")
        for i in range(4):
            cl(lamv[:, i, :], lam_d[i, :].partition_broadcast(128), "lamv")
        uv_dram = nc.dram_tensor("uv_scratch", [16384, 2048], BF16).ap()
        TB = ARENA_W - 6144
        ust = [arena[:, TB + k * 1024:TB + (k + 1) * 1024] for k in range(2)]
        vst = [arena[:, TB + 2048 + k * 1024:TB + 2048 + (k + 1) * 1024] for k in range(2)]
        uvb = [arena[:, TB + 4096 + k * 1024:TB + 4096 + (k + 1) * 1024].bitcast(BF16) for k in range(2)]

        def table_build():
            for n in range(129):
                if n < 128:
                    k = n % 2
                    rows = slice(n * 128, (n + 1) * 128)
                    self.dma("sp", ust[k], u_d[rows, :], [], ["ust%d" % k], s_pu[k])
                    self.dma("sp", vst[k], v_d[rows, :], [], ["vst%d" % k], s_pv[k])
                if n >= 1:
                    m = n - 1
                    k = m % 2
                    rows = slice(m * 128, (m + 1) * 128)
                    self.cp("pool", uvb[k][:, 0:1024], ust[k], ["ust%d" % k], ["uvbA%d" % k])
                    self.cp("pool", uvb[k][:, 1024:2048], vst[k], ["vst%d" % k], ["uvbB%d" % k])
                    self.dma("pool", uv_dram[rows, :], uvb[k], ["uvbA%d" % k, "uvbB%d" % k], [], s_ps[k])
                yield

        bgen = table_build()
        areset()
        mod_rows = aF(6144)
        brow = aF(6144)
        cl(brow[0:2, :], b_ada_d.partition_broadcast(2), "brow")
        skst = wst[0].rearrange("p c n -> p (c n)")
        cl(skst, skT_d, "wst0")
        p.barrier()
        self.cp("pool", skT[:].rearrange("p c n -> p (c n)"), skst, ["wst0"], ["skT"])
        self.act(condT[:], condT[:], AF.Silu, ["condT"], ["condT"])
        self.cp("dve", ident_b[:], ident_f, ["cst"], ["ident_b"])
        for h_ in range(4):
            self.ts("dve", biasT[:, h_, :], biasT[:, h_, :], cfar[:, h_:h_ + 1], ALU.subtract, ["biasT", "cfar"], ["biasT"])
        self.ts("dve", gsub_bc[:], gsub_bc[:], 0.8, ALU.mult, ["gsub_bc"], ["gsub_bc"])
        junk64 = aF(64)
        for j in range(2):
            self.tt("dve", junk64, lamv[:, 2 * j, :], lamv[:, 2 * j + 1, :], ALU.mult, ["lamv"], ["junk64"])
            self.red(lams[:, j:j + 1], junk64, ALU.add, ["junk64"], ["lams"])
        self.act(lams[:, 2:4], lams[:, 0:2], AF.Exp, ["lams"], ["lams"])
        self.tt("dve", lams[:, 4:5], lams[:, 3:4], lams[:, 2:3], ALU.subtract, ["lams"], ["lams"])
        self.ts("dve", lams[:, 4:5], lams[:, 4:5], -0.2, ALU.add, ["lams"], ["lams"])
        neglam = lams[:, 4:5]
        for blk in range(24):
            k = wstate["k"]
            wstate["k"] ^= 1
            wk_ = "wst%d" % k
            self.dma("sp", wst[k], w_ada_v[:, :, blk * 256:(blk + 1) * 256], [], [wk_], s_w[k])
            for kc in range(8):
                self.mm(pB[0:2, 0:256], condT[:, kc, :], wst[k][:, kc, :], kc == 0, kc == 7, ["condT", wk_], ["pB"])
            self.tt("dve", mod_rows[0:2, blk * 256:(blk + 1) * 256], pB[0:2, 0:256], brow[0:2, blk * 256:(blk + 1) * 256],
                    ALU.add, ["pB", "brow"], ["mod_rows"])
            for jl in range(2):
                j = blk * 2 + jl
                for kc in range(8):
                    self.mm(pC[:, 2 * j:2 * j + 2], wst[k][:, kc, jl * 128:(jl + 1) * 128], condT[:, kc, :],
                            kc == 0, kc == 7, ["condT", wk_], ["pC"])
        self.tt("dve", modT[:], pC[:, 0:96].rearrange("p (j b) -> p j b", b=2), cap(b_adaT[:], 0, [[1, 48], [0, 2]]),
                ALU.add, ["pC", "b_adaT"], ["modT"])
        for b in range(2):
            self.stt(A1[:, b, :], modT[:, 8:16, b], 1.0, n1gT[:], ALU.add, ALU.mult, ["modT", "n1gT"], ["A1"])
            self.stt(A2[:, b, :], modT[:, 32:40, b], 1.0, n2gT[:], ALU.add, ALU.mult, ["modT", "n2gT"], ["A2"])
        if "modT" in dbg_d:
            dbg_out("modT", modT[:].rearrange("p j b -> p (j b)"), ["modT"])
        mod_dram = nc.dram_tensor("mod_scratch", [2, 6144], F32).ap()
        self.dma("sp", mod_dram, mod_rows[0:2, :], ["mod_rows"], [], s_c)
        p.barrier()

        self.chk("p0")
        for b in range(self.nseq):
            areset()
            g1_bc = aF(D)
            n2g_bc = aF(D)
            self.dma("sp", n2g_bc, n2g_d.partition_broadcast(128), [], ["n2g_bc"], s_c)
            for dst, key, col0 in ((g1_bc, "g1_bc", 2048), (B2_bc[:], "B2_bc", 3072), (A2_bc[:], "A2_bc", 4096), (g2_bc[:], "g2_bc", 5120)):
                self.dma("sp", dst, mod_dram[b, col0:col0 + 1024].partition_broadcast(128), [], [key], s_c)
            p.barrier()
            self.stt(A2_bc[:], A2_bc[:], 1.0, n2g_bc, ALU.add, ALU.mult, ["A2_bc", "n2g_bc"], ["A2_bc"])
            for blk in range(4):
                load_w(w_out_v, blk * 256, 256, scale_bc=(g1_bc[:, blk * 256:(blk + 1) * 256], "g1_bc"),
                       dst=(woutp[:, :, blk * 256:(blk + 1) * 256], "woutp"))
            p.barrier()

            areset()
            xt = aF(D)
            junk = aF(D)
            xn = aB(D)
            for i in range(NT):
                self.dma("sp", xt, x_d[b, i * 128:(i + 1) * 128, :], [], ["xt"], s_xt)
                self.act(junk, xt, AF.Square, ["xt"], ["junk", "ss"], accum=small[:, 0:1])
                self.rstd(small[:, 0:1], small[:, 1:2], small[:, 2:3], 1.0 / D, "ss", "tmpr", "rstd")
                self.ts("dve", xn, xt, small[:, 2:3], ALU.mult, ["xt", "rstd"], ["xn"])
                for c in range(8):
                    self.tr(pT[:, c * 128:(c + 1) * 128], xn[:, c * 128:(c + 1) * 128], ident_b[:], ["xn", "ident_b"], ["pT"])
                for c in range(8):
                    self.act(hT[:, c, i * 128:(i + 1) * 128], pT[:, c * 128:(c + 1) * 128], AF.Identity, ["pT", "A1", "modT"], ["hT"],
                             scale=A1[:, b, c:c + 1], bias=modT[:, c, b:b + 1])
            p.barrier()

            self.chk("p1a")

            def proj_fm(wcols, wkey, evac):
                for g in range(4):
                    for kc in range(8):
                        self.mm(pB[:, 0:512], wcols[:, kc, :], hT[:, kc, g * 512:(g + 1) * 512], kc == 0, kc == 7,
                                [wkey, "hT"], ["pB"])
                    evac(g)

            def proj_tm(wcols, wkey, ncols, evac):
                for i in range(NT):
                    for kc in range(8):
                        self.mm(pC[:, 0:ncols], hT[:, kc, i * 128:(i + 1) * 128], wcols[:, kc, 0:ncols], kc == 0, kc == 7,
                                [wkey, "hT"], ["pC"])
                    evac(i)

            areset()
            qT = aB(S)
            kT = aB(S)
            vh = aB(16 * 130).rearrange("p (i e) -> p i e", e=130)
            PTs = [aB(1024) for _ in range(2)]
            tmpSs = [aF(256) for _ in range(2)]
            o_sb = aF(128)
            o_bf = aB(128)
            junk128 = aF(128)
            self.memset("pool", vh[:, :, 128:129], 1.0, ["vh"])
            for h in range(4):
                wc, wkey = load_w(w_in_v, h * 128, 128)
                proj_fm(wc, wkey, lambda g: self.act(qT[:, g * 512:(g + 1) * 512], pB[:, 0:512], AF.Copy, ["pB"], ["qT"], scale=0.125))
                wc, wkey = load_w(w_in_v, 512 + h * 128, 128)
                proj_fm(wc, wkey, lambda g: self.act(kT[:, g * 512:(g + 1) * 512], pB[:, 0:512], AF.Copy, ["pB"], ["kT"]))
                wc, wkey = load_w(w_in_v, 1024 + h * 128, 128)
                proj_tm(wc, wkey, 128, lambda i: self.act(vh[:, i, 0:128], pC[:, 0:128], AF.Copy, ["pC"], ["vh"]))
                self.chk("p1b_proj")
                units = []
                for qb in range(NT):
                    for t in range(2):
                        if qb < 8:
                            units.append((qb, t, 0, qb))
                        else:
                            units.append((qb, t, 0, 7))
                            units.append((qb, t, 8, qb))
                Os = (pC, pD)

                def s1(n):
                    qb, t, kb0, kb1 = units[n]
                    reg = n % 2
                    ps = slice(t * 64, (t + 1) * 64)
                    for kb in range(kb0, kb1 + 1):
                        col = reg * 1024 + (kb - kb0) * 128
                        self.mm(pA[:, col:col + 128], kT[ps, kb * 128:(kb + 1) * 128], qT[ps, qb * 128:(qb + 1) * 128],
                                True, True, ["kT", "qT"], ["pA%d" % reg])

                def s2(n):
                    qb, t, kb0, kb1 = units[n]
                    reg = n % 2
                    rk, pk, tk = "pA%d" % reg, "PT%d" % reg, "tmpS%d" % reg
                    base = reg * 1024
                    far_hi = min(kb1, qb - 2)
                    nears = []
                    for kb, boff in ((qb - 1, 0), (qb, 128)):
                        if kb < kb0 or kb > kb1 or kb < 0:
                            continue
                        lc = (kb - kb0) * 128
                        self.tt("dve", tmpSs[reg][:, boff:boff + 128], pA[:, base + lc:base + lc + 128], biasT[:, h, boff:boff + 128], ALU.add,
                                [rk, "biasT"], [tk])
                        nears.append((lc, boff))
                    if far_hi >= kb0:
                        ncol = (far_hi - kb0 + 1) * 128
                        self.act(PTs[reg][:, 0:ncol], pA[:, base:base + ncol], AF.Exp, [rk, tk], [pk])
                    for lc, boff in nears:
                        self.act(PTs[reg][:, lc:lc + 128], tmpSs[reg][:, boff:boff + 128], AF.Exp, [tk], [pk])

                def s3(n):
                    qb, t, kb0, kb1 = units[n]
                    reg = n % 2
                    okey = "pC" if t == 0 else "pD"
                    for kb in range(kb0, kb1 + 1):
                        lc = (kb - kb0) * 128
                        self.mm(Os[t][:, 0:129], PTs[reg][:, lc:lc + 128], vh[:, kb, 0:129], kb == 0, kb == qb,
                                ["PT%d" % reg, "vh"], [okey])
                    if t == 1 and kb1 == qb:
                        self.recip(small[:, 4:5], pC[:, 128:129], ["pC"], ["r1"])
                        self.recip(small[:, 5:6], pD[:, 128:129], ["pD"], ["r2"])
                        self.tt("dve", small[:, 5:6], small[:, 5:6], neglam, ALU.mult, ["r2", "lams"], ["r2"])
                        self.act(o_sb, pC[:, 0:128], AF.Identity, ["pC", "r1"], ["o_sb"], scale=small[:, 4:5])
                        self.stt(o_sb, pD[:, 0:128], small[:, 5:6], o_sb, ALU.mult, ALU.add, ["pD", "r2", "o_sb"], ["o_sb"])
                        self.act(junk128, o_sb, AF.Square, ["o_sb"], ["junk128", "ss"], accum=small[:, 0:1])
                        self.rstd(small[:, 0:1], small[:, 1:2], small[:, 2:3], 1.0 / 128, "ss", "tmpr", "rstd")
                        self.stt(o_bf, o_sb, small[:, 2:3], gsub_bc[:], ALU.mult, ALU.mult, ["o_sb", "rstd", "gsub_bc"], ["o_bf"])
                        self.tr(pT[:, 0:128], o_bf, ident_b[:], ["o_bf", "ident_b"], ["pT"])
                        self.act(mixT[:, h, qb * 128:(qb + 1) * 128], pT[:, 0:128], AF.Copy, ["pT"], ["mixT"])

                for n in range(len(units) + 1):
                    if n < len(units):
                        s1(n)
                        s2(n)
                    if n >= 1:
                        s3(n - 1)
                    if b == 0 and n % 3 != 2:
                        next(bgen, None)
            if b == 0:
                for _ in bgen:
                    pass
            p.barrier()

            self.chk("p1b")
            areset()
            mraw = aF(2052)
            cacc = aF(S)
            mqT = aB(S)
            mkT = aB(S)
            mk_tm = aB(S).rearrange("p (i d) -> p i d", d=128)
            Vp = aB(16 * 130).rearrange("p (i e) -> p i e", e=130)
            sigo = aF(S).rearrange("p (i d) -> p i d", d=128)
            gates = aF(128).rearrange("p (i g) -> p i g", g=8)
            ef = aF(64).rearrange("p (i g) -> p i g", g=4)
            logf = aF(64)
            a_t = aF(64).rearrange("p (i g) -> p i g", g=4)
            e_t = aF(64).rearrange("p (i g) -> p i g", g=4)
            ebl_t = aF(64).rearrange("p (i g) -> p i g", g=4)
            tmp64 = aF(64).rearrange("p (i g) -> p i g", g=4)
            C_f = aF(130)
            tmpC = aF(130)
            C_b = aB(130)
            MS = aB(128)
            hm = aF(128)
            hm_bf = aB(128)
            junk128 = aF(128)
            sm2 = aF(8)
            self.memset("pool", mraw[:, 0:4], 0.0, ["mraw"])
            wc, wkey = load_w(w_in_v, 3584, 8)
            for i in range(NT):
                for kc in range(8):
                    self.mm(pB[:, i * 8:(i + 1) * 8], hT[:, kc, i * 128:(i + 1) * 128], wc[:, kc, 0:8], kc == 0, kc == 7, [wkey, "hT"], ["pB"])
            self.tt("dve", gates, pB[:, 0:128].rearrange("p (i g) -> p i g", g=8), cap(bg_bc[:], 0, [[0, 16], [1, 8]]), ALU.add,
                    ["pB", "bg_bc"], ["gates"])
            self.act(ef, gates[:, :, 4:8], AF.Exp, ["gates"], ["ef"], scale=-1.0)
            self.act(ef, ef, AF.Ln, ["ef"], ["ef"], bias=1.0, scale=1.0)
            self.ts("dve", logf, ef.rearrange("p i g -> p (i g)"), -1.0, ALU.mult, ["ef"], ["logf"])
            self.mm(pC[:, 0:64], maskU_f, logf, True, True, ["cst", "logf"], ["pC"])
            self.mm(pC[:, 64:128], ones_f, logf, True, True, ["cst", "logf"], ["pC"])
            self.tt("dve", tmp64, gates[:, :, 0:4], pC[:, 0:64].rearrange("p (i g) -> p i g", g=4), ALU.subtract, ["gates", "pC"], ["tmp64"])
            self.act(a_t, tmp64, AF.Exp, ["tmp64"], ["a_t"])
            self.act(e_t, pC[:, 0:64].rearrange("p (i g) -> p i g", g=4), AF.Exp, ["pC", "tmp64"], ["e_t"])
            self.act(ebl_t, pC[:, 64:128].rearrange("p (i g) -> p i g", g=4), AF.Exp, ["pC", "tmp64"], ["ebl_t"])
            for h in range(4):
                for which, dstT in ((0, mqT), (1, mkT)):
                    cc = which * 4 + h
                    wc, wkey = load_w(w_in_v, 1536 + which * 512 + h * 128, 128)
                    proj_fm(wc, wkey, lambda g: self.act(mraw[:, 3 + g * 512:3 + (g + 1) * 512], pB[:, 0:512], AF.Copy, ["pB"], ["mraw"]))
                    self.ts("dve", cacc, mraw[:, 0:S], convwT[:, cc, 0:1], ALU.mult, ["mraw", "convwT"], ["cacc"])
                    for j in range(1, 4):
                        self.stt(cacc, mraw[:, j:j + S], convwT[:, cc, j:j + 1], cacc, ALU.mult, ALU.add, ["mraw", "convwT", "cacc"], ["cacc"])
                    if which == 0:
                        self.act(mqT, cacc, AF.Silu, ["cacc", "convbT", "cst"], ["mqT"], bias=convbT[:, cc:cc + 1], scale=ones_f[:, 0:1])
                    else:
                        self.act(cacc, cacc, AF.Silu, ["cacc", "convbT", "cst"], ["cacc"], bias=convbT[:, cc:cc + 1], scale=ones_f[:, 0:1])
                        self.ts("dve", mkT, cacc, 128.0 ** -0.5, ALU.mult, ["cacc"], ["mkT"])
                for i0 in range(0, NT, 8):
                    for j in range(8):
                        self.tr(pT[:, j * 128:(j + 1) * 128], mkT[:, (i0 + j) * 128:(i0 + j + 1) * 128], ident_b[:], ["mkT", "ident_b"], ["pT"])
                    self.act(mk_tm[:, i0:i0 + 8, :], pT[:, 0:1024].rearrange("p (i d) -> p i d", d=128), AF.Copy, ["pT"], ["mk_tm"])
                wc, wkey = load_w(w_in_v, 2560 + h * 128, 128)
                proj_tm(wc, wkey, 128, lambda i: self.act(Vp[:, i, 0:128], pC[:, 0:128], AF.Identity, ["pC", "a_t"], ["Vp"], scale=a_t[:, i, h:h + 1]))
                self.cp("dve", Vp[:, :, 128:129], a_t[:, :, h:h + 1], ["a_t"], ["Vp"])
                wc, wkey = load_w(w_in_v, 3072 + h * 128, 128)
                proj_tm(wc, wkey, 128, lambda i: self.act(sigo[:, i, :], pC[:, 0:128], AF.Sigmoid, ["pC"], ["sigo"]))
                for i in range(NT):
                    tl = slice(i * 128, (i + 1) * 128)
                    self.mm(pA[:, 0:128], mkT[:, tl], mqT[:, tl], True, True, ["mkT", "mqT"], ["pA"])
                    self.tt("dve", MS, pA[:, 0:128], maskU_f, ALU.mult, ["pA", "cst"], ["MS"])
                    self.mm(pC[:, 0:129], MS, Vp[:, i, 0:129], True, i == 0, ["MS", "Vp"], ["pC"])
                    if i > 0:
                        self.mm(pC[:, 0:129], mqT[:, tl], C_b[:, 0:129], False, True, ["mqT", "C_b"], ["pC"])
                    if i < NT - 1:
                        self.mm(pD[:, 0:129], mk_tm[:, i, :], Vp[:, i, 0:129], True, True, ["mk_tm", "Vp"], ["pD"])
                        if i == 0:
                            self.cp("dve", tmpC[:, 0:129], pD[:, 0:129], ["pD"], ["tmpC"])
                        else:
                            self.tt("dve", tmpC[:, 0:129], pD[:, 0:129], C_f[:, 0:129], ALU.add, ["pD", "C_f"], ["tmpC"])
                        self.act(C_f[:, 0:129], tmpC[:, 0:129], AF.Identity, ["tmpC", "ebl_t"], ["C_f"], scale=ebl_t[:, i, h:h + 1])
                        self.cp("dve", C_b[:, 0:129], C_f[:, 0:129], ["C_f"], ["C_b"])
                    self.tt("dve", sm2[:, 0:1], pC[:, 128:129], e_t[:, i, h:h + 1], ALU.mult, ["pC", "e_t"], ["sm2"])
                    self.stt(sm2[:, 1:2], sm2[:, 0:1], -1.0, sm2[:, 0:1], ALU.mult, ALU.max, ["sm2"], ["sm2"])
                    self.ts("dve", sm2[:, 1:2], sm2[:, 1:2], 1.0, ALU.max, ["sm2"], ["sm2"])
                    self.recip(sm2[:, 2:3], sm2[:, 1:2], ["sm2"], ["sm2"])
                    self.tt("dve", sm2[:, 3:4], sm2[:, 2:3], e_t[:, i, h:h + 1], ALU.mult, ["sm2", "e_t"], ["sm2"])
                    self.stt(hm, pC[:, 0:128], sm2[:, 3:4], sigo[:, i, :], ALU.mult, ALU.mult, ["pC", "sm2", "sigo"], ["hm"])
                    self.act(junk128, hm, AF.Square, ["hm"], ["junk128", "ss"], accum=small[:, 0:1])
                    self.rstd(small[:, 0:1], small[:, 1:2], small[:, 2:3], 1.0 / 128, "ss", "tmpr", "rstd")
                    self.stt(hm_bf, hm, small[:, 2:3], gm_bc[:, h * 128:(h + 1) * 128], ALU.mult, ALU.mult, ["hm", "rstd", "gm_bc"], ["hm_bf"])
                    self.tr(pT[:, 0:128], hm_bf, ident_b[:], ["hm_bf", "ident_b"], ["pT"])
                    self.act(mixT[:, 4 + h, tl], pT[:, 0:128], AF.Copy, ["pT"], ["mixT"])
            p.barrier()

            self.chk("p1c")
            areset()
            xts = [aF(D) for _ in range(2)]
            x1s = [aF(D) for _ in range(2)]
            for i in range(NT):
                k = i % 2
                tl = slice(i * 128, (i + 1) * 128)
                pcol = k * 1024
                self.dma("sp", xts[k], x_d[b, tl, :], [], ["xts%d" % k], s_xts[k])
                for half in range(2):
                    for c in range(8):
                        self.mm(pA[:, pcol + half * 512:pcol + (half + 1) * 512], mixT[:, c, tl], woutp[:, c, half * 512:(half + 1) * 512],
                                c == 0, c == 7, ["mixT", "woutp"], ["pA%d" % k])
                self.tt("dve", x1s[k], pA[:, pcol:pcol + D], xts[k], ALU.add, ["pA%d" % k, "xts%d" % k], ["x1s%d" % k])
                self.dma("sp", x1_dram[tl, :], x1s[k], ["x1s%d" % k], [], s_x1s[k])
            p.barrier()
            for blk in range(8):
                load_w(w_q_v, blk * 256, 256, dst=(wq[:, :, blk * 256:(blk + 1) * 256], "wq"))
            p.barrier()
            areset(0)
            pers = [dict(x1=aF(D), h2=aF(D), idx_i=aI(128), gte=aF(128)) for _ in range(2)]
            R = aF(2048)
            xt = R[:, 0:1024]
            qTp = R[:, 0:1024].bitcast(BF16).rearrange("p (c t) -> p c t", t=128)
            sc = R.rearrange("p (c k) -> p c k", k=128)
            cand = R
            eq = R
            wk = aF(2048)
            wk2 = wk
            xn = aB(D)
            h2T = aB(8 * 128).rearrange("p (c t) -> p c t", t=128)
            sv = aF(256)
            si = aU(256)
            sif = aF(256)
            tops = aF(128)
            pos = aU(128)
            pa_u = aU(128)
            pb_u = aU(128)
            paf = aF(128)
            pbf = aF(128)
            i1f = aF(128)
            i2f = aF(128)
            idxf = aF(128)
            tg = aF(128)
            zs = aF(8)
            uvbuf = [mixT[:, j, :] for j in range(8)] + \
                    [woutp[:, 2 * j:2 * j + 2, :].rearrange("p c n -> p (c n)") for j in range(4)] + \
                    [aB(2048) for _ in range(NB - 12)]
            prod = [aF(D) for _ in range(2)]
            junkb = aB(D)
            diag = [aB(128) for _ in range(2)]
            dots = aF(128)
            t1 = aF(128)
            t2 = aF(128)
            wgt = aF(128)
            acc_sb = aF(D)
            V = lambda fn, r, w: p.op("dve", fn, reads=r, writes=w)

            def top16_multi(items):
                for (vals, idxs, src, scratch, tag) in items:
                    V(lambda e, o=vals, i_=src: e.max(out=o[:, 0:8], in_=i_), ["src" + tag], ["v" + tag])
                for (vals, idxs, src, scratch, tag) in items:
                    V(lambda e, o=idxs, m=vals, i_=src: e.max_index(out=o[:, 0:8], in_max=m[:, 0:8], in_values=i_), ["src" + tag, "v" + tag], ["i" + tag])
                for (vals, idxs, src, scratch, tag) in items:
                    V(lambda e, o=scratch, m=vals, i_=src: e.match_replace(out=o, in_to_replace=m[:, 0:8], in_values=i_, imm_value=-1e30), ["src" + tag, "v" + tag], ["w" + tag])
                for (vals, idxs, src, scratch, tag) in items:
                    V(lambda e, o=vals, i_=scratch: e.max(out=o[:, 8:16], in_=i_), ["w" + tag], ["v" + tag])
                for (vals, idxs, src, scratch, tag) in items:
                    V(lambda e, o=idxs, m=vals, i_=scratch: e.max_index(out=o[:, 8:16], in_max=m[:, 8:16], in_values=i_), ["w" + tag, "v" + tag], ["i" + tag])

            SA = ["srcA%d" % hp for hp in range(16)]
            VA = ["vA%d" % hp for hp in range(16)]
            IA = ["iA%d" % hp for hp in range(16)]
            SB_ = ["srcB%d" % h for h in range(8)]
            TOPS = ["vB%d" % h for h in range(8)]
            POS = ["iB%d" % h for h in range(8)]

            def front(i, par):
                tl = slice(i * 128, (i + 1) * 128)
                P = pers[par]
                x1, h2, idx_i, gte = P["x1"], P["h2"], P["idx_i"], P["gte"]
                kx1, kh2, kidx, kg = "x1_%d" % par, "h2_%d" % par, "idx_%d" % par, "gte_%d" % par
                self.dma("sp", x1, x1_dram[tl, :], [], [kx1], s_x1[par])
                yield
                self.act(junkb, x1, AF.Square, [kx1], ["junkb", "ss"], accum=small[:, 0:1])
                yield
                self.ts("dve", small[:, 1:2], small[:, 0:1], 1.0 / D, ALU.mult, ["ss"], ["tmpr"], s2=EPS, op1=ALU.add)
                yield
                self.act(small[:, 1:2], small[:, 1:2], AF.Sqrt, ["tmpr"], ["tmpr"])
                yield
                self.recip(small[:, 2:3], small[:, 1:2], ["tmpr"], ["rstd"])
                self.ts("dve", xn, x1, small[:, 2:3], ALU.mult, [kx1, "rstd"], ["xn"])
                self.stt(h2, x1, small[:, 2:3], A2_bc[:], ALU.mult, ALU.mult, [kx1, "rstd", "A2_bc"], [kh2])
                self.tt("dve", h2, h2, B2_bc[:], ALU.add, [kh2, "B2_bc"], [kh2])
                yield
                for c in range(8):
                    self.tr(pT[:, c * 128:(c + 1) * 128], xn[:, c * 128:(c + 1) * 128], ident_b[:], ["xn", "ident_b"], ["pT"])
                yield
                for c in range(8):
                    self.act(h2T[:, c, :], pT[:, c * 128:(c + 1) * 128], AF.Identity, ["pT", "A2", "modT"], ["h2T"],
                             scale=A2[:, b, c:c + 1], bias=modT[:, 24 + c, b:b + 1])
                yield
                for hp in range(16):
                    for kc in range(8):
                        self.mm(pA[:, hp * 128:(hp + 1) * 128], wq[:, kc, hp * 128:(hp + 1) * 128], h2T[:, kc, :], kc == 0, kc == 7,
                                ["wq", "h2T"], ["pA"])
                    if hp % 4 == 3:
                        yield
                self.act(qTp.rearrange("p c t -> p (c t)"), pA[:, 0:2048], AF.Copy, ["pA"], ["R"])
                yield
                for hp in range(16):
                    self.mm(pA[:, hp * 128:(hp + 1) * 128], qTp[:, hp, :], skT[:, hp, :], True, True, ["R", "skT"], ["pA"])
                yield
                self.act(sc.rearrange("p c k -> p (c k)"), pA[:, 0:2048], AF.Copy, ["pA"], ["R"] + SA)
                yield
                itemsA = [(sv[:, hp * 16:(hp + 1) * 16], si[:, hp * 16:(hp + 1) * 16], sc[:, hp, :], wk[:, hp * 128:(hp + 1) * 128], "A%d" % hp)
                          for hp in range(16)]
                top16_multi(itemsA[0:8])
                yield
                top16_multi(itemsA[8:16])
                yield
                self.cp("dve", sif, si, IA, ["sif"])
                cand4 = cand.rearrange("p (h a b) -> p h a b", h=8, a=16)
                self.tt("dve", cand4, cap(sv, 0, [[32, 8], [1, 16], [0, 16]]), cap(sv, 16, [[32, 8], [0, 16], [1, 16]]), ALU.add,
                        VA + IA, ["R"] + SA + SB_)
                yield
                top16_multi([(tops[:, h * 16:(h + 1) * 16], pos[:, h * 16:(h + 1) * 16], cand[:, h * 256:(h + 1) * 256], wk2[:, h * 256:(h + 1) * 256], "B%d" % h)
                             for h in range(8)])
                yield
                V(lambda e, o=pa_u, i_=pos: e.tensor_single_scalar(out=o, in_=i_, scalar=4, op=ALU.logical_shift_right), POS, ["pa_u"])
                V(lambda e, o=pb_u, i_=pos: e.tensor_single_scalar(out=o, in_=i_, scalar=15, op=ALU.bitwise_and), POS, ["pb_u"])
                self.cp("dve", paf, pa_u, ["pa_u"], ["paf"])
                self.cp("dve", pbf, pb_u, ["pb_u"], ["pbf"])
                yield
                eq4 = eq.rearrange("p (h k a) -> p h k a", h=8, k=16)
                for which, pf, outf, key in ((0, paf, i1f, "i1f"), (1, pbf, i2f, "i2f")):
                    self.tt("dve", eq4, cap(pf, 0, [[16, 8], [1, 16], [0, 16]]), cap(iota16, 0, [[0, 8], [0, 16], [1, 16]]), ALU.is_equal,
                            ["paf", "pbf", "cst"] + POS + TOPS, ["R"] + SB_)
                    self.tt("dve", eq4, eq4, cap(sif, which * 16, [[32, 8], [0, 16], [1, 16]]), ALU.mult, ["R", "sif"], ["R"])
                    self.red(outf.rearrange("p (h k) -> p h k", h=8), eq4, ALU.add, ["R"], [key])
                    yield
                self.stt(idxf, i1f, 128.0, i2f, ALU.mult, ALU.add, ["i1f", "i2f"], ["idxf"])
                self.cp("dve", idx_i, idxf, ["idxf"], [kidx])
                tops3 = tops.rearrange("p (h k) -> p h k", h=8)
                tg3 = tg.rearrange("p (h k) -> p h k", h=8)
                self.tt("dve", tg3, tops3, cap(tops, 0, [[16, 8], [0, 16]]), ALU.subtract, TOPS, ["tg"])
                yield
                self.act(tg, tg, AF.Exp, ["tg"], ["tg"])
                yield
                self.red(zs, tg3, ALU.add, ["tg"], ["zs"])
                self.recip(zs, zs, ["zs"], ["zs"])
                self.tt("dve", gte.rearrange("p (h k) -> p h k", h=8), tg3, cap(zs, 0, [[1, 8], [0, 16]]), ALU.mult, ["tg", "zs"], [kg])
                if b == 0 and i == 0:
                    dbg_out("x1", x1, [kx1])
                    dbg_out("idxf", idxf, ["idxf"])
                    dbg_out("gte", gte, [kg])
                    dbg_out("h2", h2, [kh2])
                if b == 0 and "x1full" in dbg_d:
                    self.dma("sp", dbg_d["x1full"][tl, :], x1, [kx1], [], s_dbg)
                    self.dma("sp", dbg_d["idxfull"][tl, :], idxf, ["idxf"], [], s_dbg)
                yield

            def back(i, par):
                tl = slice(i * 128, (i + 1) * 128)
                P = pers[par]
                x1, h2, idx_i, gte = P["x1"], P["h2"], P["idx_i"], P["gte"]
                kx1, kh2, kidx, kg = "x1_%d" % par, "h2_%d" % par, "idx_%d" % par, "gte_%d" % par
                NG = 128 // GS

                def stage_a(g):
                    for s in range(g * GS, (g + 1) * GS):
                        j = s % NB
                        uk = "uv%d" % j
                        self.gather(uvbuf[j], uv_dram, idx_i[:, s:s + 1], [kidx], [uk], s_uv[j])
                        pk = "prod%d" % (s % 2)
                        self.tt("dve", prod[s % 2], uvbuf[j][:, 0:1024], h2, ALU.mult, [uk, kh2], [pk])
                        self.act(junkb, prod[s % 2], AF.Identity, [pk], ["junkb", "dots%d" % (g % 4)], accum=dots[:, s:s + 1])

                def stage_b(g):
                    sl = slice(g * GS, (g + 1) * GS)
                    dk_, t1k, t2k, wk_ = "dots%d" % (g % 4), "t1_%d" % (g % 4), "t2_%d" % (g % 4), "wgt%d" % (g % 4)
                    self.tt("dve", t1[:, sl], dots[:, sl], dots[:, sl], ALU.mult, [dk_], [t1k])
                    self.tt("dve", t1[:, sl], t1[:, sl], dots[:, sl], ALU.mult, [t1k, dk_], [t1k])
                    self.stt(t1[:, sl], t1[:, sl], 0.044715, dots[:, sl], ALU.mult, ALU.add, [t1k, dk_], [t1k])
                    self.act(t2[:, sl], t1[:, sl], AF.Tanh, [t1k], [t2k], scale=0.7978845608028654)
                    self.ts("dve", t2[:, sl], t2[:, sl], 1.0, ALU.add, [t2k], [t2k], s2=0.5, op1=ALU.mult)
                    self.tt("dve", t2[:, sl], t2[:, sl], dots[:, sl], ALU.mult, [t2k, dk_], [t2k])
                    self.tt("dve", wgt[:, sl], t2[:, sl], gte[:, sl], ALU.mult, [t2k, kg], [wk_])
                    for s in range(g * GS, (g + 1) * GS):
                        j = s % NB
                        uk = "uv%d" % j
                        dk = "diag%d" % (s % 2)
                        self.act(diag[s % 2], ident_b[:], AF.Identity, ["ident_b", wk_], [dk], scale=wgt[:, s:s + 1])
                        self.mm(pB[:, 0:512], diag[s % 2], uvbuf[j][:, 1024:1536], s == 0, s == 127, [dk, uk], ["pB"])
                        self.mm(pC[:, 0:512], diag[s % 2], uvbuf[j][:, 1536:2048], s == 0, s == 127, [dk, uk], ["pC"])

                for g in range(NG + 1):
                    if g < NG:
                        stage_a(g)
                    if g >= 1:
                        stage_b(g - 1)
                    yield
                if b == 0 and i == 0:
                    dbg_out("dots", dots, ["dots%d" % k_ for k_ in range(4)])
                self.tt("dve", acc_sb[:, 0:512], pB[:, 0:512], g2_bc[:, 0:512], ALU.mult, ["pB", "g2_bc"], ["acc_sb"])
                self.tt("dve", acc_sb[:, 512:1024], pC[:, 0:512], g2_bc[:, 512:1024], ALU.mult, ["pC", "g2_bc"], ["acc_sb"])
                self.tt("dve", acc_sb, acc_sb, x1, ALU.add, ["acc_sb", kx1], ["acc_sb"])
                self.act(junkb, acc_sb, AF.Square, ["acc_sb"], ["junkb", "ssb"], accum=small[:, 8:9])
                self.rstd(small[:, 8:9], small[:, 9:10], small[:, 10:11], 1.0 / D, "ssb", "tmprb", "rstdb")
                self.stt(acc_sb, acc_sb, small[:, 10:11], fin_bc[:], ALU.mult, ALU.mult, ["acc_sb", "rstdb", "fin_bc"], ["acc_sb"])
                self.dma("sp", out_d[b, tl, :], acc_sb, ["acc_sb"], [], s_out)
                yield

            def run_all(gen):
                for _ in gen:
                    pass

            nt2 = self.ntiles2
            run_all(front(0, 0))
            self.chk("p2f")
            for i in range(nt2):
                par = i % 2
                fg = front(i + 1, par ^ 1) if i + 1 < nt2 else None
                if not self.no_back:
                    for _ in back(i, par):
                        if fg is not None:
                            next(fg, None)
                if fg is not None:
                    run_all(fg)
            p.barrier()


def _rel_bucket_np(n):
    n = np.asarray(n)
    max_exact = 16
    nf = np.maximum(n, 1).astype(np.float32)
    large = max_exact + (np.log(nf / np.float32(max_exact)) / np.float32(math.log(128 / max_exact))
                         * np.float32(32 - max_exact)).astype(np.int32)
    large = np.minimum(large, 31)
    return np.where(n < max_exact, n, large)


def _prep_inputs(inp):
    f = lambda a: np.ascontiguousarray(np.asarray(a, dtype=np.float32))
    x = f(inp["x"])
    c = f(inp["c"])
    shared = {}
    shared["w_ada"] = f(inp["w_ada"][0])
    shared["b_ada"] = f(inp["b_ada"][0])
    shared["b_adaT"] = f(inp["b_ada"][0].reshape(48, 128).T)
    shared["n1gT"] = f(inp["norm1_g"][0].reshape(8, 128).T)
    shared["n2gT"] = f(inp["norm2_g"][0].reshape(8, 128).T)
    shared["n2g"] = f(inp["norm2_g"][0])
    shared["w_in"] = f(inp["w_in"][0])
    shared["convwT"] = f(np.asarray(inp["conv_w"][0]).T.reshape(8, 128, 4).transpose(1, 0, 2))
    shared["convbT"] = f(np.asarray(inp["conv_b"][0]).reshape(8, 128).T)
    shared["bgate"] = f(np.concatenate([np.asarray(inp["b_igate"][0]), np.asarray(inp["b_fgate"][0])]))
    shared["lam"] = f(np.stack([np.asarray(inp[k][0]) for k in ("lam_q1", "lam_k1", "lam_q2", "lam_k2")]))
    shared["gsub"] = f(inp["diff_sub_g"][0])
    shared["gm"] = f(inp["mlstm_norm_g"][0])
    shared["w_out"] = f(inp["w_out"][0])
    shared["w_q"] = f(inp["peer_w_q"][0])
    sk = np.asarray(inp["peer_sub_keys"][0])
    shared["skT"] = f(sk.transpose(3, 0, 1, 2).reshape(128, 2048))
    shared["peer_u"] = f(inp["peer_u"][0])
    shared["peer_v"] = f(inp["peer_v"][0])
    rb = f(inp["rel_bias"])
    shared["rel_bias"] = rb
    kk = np.arange(128)[:, None]
    qq = np.arange(128)[None, :]
    rel1 = qq - kk + 128
    rel0 = qq - kk
    b1 = rb[_rel_bucket_np(rel1)]
    b0 = rb[_rel_bucket_np(np.maximum(rel0, 0))]
    biasT = np.empty((128, 4, 256), np.float32)
    biasT[:, :, 0:128] = b1.transpose(0, 2, 1)
    biasT[:, :, 128:256] = b0.transpose(0, 2, 1)
    biasT[:, :, 128:256][np.broadcast_to((rel0 < 0)[:, None, :], (128, 4, 128))] = -1e9
    shared["biasT"] = biasT
    shared["final_g"] = f(inp["final_g"])
    cst = np.zeros((128, 400), np.float32)
    cst[:, 0:128] = np.eye(128)
    cst[:, 128:256] = np.triu(np.ones((128, 128)))
    cst[:, 256:384] = 1.0
    cst[:, 384:400] = np.arange(16)[None, :]
    shared["cst"] = cst
    in_maps = []
    for core in range(8):
        m = dict(shared)
        m["x"] = np.ascontiguousarray(x[2 * core:2 * core + 2])
        cc = c[2 * core:2 * core + 2]
        m["cT"] = np.ascontiguousarray(cc.reshape(2, 8, 128).transpose(2, 1, 0))
        in_maps.append(m)
    return in_maps


_NC_CACHE = {}


def kernel(**inputs):
    in_maps = _prep_inputs(inputs)
    if "nc" not in _NC_CACHE:
        _NC_CACHE["nc"] = KB().build()
    nc = _NC_CACHE["nc"]
    res = run_bass_kernel_spmd(nc, in_maps, core_ids=list(range(8)))
    out = np.concatenate([np.asarray(r["out"]) for r in res.results], axis=0)
    return out.astype(np.float32)
```

```python
import math
import numpy as np
import concourse.bass as bass
import concourse.mybir as mybir
from concourse.bass_utils import run_bass_kernel_spmd

F32 = mybir.dt.float32
BF16 = mybir.dt.bfloat16
U32 = mybir.dt.uint32
I32 = mybir.dt.int32
ALU = mybir.AluOpType
AF = mybir.ActivationFunctionType
AX = mybir.AxisListType

ENGS = ("pe", "act", "dve", "pool", "sp")
SAME_ENGINE_SYNC = {"pe": False, "act": True, "dve": True, "pool": True, "sp": True}

S = 2048
D = 1024
NT = 16
EPS = 1e-6
NU = 3
NV = 3
ARENA_W = 24320
WREG = 6144
GS = 4
NB = 16


class Prog:
    def __init__(self, nc):
        self.nc = nc
        self.q = {e: [] for e in ENGS}
        self.cnt = {}
        self.semobj = {}
        self.lastw = {}
        self.readers = {}
        self.seen = {e: {} for e in ENGS}
        self.ownsem = {}
        for e in ENGS:
            s = nc.alloc_semaphore("es_" + e)
            self.ownsem[e] = s.name
            self.semobj[s.name] = s
            self.cnt[s.name] = 0
        self.nsem = 0

    def new_sem(self, name=None):
        self.nsem += 1
        s = self.nc.alloc_semaphore(name or ("ds_%d" % self.nsem))
        self.semobj[s.name] = s
        self.cnt[s.name] = 0
        return s.name

    def op(self, eng, fn, reads=(), writes=(), sem=None, inc=None):
        if sem is None:
            sem = self.ownsem[eng]
            inc = 1
        elif inc is None:
            inc = 16
        deps = {}

        def add(d):
            if d is None:
                return
            s, v = d
            if deps.get(s, 0) < v:
                deps[s] = v

        for k in reads:
            add(self.lastw.get(k))
        for k in writes:
            add(self.lastw.get(k))
            for s, v in self.readers.get(k, {}).items():
                add((s, v))
        waits = []
        for s, v in deps.items():
            if s == self.ownsem[eng] and not SAME_ENGINE_SYNC[eng]:
                continue
            if self.seen[eng].get(s, 0) >= v:
                continue
            self.seen[eng][s] = v
            waits.append((s, v))
        self.cnt[sem] += inc
        val = self.cnt[sem]
        for k in reads:
            r = self.readers.setdefault(k, {})
            if r.get(sem, 0) < val:
                r[sem] = val
        for k in writes:
            self.lastw[k] = (sem, val)
            self.readers[k] = {}
        self.q[eng].append((waits, fn, sem, inc))

    def barrier(self):
        snap = dict(self.cnt)
        for e in ENGS:
            waits = []
            for s, v in snap.items():
                if v > self.seen[e].get(s, 0):
                    self.seen[e][s] = v
                    waits.append((s, v))
            own = self.ownsem[e]
            self.cnt[own] += 1
            self.q[e].append((waits, (lambda eng: eng.nop()), own, 1))
        self.lastw.clear()
        self.readers.clear()

    def emit(self):
        nc = self.nc
        finals = [(s, v) for s, v in self.cnt.items() if v > 0]
        with nc.Block() as block:
            def run(eng_name, wait_all=False):
                def f(eng):
                    for waits, fn, sem, inc in self.q[eng_name]:
                        for s, v in waits:
                            eng.wait_ge(self.semobj[s], v)
                        ins = fn(eng)
                        ins.then_inc(self.semobj[sem], inc)
                    if wait_all:
                        for s, v in finals:
                            eng.wait_ge(self.semobj[s], v)
                return f
            block.tensor(run("pe"))
            block.scalar(run("act"))
            block.vector(run("dve"))
            block.gpsimd(run("pool"))
            block.sync(run("sp", wait_all=True))


def cap(t, off, dims):
    return bass.AP(t.tensor, t.offset + off, [list(t.ap[0])] + [list(d) for d in dims])


class _Stop(Exception):
    pass


class KB:
    def __init__(self, dbg=None, nseq=2, ntiles2=NT, stop=None, no_back=False):
        self.no_back = no_back
        self.dbg = dbg
        self.stop = stop
        self.nseq = nseq
        self.ntiles2 = ntiles2
        nc = self.nc = bass.Bass("TRN2", target_bir_lowering=False)
        self.p = Prog(nc)
        self._n = 0

    def mm(self, out, lhsT, rhs, start, stop, r, w):
        self.p.op("pe", lambda e: e.matmul(out, lhsT=lhsT, rhs=rhs, start=start, stop=stop), reads=r, writes=w)

    def tr(self, out, in_, ident, r, w):
        self.p.op("pe", lambda e: e.transpose(out=out, in_=in_, identity=ident), reads=r, writes=w)

    def act(self, out, in_, func, r, w, scale=None, bias=None, accum=None):
        kw = {}
        if scale is not None:
            kw["scale"] = scale
        if bias is not None:
            kw["bias"] = bias
        if accum is not None:
            kw["accum_out"] = accum
        self.p.op("act", lambda e: e.activation(out=out, in_=in_, func=func, **kw), reads=r, writes=w)

    def tt(self, eng, out, in0, in1, op, r, w):
        self.p.op(eng, lambda e: e.tensor_tensor(out=out, in0=in0, in1=in1, op=op), reads=r, writes=w)

    def ts(self, eng, out, in0, s1, op0, r, w, s2=None, op1=None):
        if op1 is None:
            self.p.op(eng, lambda e: e.tensor_scalar(out=out, in0=in0, scalar1=s1, scalar2=None, op0=op0), reads=r, writes=w)
        else:
            self.p.op(eng, lambda e: e.tensor_scalar(out=out, in0=in0, scalar1=s1, scalar2=s2, op0=op0, op1=op1), reads=r, writes=w)

    def stt(self, out, in0, scalar, in1, op0, op1, r, w, accum=None):
        if accum is None:
            self.p.op("dve", lambda e: e.scalar_tensor_tensor(out=out, in0=in0, scalar=scalar, in1=in1, op0=op0, op1=op1), reads=r, writes=w)
        else:
            self.p.op("dve", lambda e: e.scalar_tensor_tensor(out=out, in0=in0, scalar=scalar, in1=in1, op0=op0, op1=op1, accum_out=accum), reads=r, writes=w)

    def cp(self, eng, out, in_, r, w):
        self.p.op(eng, lambda e: e.tensor_copy(out=out, in_=in_), reads=r, writes=w)

    def red(self, out, in_, op, r, w):
        self.p.op("dve", lambda e: e.tensor_reduce(out=out, in_=in_, axis=AX.X, op=op), reads=r, writes=w)

    def recip(self, out, in_, r, w):
        self.p.op("dve", lambda e: e.reciprocal(out=out, in_=in_), reads=r, writes=w)

    def memset(self, eng, ap, val, w):
        self.p.op(eng, lambda e: e.memset(ap, val), writes=w)

    def dma(self, eng, out, in_, r, w, sem):
        self.p.op(eng, lambda e: e.dma_start(out=out, in_=in_), reads=r, writes=w, sem=sem)

    def gather(self, out, table, idx, r, w, sem):
        self.p.op("pool", lambda e: e.indirect_dma_start(
            out=out, out_offset=None, in_=table,
            in_offset=bass.IndirectOffsetOnAxis(ap=idx, axis=0)), reads=r, writes=w, sem=sem)

    def sb(self, shape, dt, name=None):
        self._n += 1
        return self.nc.alloc_sbuf_tensor("sb_" + (name or ("t%d" % self._n)), shape, dt)

    def rstd(self, ss, tmp, out, inv_n, key_ss, key_tmp, key_out):
        self.ts("dve", tmp, ss, inv_n, ALU.mult, [key_ss], [key_tmp], s2=EPS, op1=ALU.add)
        self.act(tmp, tmp, AF.Sqrt, [key_tmp], [key_tmp])
        self.recip(out, tmp, [key_tmp], [key_out])

    def build(self):
        try:
            self._build()
        except _Stop:
            pass
        self.p.emit()
        return self.nc

    def chk(self, tag):
        if self.stop == tag:
            raise _Stop()

    def _build(self):
        nc, p = self.nc, self.p

        def din(name, shape, dt=F32):
            return nc.dram_tensor(name, shape, dt, kind="ExternalInput").ap()

        x_d = din("x", [2, S, D])
        cT_d = din("cT", [128, 8, 2])
        w_ada_d = din("w_ada", [D, 6 * D])
        b_ada_d = din("b_ada", [6 * D])
        b_adaT_d = din("b_adaT", [128, 48])
        n1gT_d = din("n1gT", [128, 8])
        n2gT_d = din("n2gT", [128, 8])
        n2g_d = din("n2g", [D])
        w_in_d = din("w_in", [D, 3592])
        convwT_d = din("convwT", [128, 8, 4])
        convbT_d = din("convbT", [128, 8])
        bgate_d = din("bgate", [8])
        lam_d = din("lam", [4, 64])
        gsub_d = din("gsub", [128])
        gm_d = din("gm", [512])
        w_out_d = din("w_out", [D, D])
        w_q_d = din("w_q", [D, 2048])
        skT_d = din("skT", [128, 2048])
        u_d = din("peer_u", [16384, D])
        v_d = din("peer_v", [16384, D])
        relb_d = din("rel_bias", [32, 4])
        biasT_d = din("biasT", [128, 4, 256])
        fin_d = din("final_g", [D])
        cst_d = din("cst", [128, 400])
        out_d = nc.dram_tensor("out", [2, S, D], F32, kind="ExternalOutput").ap()
        dbg_d = {}
        if self.dbg:
            for name, shape in self.dbg.items():
                dbg_d[name] = nc.dram_tensor("dbg_" + name, shape, F32, kind="ExternalOutput").ap()

        w_ada_v = w_ada_d.rearrange("(c p) n -> p c n", p=128)
        w_in_v = w_in_d.rearrange("(c p) n -> p c n", p=128)
        w_out_v = w_out_d.rearrange("(c p) n -> p c n", p=128)
        w_q_v = w_q_d.rearrange("(c p) n -> p c n", p=128)

        sb = self.sb
        cst = sb([128, 400], F32, "cst")
        ident_f = cst[:, 0:128]
        maskU_f = cst[:, 128:256]
        ones_f = cst[:, 256:384]
        iota16 = cst[:, 384:400]
        ident_b = sb([128, 128], BF16, "ident_b")
        biasT = sb([128, 4, 256], F32, "biasT")
        cfar = sb([128, 4], F32, "cfar")
        condT = sb([128, 8, 2], F32, "condT")
        modT = sb([128, 48, 2], F32, "modT")
        b_adaT = sb([128, 48], F32, "b_adaT")
        n1gT = sb([128, 8], F32, "n1gT")
        n2gT = sb([128, 8], F32, "n2gT")
        convwT = sb([128, 8, 4], F32, "convwT")
        convbT = sb([128, 8], F32, "convbT")
        A1 = sb([128, 2, 8], F32, "A1")
        A2 = sb([128, 2, 8], F32, "A2")
        fin_bc = sb([128, D], F32, "fin_bc")
        gsub_bc = sb([128, 128], F32, "gsub_bc")
        gm_bc = sb([128, 512], F32, "gm_bc")
        bg_bc = sb([128, 8], F32, "bg_bc")
        A2_bc = sb([128, D], F32, "A2_bc")
        B2_bc = sb([128, D], F32, "B2_bc")
        g2_bc = sb([128, D], F32, "g2_bc")
        lamv = sb([128, 4, 64], F32, "lamv")
        lams = sb([128, 8], F32, "lams")
        small = sb([128, 16], F32, "small")
        mixT = sb([128, 8, S], BF16, "mixT")
        arena1 = sb([128, 8, S], BF16, "arena1")
        hT = arena1
        wq = arena1
        woutp = sb([128, 8, D], BF16, "woutp")
        skT = sb([128, 16, 128], BF16, "skT")
        arena = sb([128, ARENA_W], F32, "arena")
        wst = [arena[:, i * 2048:(i + 1) * 2048].rearrange("p (c n) -> p c n", c=8) for i in range(2)]
        wbf = [arena[:, 4096 + i * 1024:4096 + (i + 1) * 1024].bitcast(BF16).rearrange("p (c n) -> p c n", c=8) for i in range(2)]

        pA = nc.alloc_psum_tensor("pA", [128, 2048], F32)
        pT = nc.alloc_psum_tensor("pT", [128, 1024], BF16)
        pB = nc.alloc_psum_tensor("pB", [128, 512], F32)
        pC = nc.alloc_psum_tensor("pC", [128, 512], F32)
        pD = nc.alloc_psum_tensor("pD", [128, 512], F32)

        s_c = p.new_sem("s_const")
        s_xt = p.new_sem("s_xt")
        s_w = [p.new_sem("s_w0"), p.new_sem("s_w1")]
        s_uv = [p.new_sem("s_uv%d" % i) for i in range(NB)]
        s_pu = [p.new_sem("s_pu%d" % i) for i in range(2)]
        s_pv = [p.new_sem("s_pv%d" % i) for i in range(2)]
        s_ps = [p.new_sem("s_ps%d" % i) for i in range(2)]
        s_xts = [p.new_sem("s_xts%d" % i) for i in range(2)]
        s_x1s = [p.new_sem("s_x1s%d" % i) for i in range(2)]
        s_x1 = [p.new_sem("s_x1_%d" % i) for i in range(2)]
        x1_dram = nc.dram_tensor("x1_scratch", [S, D], F32).ap()
        s_out = p.new_sem("s_out")
        s_dbg = p.new_sem("s_dbg")

        ar = {"off": 0}

        def areset(off=WREG):
            ar["off"] = off

        def aF(n):
            o = ar["off"]
            ar["off"] += n
            assert ar["off"] <= ARENA_W, ar["off"]
            return arena[:, o:o + n]

        def aB(n):
            w = (n + 1) // 2
            return aF(w).bitcast(BF16)[:, 0:n]

        def aU(n):
            return aF(n).bitcast(U32)

        def aI(n):
            return aF(n).bitcast(I32)

        wstate = {"k": 0}

        def load_w(dram_v, c0, ncols, scale_bc=None, dst=None):
            k = wstate["k"]
            wstate["k"] ^= 1
            self.dma("sp", wst[k][:, :, 0:ncols], dram_v[:, :, c0:c0 + ncols], [], ["wst%d" % k], s_w[k])
            if dst is None:
                dst = wbf[k][:, :, 0:ncols]
                key = "wbf%d" % k
            else:
                dst, key = dst
            if scale_bc is None:
                self.cp("pool", dst, wst[k][:, :, 0:ncols], ["wst%d" % k], [key])
            else:
                bc, bkey = scale_bc
                self.tt("pool", dst, wst[k][:, :, 0:ncols], cap(bc, 0, [[0, 8], [1, ncols]]), ALU.mult,
                        ["wst%d" % k, bkey], [key])
            return dst, key

        def dbg_out(name, src, keys):
            if name in dbg_d:
                self.dma("sp", dbg_d[name], src, keys, [], s_dbg)

        cl = lambda o, i, w: self.dma("sp", o, i, [], [w], s_c)
        cl(cst[:], cst_d, "cst")
        cl(biasT[:], biasT_d, "biasT")
        cl(cfar[:], relb_d[31, :].partition_broadcast(128), "cfar")
        cl(condT[:], cT_d, "condT")
        cl(b_adaT[:], b_adaT_d, "b_adaT")
        cl(n1gT[:], n1gT_d, "n1gT")
        cl(n2gT[:], n2gT_d, "n2gT")
        cl(convwT[:], convwT_d, "convwT")
        cl(convbT[:], convbT_d, "convbT")
        cl(fin_bc[:], fin_d.partition_broadcast(128), "fin_bc")
        cl(gsub_bc[:], gsub_d.partition_broadcast(128), "gsub_bc")
        cl(gm_bc[:], gm_d.partition_broadcast(128), "gm_bc")
        cl(bg_bc[:], bgate_d.partition_broadcast(128), "bg_bc")
        for i in range(4):
            cl(lamv[:, i, :], lam_d[i, :].partition_broadcast(128), "lamv")
        uv_dram = nc.dram_tensor("uv_scratch", [16384, 2048], BF16).ap()
        TB = ARENA_W - 6144
        ust = [arena[:, TB + k * 1024:TB + (k + 1) * 1024] for k in range(2)]
        vst = [arena[:, TB + 2048 + k * 1024:TB + 2048 + (k + 1) * 1024] for k in range(2)]
        uvb = [arena[:, TB + 4096 + k * 1024:TB + 4096 + (k + 1) * 1024].bitcast(BF16) for k in range(2)]

        def table_build():
            for n in range(129):
                if n < 128:
                    k = n % 2
                    rows = slice(n * 128, (n + 1) * 128)
                    self.dma("sp", ust[k], u_d[rows, :], [], ["ust%d" % k], s_pu[k])
                    self.dma("sp", vst[k], v_d[rows, :], [], ["vst%d" % k], s_pv[k])
                if n >= 1:
                    m = n - 1
                    k = m % 2
                    rows = slice(m * 128, (m + 1) * 128)
                    self.cp("pool", uvb[k][:, 0:1024], ust[k], ["ust%d" % k], ["uvbA%d" % k])
                    self.cp("pool", uvb[k][:, 1024:2048], vst[k], ["vst%d" % k], ["uvbB%d" % k])
                    self.dma("pool", uv_dram[rows, :], uvb[k], ["uvbA%d" % k, "uvbB%d" % k], [], s_ps[k])
                yield

        bgen = table_build()
        areset()
        mod_rows = aF(6144)
        brow = aF(6144)
        cl(brow[0:2, :], b_ada_d.partition_broadcast(2), "brow")
        skst = wst[0].rearrange("p c n -> p (c n)")
        cl(skst, skT_d, "wst0")
        p.barrier()
        self.cp("pool", skT[:].rearrange("p c n -> p (c n)"), skst, ["wst0"], ["skT"])
        self.act(condT[:], condT[:], AF.Silu, ["condT"], ["condT"])
        self.cp("dve", ident_b[:], ident_f, ["cst"], ["ident_b"])
        for h_ in range(4):
            self.ts("dve", biasT[:, h_, :], biasT[:, h_, :], cfar[:, h_:h_ + 1], ALU.subtract, ["biasT", "cfar"], ["biasT"])
        self.ts("dve", gsub_bc[:], gsub_bc[:], 0.8, ALU.mult, ["gsub_bc"], ["gsub_bc"])
        junk64 = aF(64)
        for j in range(2):
            self.tt("dve", junk64, lamv[:, 2 * j, :], lamv[:, 2 * j + 1, :], ALU.mult, ["lamv"], ["junk64"])
            self.red(lams[:, j:j + 1], junk64, ALU.add, ["junk64"], ["lams"])
        self.act(lams[:, 2:4], lams[:, 0:2], AF.Exp, ["lams"], ["lams"])
        self.tt("dve", lams[:, 4:5], lams[:, 3:4], lams[:, 2:3], ALU.subtract, ["lams"], ["lams"])
        self.ts("dve", lams[:, 4:5], lams[:, 4:5], -0.2, ALU.add, ["lams"], ["lams"])
        neglam = lams[:, 4:5]
        for blk in range(24):
            k = wstate["k"]
            wstate["k"] ^= 1
            wk_ = "wst%d" % k
            self.dma("sp", wst[k], w_ada_v[:, :, blk * 256:(blk + 1) * 256], [], [wk_], s_w[k])
            for kc in range(8):
                self.mm(pB[0:2, 0:256], condT[:, kc, :], wst[k][:, kc, :], kc == 0, kc == 7, ["condT", wk_], ["pB"])
            self.tt("dve", mod_rows[0:2, blk * 256:(blk + 1) * 256], pB[0:2, 0:256], brow[0:2, blk * 256:(blk + 1) * 256],
                    ALU.add, ["pB", "brow"], ["mod_rows"])
            for jl in range(2):
                j = blk * 2 + jl
                for kc in range(8):
                    self.mm(pC[:, 2 * j:2 * j + 2], wst[k][:, kc, jl * 128:(jl + 1) * 128], condT[:, kc, :],
                            kc == 0, kc == 7, ["condT", wk_], ["pC"])
        self.tt("dve", modT[:], pC[:, 0:96].rearrange("p (j b) -> p j b", b=2), cap(b_adaT[:], 0, [[1, 48], [0, 2]]),
                ALU.add, ["pC", "b_adaT"], ["modT"])
        for b in range(2):
            self.stt(A1[:, b, :], modT[:, 8:16, b], 1.0, n1gT[:], ALU.add, ALU.mult, ["modT", "n1gT"], ["A1"])
            self.stt(A2[:, b, :], modT[:, 32:40, b], 1.0, n2gT[:], ALU.add, ALU.mult, ["modT", "n2gT"], ["A2"])
        if "modT" in dbg_d:
            dbg_out("modT", modT[:].rearrange("p j b -> p (j b)"), ["modT"])
        mod_dram = nc.dram_tensor("mod_scratch", [2, 6144], F32).ap()
        self.dma("sp", mod_dram, mod_rows[0:2, :], ["mod_rows"], [], s_c)
        p.barrier()

        self.chk("p0")
        for b in range(self.nseq):
            areset()
            g1_bc = aF(D)
            n2g_bc = aF(D)
            self.dma("sp", n2g_bc, n2g_d.partition_broadcast(128), [], ["n2g_bc"], s_c)
            for dst, key, col0 in ((g1_bc, "g1_bc", 2048), (B2_bc[:], "B2_bc", 3072), (A2_bc[:], "A2_bc", 4096), (g2_bc[:], "g2_bc", 5120)):
                self.dma("sp", dst, mod_dram[b, col0:col0 + 1024].partition_broadcast(128), [], [key], s_c)
            p.barrier()
            self.stt(A2_bc[:], A2_bc[:], 1.0, n2g_bc, ALU.add, ALU.mult, ["A2_bc", "n2g_bc"], ["A2_bc"])
            for blk in range(4):
                load_w(w_out_v, blk * 256, 256, scale_bc=(g1_bc[:, blk * 256:(blk + 1) * 256], "g1_bc"),
                       dst=(woutp[:, :, blk * 256:(blk + 1) * 256], "woutp"))
            p.barrier()

            areset()
            xt = aF(D)
            junk = aF(D)
            xn = aB(D)
            for i in range(NT):
                self.dma("sp", xt, x_d[b, i * 128:(i + 1) * 128, :], [], ["xt"], s_xt)
                self.act(junk, xt, AF.Square, ["xt"], ["junk", "ss"], accum=small[:, 0:1])
                self.rstd(small[:, 0:1], small[:, 1:2], small[:, 2:3], 1.0 / D, "ss", "tmpr", "rstd")
                self.ts("dve", xn, xt, small[:, 2:3], ALU.mult, ["xt", "rstd"], ["xn"])
                for c in range(8):
                    self.tr(pT[:, c * 128:(c + 1) * 128], xn[:, c * 128:(c + 1) * 128], ident_b[:], ["xn", "ident_b"], ["pT"])
                for c in range(8):
                    self.act(hT[:, c, i * 128:(i + 1) * 128], pT[:, c * 128:(c + 1) * 128], AF.Identity, ["pT", "A1", "modT"], ["hT"],
                             scale=A1[:, b, c:c + 1], bias=modT[:, c, b:b + 1])
            p.barrier()

            self.chk("p1a")

            def proj_fm(wcols, wkey, evac):
                for g in range(4):
                    for kc in range(8):
                        self.mm(pB[:, 0:512], wcols[:, kc, :], hT[:, kc, g * 512:(g + 1) * 512], kc == 0, kc == 7,
                                [wkey, "hT"], ["pB"])
                    evac(g)

            def proj_tm(wcols, wkey, ncols, evac):
                for i in range(NT):
                    for kc in range(8):
                        self.mm(pC[:, 0:ncols], hT[:, kc, i * 128:(i + 1) * 128], wcols[:, kc, 0:ncols], kc == 0, kc == 7,
                                [wkey, "hT"], ["pC"])
                    evac(i)

            areset()
            qT = aB(S)
            kT = aB(S)
            vh = aB(16 * 130).rearrange("p (i e) -> p i e", e=130)
            PTs = [aB(1024) for _ in range(2)]
            tmpSs = [aF(256) for _ in range(2)]
            o_sb = aF(128)
            o_bf = aB(128)
            junk128 = aF(128)
            self.memset("pool", vh[:, :, 128:129], 1.0, ["vh"])
            for h in range(4):
                wc, wkey = load_w(w_in_v, h * 128, 128)
                proj_fm(wc, wkey, lambda g: self.act(qT[:, g * 512:(g + 1) * 512], pB[:, 0:512], AF.Copy, ["pB"], ["qT"], scale=0.125))
                wc, wkey = load_w(w_in_v, 512 + h * 128, 128)
                proj_fm(wc, wkey, lambda g: self.act(kT[:, g * 512:(g + 1) * 512], pB[:, 0:512], AF.Copy, ["pB"], ["kT"]))
                wc, wkey = load_w(w_in_v, 1024 + h * 128, 128)
                proj_tm(wc, wkey, 128, lambda i: self.act(vh[:, i, 0:128], pC[:, 0:128], AF.Copy, ["pC"], ["vh"]))
                self.chk("p1b_proj")
                units = []
                for qb in range(NT):
                    for t in range(2):
                        if qb < 8:
                            units.append((qb, t, 0, qb))
                        else:
                            units.append((qb, t, 0, 7))
                            units.append((qb, t, 8, qb))
                Os = (pC, pD)

                def s1(n):
                    qb, t, kb0, kb1 = units[n]
                    reg = n % 2
                    ps = slice(t * 64, (t + 1) * 64)
                    for kb in range(kb0, kb1 + 1):
                        col = reg * 1024 + (kb - kb0) * 128
                        self.mm(pA[:, col:col + 128], kT[ps, kb * 128:(kb + 1) * 128], qT[ps, qb * 128:(qb + 1) * 128],
                                True, True, ["kT", "qT"], ["pA%d" % reg])

                def s2(n):
                    qb, t, kb0, kb1 = units[n]
                    reg = n % 2
                    rk, pk, tk = "pA%d" % reg, "PT%d" % reg, "tmpS%d" % reg
                    base = reg * 1024
                    far_hi = min(kb1, qb - 2)
                    nears = []
                    for kb, boff in ((qb - 1, 0), (qb, 128)):
                        if kb < kb0 or kb > kb1 or kb < 0:
                            continue
                        lc = (kb - kb0) * 128
                        self.tt("dve", tmpSs[reg][:, boff:boff + 128], pA[:, base + lc:base + lc + 128], biasT[:, h, boff:boff + 128], ALU.add,
                                [rk, "biasT"], [tk])
                        nears.append((lc, boff))
                    if far_hi >= kb0:
                        ncol = (far_hi - kb0 + 1) * 128
                        self.act(PTs[reg][:, 0:ncol], pA[:, base:base + ncol], AF.Exp, [rk, tk], [pk])
                    for lc, boff in nears:
                        self.act(PTs[reg][:, lc:lc + 128], tmpSs[reg][:, boff:boff + 128], AF.Exp, [tk], [pk])

                def s3(n):
                    qb, t, kb0, kb1 = units[n]
                    reg = n % 2
                    okey = "pC" if t == 0 else "pD"
                    for kb in range(kb0, kb1 + 1):
                        lc = (kb - kb0) * 128
                        self.mm(Os[t][:, 0:129], PTs[reg][:, lc:lc + 128], vh[:, kb, 0:129], kb == 0, kb == qb,
                                ["PT%d" % reg, "vh"], [okey])
                    if t == 1 and kb1 == qb:
                        self.recip(small[:, 4:5], pC[:, 128:129], ["pC"], ["r1"])
                        self.recip(small[:, 5:6], pD[:, 128:129], ["pD"], ["r2"])
                        self.tt("dve", small[:, 5:6], small[:, 5:6], neglam, ALU.mult, ["r2", "lams"], ["r2"])
                        self.act(o_sb, pC[:, 0:128], AF.Identity, ["pC", "r1"], ["o_sb"], scale=small[:, 4:5])
                        self.stt(o_sb, pD[:, 0:128], small[:, 5:6], o_sb, ALU.mult, ALU.add, ["pD", "r2", "o_sb"], ["o_sb"])
                        self.act(junk128, o_sb, AF.Square, ["o_sb"], ["junk128", "ss"], accum=small[:, 0:1])
                        self.rstd(small[:, 0:1], small[:, 1:2], small[:, 2:3], 1.0 / 128, "ss", "tmpr", "rstd")
                        self.stt(o_bf, o_sb, small[:, 2:3], gsub_bc[:], ALU.mult, ALU.mult, ["o_sb", "rstd", "gsub_bc"], ["o_bf"])
                        self.tr(pT[:, 0:128], o_bf, ident_b[:], ["o_bf", "ident_b"], ["pT"])
                        self.act(mixT[:, h, qb * 128:(qb + 1) * 128], pT[:, 0:128], AF.Copy, ["pT"], ["mixT"])

                for n in range(len(units) + 1):
                    if n < len(units):
                        s1(n)
                        s2(n)
                    if n >= 1:
                        s3(n - 1)
                    if b == 0 and n % 3 != 2:
                        next(bgen, None)
            if b == 0:
                for _ in bgen:
                    pass
            p.barrier()

            self.chk("p1b")
            areset()
            mraw = aF(2052)
            cacc = aF(S)
            mqT = aB(S)
            mkT = aB(S)
            mk_tm = aB(S).rearrange("p (i d) -> p i d", d=128)
            Vp = aB(16 * 130).rearrange("p (i e) -> p i e", e=130)
            sigo = aF(S).rearrange("p (i d) -> p i d", d=128)
            gates = aF(128).rearrange("p (i g) -> p i g", g=8)
            ef = aF(64).rearrange("p (i g) -> p i g", g=4)
            logf = aF(64)
            a_t = aF(64).rearrange("p (i g) -> p i g", g=4)
            e_t = aF(64).rearrange("p (i g) -> p i g", g=4)
            ebl_t = aF(64).rearrange("p (i g) -> p i g", g=4)
            tmp64 = aF(64).rearrange("p (i g) -> p i g", g=4)
            C_f = aF(130)
            tmpC = aF(130)
            C_b = aB(130)
            MS = aB(128)
            hm = aF(128)
            hm_bf = aB(128)
            junk128 = aF(128)
            sm2 = aF(8)
            self.memset("pool", mraw[:, 0:4], 0.0, ["mraw"])
            wc, wkey = load_w(w_in_v, 3584, 8)
            for i in range(NT):
                for kc in range(8):
                    self.mm(pB[:, i * 8:(i + 1) * 8], hT[:, kc, i * 128:(i + 1) * 128], wc[:, kc, 0:8], kc == 0, kc == 7, [wkey, "hT"], ["pB"])
            self.tt("dve", gates, pB[:, 0:128].rearrange("p (i g) -> p i g", g=8), cap(bg_bc[:], 0, [[0, 16], [1, 8]]), ALU.add,
                    ["pB", "bg_bc"], ["gates"])
            self.act(ef, gates[:, :, 4:8], AF.Exp, ["gates"], ["ef"], scale=-1.0)
            self.act(ef, ef, AF.Ln, ["ef"], ["ef"], bias=1.0, scale=1.0)
            self.ts("dve", logf, ef.rearrange("p i g -> p (i g)"), -1.0, ALU.mult, ["ef"], ["logf"])
            self.mm(pC[:, 0:64], maskU_f, logf, True, True, ["cst", "logf"], ["pC"])
            self.mm(pC[:, 64:128], ones_f, logf, True, True, ["cst", "logf"], ["pC"])
            self.tt("dve", tmp64, gates[:, :, 0:4], pC[:, 0:64].rearrange("p (i g) -> p i g", g=4), ALU.subtract, ["gates", "pC"], ["tmp64"])
            self.act(a_t, tmp64, AF.Exp, ["tmp64"], ["a_t"])
            self.act(e_t, pC[:, 0:64].rearrange("p (i g) -> p i g", g=4), AF.Exp, ["pC", "tmp64"], ["e_t"])
            self.act(ebl_t, pC[:, 64:128].rearrange("p (i g) -> p i g", g=4), AF.Exp, ["pC", "tmp64"], ["ebl_t"])
            for h in range(4):
                for which, dstT in ((0, mqT), (1, mkT)):
                    cc = which * 4 + h
                    wc, wkey = load_w(w_in_v, 1536 + which * 512 + h * 128, 128)
                    proj_fm(wc, wkey, lambda g: self.act(mraw[:, 3 + g * 512:3 + (g + 1) * 512], pB[:, 0:512], AF.Copy, ["pB"], ["mraw"]))
                    self.ts("dve", cacc, mraw[:, 0:S], convwT[:, cc, 0:1], ALU.mult, ["mraw", "convwT"], ["cacc"])
                    for j in range(1, 4):
                        self.stt(cacc, mraw[:, j:j + S], convwT[:, cc, j:j + 1], cacc, ALU.mult, ALU.add, ["mraw", "convwT", "cacc"], ["cacc"])
                    if which == 0:
                        self.act(mqT, cacc, AF.Silu, ["cacc", "convbT", "cst"], ["mqT"], bias=convbT[:, cc:cc + 1], scale=ones_f[:, 0:1])
                    else:
                        self.act(cacc, cacc, AF.Silu, ["cacc", "convbT", "cst"], ["cacc"], bias=convbT[:, cc:cc + 1], scale=ones_f[:, 0:1])
                        self.ts("dve", mkT, cacc, 128.0 ** -0.5, ALU.mult, ["cacc"], ["mkT"])
                for i0 in range(0, NT, 8):
                    for j in range(8):
                        self.tr(pT[:, j * 128:(j + 1) * 128], mkT[:, (i0 + j) * 128:(i0 + j + 1) * 128], ident_b[:], ["mkT", "ident_b"], ["pT"])
                    self.act(mk_tm[:, i0:i0 + 8, :], pT[:, 0:1024].rearrange("p (i d) -> p i d", d=128), AF.Copy, ["pT"], ["mk_tm"])
                wc, wkey = load_w(w_in_v, 2560 + h * 128, 128)
                proj_tm(wc, wkey, 128, lambda i: self.act(Vp[:, i, 0:128], pC[:, 0:128], AF.Identity, ["pC", "a_t"], ["Vp"], scale=a_t[:, i, h:h + 1]))
                self.cp("dve", Vp[:, :, 128:129], a_t[:, :, h:h + 1], ["a_t"], ["Vp"])
                wc, wkey = load_w(w_in_v, 3072 + h * 128, 128)
                proj_tm(wc, wkey, 128, lambda i: self.act(sigo[:, i, :], pC[:, 0:128], AF.Sigmoid, ["pC"], ["sigo"]))
                for i in range(NT):
                    tl = slice(i * 128, (i + 1) * 128)
                    self.mm(pA[:, 0:128], mkT[:, tl], mqT[:, tl], True, True, ["mkT", "mqT"], ["pA"])
                    self.tt("dve", MS, pA[:, 0:128], maskU_f, ALU.mult, ["pA", "cst"], ["MS"])
                    self.mm(pC[:, 0:129], MS, Vp[:, i, 0:129], True, i == 0, ["MS", "Vp"], ["pC"])
                    if i > 0:
                        self.mm(pC[:, 0:129], mqT[:, tl], C_b[:, 0:129], False, True, ["mqT", "C_b"], ["pC"])
                    if i < NT - 1:
                        self.mm(pD[:, 0:129], mk_tm[:, i, :], Vp[:, i, 0:129], True, True, ["mk_tm", "Vp"], ["pD"])
                        if i == 0:
                            self.cp("dve", tmpC[:, 0:129], pD[:, 0:129], ["pD"], ["tmpC"])
                        else:
                            self.tt("dve", tmpC[:, 0:129], pD[:, 0:129], C_f[:, 0:129], ALU.add, ["pD", "C_f"], ["tmpC"])
                        self.act(C_f[:, 0:129], tmpC[:, 0:129], AF.Identity, ["tmpC", "ebl_t"], ["C_f"], scale=ebl_t[:, i, h:h + 1])
                        self.cp("dve", C_b[:, 0:129], C_f[:, 0:129], ["C_f"], ["C_b"])
                    self.tt("dve", sm2[:, 0:1], pC[:, 128:129], e_t[:, i, h:h + 1], ALU.mult, ["pC", "e_t"], ["sm2"])
                    self.stt(sm2[:, 1:2], sm2[:, 0:1], -1.0, sm2[:, 0:1], ALU.mult, ALU.max, ["sm2"], ["sm2"])
                    self.ts("dve", sm2[:, 1:2], sm2[:, 1:2], 1.0, ALU.max, ["sm2"], ["sm2"])
                    self.recip(sm2[:, 2:3], sm2[:, 1:2], ["sm2"], ["sm2"])
                    self.tt("dve", sm2[:, 3:4], sm2[:, 2:3], e_t[:, i, h:h + 1], ALU.mult, ["sm2", "e_t"], ["sm2"])
                    self.stt(hm, pC[:, 0:128], sm2[:, 3:4], sigo[:, i, :], ALU.mult, ALU.mult, ["pC", "sm2", "sigo"], ["hm"])
                    self.act(junk128, hm, AF.Square, ["hm"], ["junk128", "ss"], accum=small[:, 0:1])
                    self.rstd(small[:, 0:1], small[:, 1:2], small[:, 2:3], 1.0 / 128, "ss", "tmpr", "rstd")
                    self.stt(hm_bf, hm, small[:, 2:3], gm_bc[:, h * 128:(h + 1) * 128], ALU.mult, ALU.mult, ["hm", "rstd", "gm_bc"], ["hm_bf"])
                    self.tr(pT[:, 0:128], hm_bf, ident_b[:], ["hm_bf", "ident_b"], ["pT"])
                    self.act(mixT[:, 4 + h, tl], pT[:, 0:128], AF.Copy, ["pT"], ["mixT"])
            p.barrier()

            self.chk("p1c")
            areset()
            xts = [aF(D) for _ in range(2)]
            x1s = [aF(D) for _ in range(2)]
            for i in range(NT):
                k = i % 2
                tl = slice(i * 128, (i + 1) * 128)
                pcol = k * 1024
                self.dma("sp", xts[k], x_d[b, tl, :], [], ["xts%d" % k], s_xts[k])
                for half in range(2):
                    for c in range(8):
                        self.mm(pA[:, pcol + half * 512:pcol + (half + 1) * 512], mixT[:, c, tl], woutp[:, c, half * 512:(half + 1) * 512],
                                c == 0, c == 7, ["mixT", "woutp"], ["pA%d" % k])
                self.tt("dve", x1s[k], pA[:, pcol:pcol + D], xts[k], ALU.add, ["pA%d" % k, "xts%d" % k], ["x1s%d" % k])
                self.dma("sp", x1_dram[tl, :], x1s[k], ["x1s%d" % k], [], s_x1s[k])
            p.barrier()
            for blk in range(8):
                load_w(w_q_v, blk * 256, 256, dst=(wq[:, :, blk * 256:(blk + 1) * 256], "wq"))
            p.barrier()
            areset(0)
            pers = [dict(x1=aF(D), h2=aF(D), idx_i=aI(128), gte=aF(128)) for _ in range(2)]
            R = aF(2048)
            xt = R[:, 0:1024]
            qTp = R[:, 0:1024].bitcast(BF16).rearrange("p (c t) -> p c t", t=128)
            sc = R.rearrange("p (c k) -> p c k", k=128)
            cand = R
            eq = R
            wk = aF(2048)
            wk2 = wk
            xn = aB(D)
            h2T = aB(8 * 128).rearrange("p (c t) -> p c t", t=128)
            sv = aF(256)
            si = aU(256)
            sif = aF(256)
            tops = aF(128)
            pos = aU(128)
            pa_u = aU(128)
            pb_u = aU(128)
            paf = aF(128)
            pbf = aF(128)
            i1f = aF(128)
            i2f = aF(128)
            idxf = aF(128)
            tg = aF(128)
            zs = aF(8)
            uvbuf = [mixT[:, j, :] for j in range(8)] + \
                    [woutp[:, 2 * j:2 * j + 2, :].rearrange("p c n -> p (c n)") for j in range(4)] + \
                    [aB(2048) for _ in range(NB - 12)]
            prod = [aF(D) for _ in range(2)]
            junkb = aB(D)
            diag = [aB(128) for _ in range(2)]
            dots = aF(128)
            t1 = aF(128)
            t2 = aF(128)
            wgt = aF(128)
            acc_sb = aF(D)
            V = lambda fn, r, w: p.op("dve", fn, reads=r, writes=w)

            def top16_multi(items):
                for (vals, idxs, src, scratch, tag) in items:
                    V(lambda e, o=vals, i_=src: e.max(out=o[:, 0:8], in_=i_), ["src" + tag], ["v" + tag])
                for (vals, idxs, src, scratch, tag) in items:
                    V(lambda e, o=idxs, m=vals, i_=src: e.max_index(out=o[:, 0:8], in_max=m[:, 0:8], in_values=i_), ["src" + tag, "v" + tag], ["i" + tag])
                for (vals, idxs, src, scratch, tag) in items:
                    V(lambda e, o=scratch, m=vals, i_=src: e.match_replace(out=o, in_to_replace=m[:, 0:8], in_values=i_, imm_value=-1e30), ["src" + tag, "v" + tag], ["w" + tag])
                for (vals, idxs, src, scratch, tag) in items:
                    V(lambda e, o=vals, i_=scratch: e.max(out=o[:, 8:16], in_=i_), ["w" + tag], ["v" + tag])
                for (vals, idxs, src, scratch, tag) in items:
                    V(lambda e, o=idxs, m=vals, i_=scratch: e.max_index(out=o[:, 8:16], in_max=m[:, 8:16], in_values=i_), ["w" + tag, "v" + tag], ["i" + tag])

            SA = ["srcA%d" % hp for hp in range(16)]
            VA = ["vA%d" % hp for hp in range(16)]
            IA = ["iA%d" % hp for hp in range(16)]
            SB_ = ["srcB%d" % h for h in range(8)]
            TOPS = ["vB%d" % h for h in range(8)]
            POS = ["iB%d" % h for h in range(8)]

            def front(i, par):
                tl = slice(i * 128, (i + 1) * 128)
                P = pers[par]
                x1, h2, idx_i, gte = P["x1"], P["h2"], P["idx_i"], P["gte"]
                kx1, kh2, kidx, kg = "x1_%d" % par, "h2_%d" % par, "idx_%d" % par, "gte_%d" % par
                self.dma("sp", x1, x1_dram[tl, :], [], [kx1], s_x1[par])
                yield
                self.act(junkb, x1, AF.Square, [kx1], ["junkb", "ss"], accum=small[:, 0:1])
                yield
                self.ts("dve", small[:, 1:2], small[:, 0:1], 1.0 / D, ALU.mult, ["ss"], ["tmpr"], s2=EPS, op1=ALU.add)
                yield
                self.act(small[:, 1:2], small[:, 1:2], AF.Sqrt, ["tmpr"], ["tmpr"])
                yield
                self.recip(small[:, 2:3], small[:, 1:2], ["tmpr"], ["rstd"])
                self.ts("dve", xn, x1, small[:, 2:3], ALU.mult, [kx1, "rstd"], ["xn"])
                self.stt(h2, x1, small[:, 2:3], A2_bc[:], ALU.mult, ALU.mult, [kx1, "rstd", "A2_bc"], [kh2])
                self.tt("dve", h2, h2, B2_bc[:], ALU.add, [kh2, "B2_bc"], [kh2])
                yield
                for c in range(8):
                    self.tr(pT[:, c * 128:(c + 1) * 128], xn[:, c * 128:(c + 1) * 128], ident_b[:], ["xn", "ident_b"], ["pT"])
                yield
                for c in range(8):
                    self.act(h2T[:, c, :], pT[:, c * 128:(c + 1) * 128], AF.Identity, ["pT", "A2", "modT"], ["h2T"],
                             scale=A2[:, b, c:c + 1], bias=modT[:, 24 + c, b:b + 1])
                yield
                for hp in range(16):
                    for kc in range(8):
                        self.mm(pA[:, hp * 128:(hp + 1) * 128], wq[:, kc, hp * 128:(hp + 1) * 128], h2T[:, kc, :], kc == 0, kc == 7,
                                ["wq", "h2T"], ["pA"])
                    if hp % 4 == 3:
                        yield
                self.act(qTp.rearrange("p c t -> p (c t)"), pA[:, 0:2048], AF.Copy, ["pA"], ["R"])
                yield
                for hp in range(16):
                    self.mm(pA[:, hp * 128:(hp + 1) * 128], qTp[:, hp, :], skT[:, hp, :], True, True, ["R", "skT"], ["pA"])
                yield
                self.act(sc.rearrange("p c k -> p (c k)"), pA[:, 0:2048], AF.Copy, ["pA"], ["R"] + SA)
                yield
                itemsA = [(sv[:, hp * 16:(hp + 1) * 16], si[:, hp * 16:(hp + 1) * 16], sc[:, hp, :], wk[:, hp * 128:(hp + 1) * 128], "A%d" % hp)
                          for hp in range(16)]
                top16_multi(itemsA[0:8])
                yield
                top16_multi(itemsA[8:16])
                yield
                self.cp("dve", sif, si, IA, ["sif"])
                cand4 = cand.rearrange("p (h a b) -> p h a b", h=8, a=16)
                self.tt("dve", cand4, cap(sv, 0, [[32, 8], [1, 16], [0, 16]]), cap(sv, 16, [[32, 8], [0, 16], [1, 16]]), ALU.add,
                        VA + IA, ["R"] + SA + SB_)
                yield
                top16_multi([(tops[:, h * 16:(h + 1) * 16], pos[:, h * 16:(h + 1) * 16], cand[:, h * 256:(h + 1) * 256], wk2[:, h * 256:(h + 1) * 256], "B%d" % h)
                             for h in range(8)])
                yield
                V(lambda e, o=pa_u, i_=pos: e.tensor_single_scalar(out=o, in_=i_, scalar=4, op=ALU.logical_shift_right), POS, ["pa_u"])
                V(lambda e, o=pb_u, i_=pos: e.tensor_single_scalar(out=o, in_=i_, scalar=15, op=ALU.bitwise_and), POS, ["pb_u"])
                self.cp("dve", paf, pa_u, ["pa_u"], ["paf"])
                self.cp("dve", pbf, pb_u, ["pb_u"], ["pbf"])
                yield
                eq4 = eq.rearrange("p (h k a) -> p h k a", h=8, k=16)
                for which, pf, outf, key in ((0, paf, i1f, "i1f"), (1, pbf, i2f, "i2f")):
                    self.tt("dve", eq4, cap(pf, 0, [[16, 8], [1, 16], [0, 16]]), cap(iota16, 0, [[0, 8], [0, 16], [1, 16]]), ALU.is_equal,
                            ["paf", "pbf", "cst"] + POS + TOPS, ["R"] + SB_)
                    self.tt("dve", eq4, eq4, cap(sif, which * 16, [[32, 8], [0, 16], [1, 16]]), ALU.mult, ["R", "sif"], ["R"])
                    self.red(outf.rearrange("p (h k) -> p h k", h=8), eq4, ALU.add, ["R"], [key])
                    yield
                self.stt(idxf, i1f, 128.0, i2f, ALU.mult, ALU.add, ["i1f", "i2f"], ["idxf"])
                self.cp("dve", idx_i, idxf, ["idxf"], [kidx])
                tops3 = tops.rearrange("p (h k) -> p h k", h=8)
                tg3 = tg.rearrange("p (h k) -> p h k", h=8)
                self.tt("dve", tg3, tops3, cap(tops, 0, [[16, 8], [0, 16]]), ALU.subtract, TOPS, ["tg"])
                yield
                self.act(tg, tg, AF.Exp, ["tg"], ["tg"])
                yield
                self.red(zs, tg3, ALU.add, ["tg"], ["zs"])
                self.recip(zs, zs, ["zs"], ["zs"])
                self.tt("dve", gte.rearrange("p (h k) -> p h k", h=8), tg3, cap(zs, 0, [[1, 8], [0, 16]]), ALU.mult, ["tg", "zs"], [kg])
                if b == 0 and i == 0:
                    dbg_out("x1", x1, [kx1])
                    dbg_out("idxf", idxf, ["idxf"])
                    dbg_out("gte", gte, [kg])
                    dbg_out("h2", h2, [kh2])
                if b == 0 and "x1full" in dbg_d:
                    self.dma("sp", dbg_d["x1full"][tl, :], x1, [kx1], [], s_dbg)
                    self.dma("sp", dbg_d["idxfull"][tl, :], idxf, ["idxf"], [], s_dbg)
                yield

            def back(i, par):
                tl = slice(i * 128, (i + 1) * 128)
                P = pers[par]
                x1, h2, idx_i, gte = P["x1"], P["h2"], P["idx_i"], P["gte"]
                kx1, kh2, kidx, kg = "x1_%d" % par, "h2_%d" % par, "idx_%d" % par, "gte_%d" % par
                NG = 128 // GS

                def stage_a(g):
                    for s in range(g * GS, (g + 1) * GS):
                        j = s % NB
                        uk = "uv%d" % j
                        self.gather(uvbuf[j], uv_dram, idx_i[:, s:s + 1], [kidx], [uk], s_uv[j])
                        pk = "prod%d" % (s % 2)
                        self.stt(prod[s % 2], uvbuf[j][:, 0:1024], 1.0, h2, ALU.mult, ALU.mult, [uk, kh2], [pk, "dots%d" % (g % 4)],
                                 accum=dots[:, s:s + 1])

                def stage_b(g):
                    sl = slice(g * GS, (g + 1) * GS)
                    dk_, t1k, t2k, wk_ = "dots%d" % (g % 4), "t1_%d" % (g % 4), "t2_%d" % (g % 4), "wgt%d" % (g % 4)
                    self.tt("dve", t1[:, sl], dots[:, sl], dots[:, sl], ALU.mult, [dk_], [t1k])
                    self.tt("dve", t1[:, sl], t1[:, sl], dots[:, sl], ALU.mult, [t1k, dk_], [t1k])
                    self.stt(t1[:, sl], t1[:, sl], 0.044715, dots[:, sl], ALU.mult, ALU.add, [t1k, dk_], [t1k])
                    self.act(t2[:, sl], t1[:, sl], AF.Tanh, [t1k], [t2k], scale=0.7978845608028654)
                    self.ts("dve", t2[:, sl], t2[:, sl], 1.0, ALU.add, [t2k], [t2k], s2=0.5, op1=ALU.mult)
                    self.tt("dve", t2[:, sl], t2[:, sl], dots[:, sl], ALU.mult, [t2k, dk_], [t2k])
                    self.tt("dve", wgt[:, sl], t2[:, sl], gte[:, sl], ALU.mult, [t2k, kg], [wk_])
                    for s in range(g * GS, (g + 1) * GS):
                        j = s % NB
                        uk = "uv%d" % j
                        dk = "diag%d" % (s % 2)
                        self.act(diag[s % 2], ident_b[:], AF.Identity, ["ident_b", wk_], [dk], scale=wgt[:, s:s + 1])
                        self.mm(pB[:, 0:512], diag[s % 2], uvbuf[j][:, 1024:1536], s == 0, s == 127, [dk, uk], ["pB"])
                        self.mm(pC[:, 0:512], diag[s % 2], uvbuf[j][:, 1536:2048], s == 0, s == 127, [dk, uk], ["pC"])

                for g in range(NG + 1):
                    if g < NG:
                        stage_a(g)
                    if g >= 1:
                        stage_b(g - 1)
                    yield
                if b == 0 and i == 0:
                    dbg_out("dots", dots, ["dots%d" % k_ for k_ in range(4)])
                self.tt("dve", acc_sb[:, 0:512], pB[:, 0:512], g2_bc[:, 0:512], ALU.mult, ["pB", "g2_bc"], ["acc_sb"])
                self.tt("dve", acc_sb[:, 512:1024], pC[:, 0:512], g2_bc[:, 512:1024], ALU.mult, ["pC", "g2_bc"], ["acc_sb"])
                self.tt("dve", acc_sb, acc_sb, x1, ALU.add, ["acc_sb", kx1], ["acc_sb"])
                self.act(junkb, acc_sb, AF.Square, ["acc_sb"], ["junkb", "ssb"], accum=small[:, 8:9])
                self.rstd(small[:, 8:9], small[:, 9:10], small[:, 10:11], 1.0 / D, "ssb", "tmprb", "rstdb")
                self.stt(acc_sb, acc_sb, small[:, 10:11], fin_bc[:], ALU.mult, ALU.mult, ["acc_sb", "rstdb", "fin_bc"], ["acc_sb"])
                self.dma("sp", out_d[b, tl, :], acc_sb, ["acc_sb"], [], s_out)
                yield

            def run_all(gen):
                for _ in gen:
                    pass

            nt2 = self.ntiles2
            run_all(front(0, 0))
            self.chk("p2f")
            for i in range(nt2):
                par = i % 2
                fg = front(i + 1, par ^ 1) if i + 1 < nt2 else None
                if not self.no_back:
                    for _ in back(i, par):
                        if fg is not None:
                            next(fg, None)
                if fg is not None:
                    run_all(fg)
            p.barrier()


def _rel_bucket_np(n):
    n = np.asarray(n)
    max_exact = 16
    nf = np.maximum(n, 1).astype(np.float32)
    large = max_exact + (np.log(nf / np.float32(max_exact)) / np.float32(math.log(128 / max_exact))
                         * np.float32(32 - max_exact)).astype(np.int32)
    large = np.minimum(large, 31)
    return np.where(n < max_exact, n, large)


def _prep_inputs(inp):
    f = lambda a: np.ascontiguousarray(np.asarray(a, dtype=np.float32))
    x = f(inp["x"])
    c = f(inp["c"])
    shared = {}
    shared["w_ada"] = f(inp["w_ada"][0])
    shared["b_ada"] = f(inp["b_ada"][0])
    shared["b_adaT"] = f(inp["b_ada"][0].reshape(48, 128).T)
    shared["n1gT"] = f(inp["norm1_g"][0].reshape(8, 128).T)
    shared["n2gT"] = f(inp["norm2_g"][0].reshape(8, 128).T)
    shared["n2g"] = f(inp["norm2_g"][0])
    shared["w_in"] = f(inp["w_in"][0])
    shared["convwT"] = f(np.asarray(inp["conv_w"][0]).T.reshape(8, 128, 4).transpose(1, 0, 2))
    shared["convbT"] = f(np.asarray(inp["conv_b"][0]).reshape(8, 128).T)
    shared["bgate"] = f(np.concatenate([np.asarray(inp["b_igate"][0]), np.asarray(inp["b_fgate"][0])]))
    shared["lam"] = f(np.stack([np.asarray(inp[k][0]) for k in ("lam_q1", "lam_k1", "lam_q2", "lam_k2")]))
    shared["gsub"] = f(inp["diff_sub_g"][0])
    shared["gm"] = f(inp["mlstm_norm_g"][0])
    shared["w_out"] = f(inp["w_out"][0])
    shared["w_q"] = f(inp["peer_w_q"][0])
    sk = np.asarray(inp["peer_sub_keys"][0])
    shared["skT"] = f(sk.transpose(3, 0, 1, 2).reshape(128, 2048))
    shared["peer_u"] = f(inp["peer_u"][0])
    shared["peer_v"] = f(inp["peer_v"][0])
    rb = f(inp["rel_bias"])
    shared["rel_bias"] = rb
    kk = np.arange(128)[:, None]
    qq = np.arange(128)[None, :]
    rel1 = qq - kk + 128
    rel0 = qq - kk
    b1 = rb[_rel_bucket_np(rel1)]
    b0 = rb[_rel_bucket_np(np.maximum(rel0, 0))]
    biasT = np.empty((128, 4, 256), np.float32)
    biasT[:, :, 0:128] = b1.transpose(0, 2, 1)
    biasT[:, :, 128:256] = b0.transpose(0, 2, 1)
    biasT[:, :, 128:256][np.broadcast_to((rel0 < 0)[:, None, :], (128, 4, 128))] = -1e9
    shared["biasT"] = biasT
    shared["final_g"] = f(inp["final_g"])
    cst = np.zeros((128, 400), np.float32)
    cst[:, 0:128] = np.eye(128)
    cst[:, 128:256] = np.triu(np.ones((128, 128)))
    cst[:, 256:384] = 1.0
    cst[:, 384:400] = np.arange(16)[None, :]
    shared["cst"] = cst
    in_maps = []
    for core in range(8):
        m = dict(shared)
        m["x"] = np.ascontiguousarray(x[2 * core:2 * core + 2])
        cc = c[2 * core:2 * core + 2]
        m["cT"] = np.ascontiguousarray(cc.reshape(2, 8, 128).transpose(2, 1, 0))
        in_maps.append(m)
    return in_maps


_NC_CACHE = {}


def kernel(**inputs):
    in_maps = _prep_inputs(inputs)
    if "nc" not in _NC_CACHE:
        _NC_CACHE["nc"] = KB().build()
    nc = _NC_CACHE["nc"]
    res = run_bass_kernel_spmd(nc, in_maps, core_ids=list(range(8)))
    out = np.concatenate([np.asarray(r["out"]) for r in res.results], axis=0)
    return out.astype(np.float32)
```

```python
import math
import numpy as np
import concourse.bass as bass
import concourse.mybir as mybir
from concourse.bass_utils import run_bass_kernel_spmd

F32 = mybir.dt.float32
BF16 = mybir.dt.bfloat16
U32 = mybir.dt.uint32
I32 = mybir.dt.int32
ALU = mybir.AluOpType
AF = mybir.ActivationFunctionType
AX = mybir.AxisListType

ENGS = ("pe", "act", "dve", "pool", "sp")
SAME_ENGINE_SYNC = {"pe": False, "act": True, "dve": True, "pool": True, "sp": True}

S = 2048
D = 1024
NT = 16
EPS = 1e-6
NU = 3
NV = 3
ARENA_W = 24320
WREG = 6144
GS = 4
NB = 16


class Prog:
    def __init__(self, nc):
        self.nc = nc
        self.q = {e: [] for e in ENGS}
        self.cnt = {}
        self.semobj = {}
        self.lastw = {}
        self.readers = {}
        self.seen = {e: {} for e in ENGS}
        self.ownsem = {}
        for e in ENGS:
            s = nc.alloc_semaphore("es_" + e)
            self.ownsem[e] = s.name
            self.semobj[s.name] = s
            self.cnt[s.name] = 0
        self.nsem = 0

    def new_sem(self, name=None):
        self.nsem += 1
        s = self.nc.alloc_semaphore(name or ("ds_%d" % self.nsem))
        self.semobj[s.name] = s
        self.cnt[s.name] = 0
        return s.name

    def op(self, eng, fn, reads=(), writes=(), sem=None, inc=None):
        if sem is None:
            sem = self.ownsem[eng]
            inc = 1
        elif inc is None:
            inc = 16
        deps = {}

        def add(d):
            if d is None:
                return
            s, v = d
            if deps.get(s, 0) < v:
                deps[s] = v

        for k in reads:
            add(self.lastw.get(k))
        for k in writes:
            add(self.lastw.get(k))
            for s, v in self.readers.get(k, {}).items():
                add((s, v))
        waits = []
        for s, v in deps.items():
            if s == self.ownsem[eng] and not SAME_ENGINE_SYNC[eng]:
                continue
            if self.seen[eng].get(s, 0) >= v:
                continue
            self.seen[eng][s] = v
            waits.append((s, v))
        self.cnt[sem] += inc
        val = self.cnt[sem]
        for k in reads:
            r = self.readers.setdefault(k, {})
            if r.get(sem, 0) < val:
                r[sem] = val
        for k in writes:
            self.lastw[k] = (sem, val)
            self.readers[k] = {}
        self.q[eng].append((waits, fn, sem, inc))

    def barrier(self):
        snap = dict(self.cnt)
        for e in ENGS:
            waits = []
            for s, v in snap.items():
                if v > self.seen[e].get(s, 0):
                    self.seen[e][s] = v
                    waits.append((s, v))
            own = self.ownsem[e]
            self.cnt[own] += 1
            self.q[e].append((waits, (lambda eng: eng.nop()), own, 1))
        self.lastw.clear()
        self.readers.clear()

    def emit(self):
        nc = self.nc
        finals = [(s, v) for s, v in self.cnt.items() if v > 0]
        with nc.Block() as block:
            def run(eng_name, wait_all=False):
                def f(eng):
                    for waits, fn, sem, inc in self.q[eng_name]:
                        for s, v in waits:
                            eng.wait_ge(self.semobj[s], v)
                        ins = fn(eng)
                        ins.then_inc(self.semobj[sem], inc)
                    if wait_all:
                        for s, v in finals:
                            eng.wait_ge(self.semobj[s], v)
                return f
            block.tensor(run("pe"))
            block.scalar(run("act"))
            block.vector(run("dve"))
            block.gpsimd(run("pool"))
            block.sync(run("sp", wait_all=True))


def cap(t, off, dims):
    return bass.AP(t.tensor, t.offset + off, [list(t.ap[0])] + [list(d) for d in dims])


class _Stop(Exception):
    pass


class KB:
    def __init__(self, dbg=None, nseq=2, ntiles2=NT, stop=None, no_back=False):
        self.no_back = no_back
        self.dbg = dbg
        self.stop = stop
        self.nseq = nseq
        self.ntiles2 = ntiles2
        nc = self.nc = bass.Bass("TRN2", target_bir_lowering=False)
        self.p = Prog(nc)
        self._n = 0

    def mm(self, out, lhsT, rhs, start, stop, r, w):
        self.p.op("pe", lambda e: e.matmul(out, lhsT=lhsT, rhs=rhs, start=start, stop=stop), reads=r, writes=w)

    def tr(self, out, in_, ident, r, w):
        self.p.op("pe", lambda e: e.transpose(out=out, in_=in_, identity=ident), reads=r, writes=w)

    def act(self, out, in_, func, r, w, scale=None, bias=None, accum=None):
        kw = {}
        if scale is not None:
            kw["scale"] = scale
        if bias is not None:
            kw["bias"] = bias
        if accum is not None:
            kw["accum_out"] = accum
        self.p.op("act", lambda e: e.activation(out=out, in_=in_, func=func, **kw), reads=r, writes=w)

    def tt(self, eng, out, in0, in1, op, r, w):
        self.p.op(eng, lambda e: e.tensor_tensor(out=out, in0=in0, in1=in1, op=op), reads=r, writes=w)

    def ts(self, eng, out, in0, s1, op0, r, w, s2=None, op1=None):
        if op1 is None:
            self.p.op(eng, lambda e: e.tensor_scalar(out=out, in0=in0, scalar1=s1, scalar2=None, op0=op0), reads=r, writes=w)
        else:
            self.p.op(eng, lambda e: e.tensor_scalar(out=out, in0=in0, scalar1=s1, scalar2=s2, op0=op0, op1=op1), reads=r, writes=w)

    def stt(self, out, in0, scalar, in1, op0, op1, r, w, accum=None):
        if accum is None:
            self.p.op("dve", lambda e: e.scalar_tensor_tensor(out=out, in0=in0, scalar=scalar, in1=in1, op0=op0, op1=op1), reads=r, writes=w)
        else:
            self.p.op("dve", lambda e: e.scalar_tensor_tensor(out=out, in0=in0, scalar=scalar, in1=in1, op0=op0, op1=op1, accum_out=accum), reads=r, writes=w)

    def cp(self, eng, out, in_, r, w):
        self.p.op(eng, lambda e: e.tensor_copy(out=out, in_=in_), reads=r, writes=w)

    def red(self, out, in_, op, r, w):
        self.p.op("dve", lambda e: e.tensor_reduce(out=out, in_=in_, axis=AX.X, op=op), reads=r, writes=w)

    def recip(self, out, in_, r, w):
        self.p.op("dve", lambda e: e.reciprocal(out=out, in_=in_), reads=r, writes=w)

    def memset(self, eng, ap, val, w):
        self.p.op(eng, lambda e: e.memset(ap, val), writes=w)

    def dma(self, eng, out, in_, r, w, sem):
        self.p.op(eng, lambda e: e.dma_start(out=out, in_=in_), reads=r, writes=w, sem=sem)

    def gather(self, out, table, idx, r, w, sem):
        self.p.op("pool", lambda e: e.indirect_dma_start(
            out=out, out_offset=None, in_=table,
            in_offset=bass.IndirectOffsetOnAxis(ap=idx, axis=0)), reads=r, writes=w, sem=sem)

    def sb(self, shape, dt, name=None):
        self._n += 1
        return self.nc.alloc_sbuf_tensor("sb_" + (name or ("t%d" % self._n)), shape, dt)

    def rstd(self, ss, tmp, out, inv_n, key_ss, key_tmp, key_out):
        self.ts("dve", tmp, ss, inv_n, ALU.mult, [key_ss], [key_tmp], s2=EPS, op1=ALU.add)
        self.act(tmp, tmp, AF.Sqrt, [key_tmp], [key_tmp])
        self.recip(out, tmp, [key_tmp], [key_out])

    def build(self):
        try:
            self._build()
        except _Stop:
            pass
        self.p.emit()
        return self.nc

    def chk(self, tag):
        if self.stop == tag:
            raise _Stop()

    def _build(self):
        nc, p = self.nc, self.p

        def din(name, shape, dt=F32):
            return nc.dram_tensor(name, shape, dt, kind="ExternalInput").ap()

        x_d = din("x", [2, S, D])
        cT_d = din("cT", [128, 8, 2])
        w_ada_d = din("w_ada", [D, 6 * D])
        b_ada_d = din("b_ada", [6 * D])
        b_adaT_d = din("b_adaT", [128, 48])
        n1gT_d = din("n1gT", [128, 8])
        n2gT_d = din("n2gT", [128, 8])
        n2g_d = din("n2g", [D])
        w_in_d = din("w_in", [D, 3592])
        convwT_d = din("convwT", [128, 8, 4])
        convbT_d = din("convbT", [128, 8])
        bgate_d = din("bgate", [8])
        lam_d = din("lam", [4, 64])
        gsub_d = din("gsub", [128])
        gm_d = din("gm", [512])
        w_out_d = din("w_out", [D, D])
        w_q_d = din("w_q", [D, 2048])
        skT_d = din("skT", [128, 2048])
        u_d = din("peer_u", [16384, D])
        v_d = din("peer_v", [16384, D])
        relb_d = din("rel_bias", [32, 4])
        biasT_d = din("biasT", [128, 4, 256])
        fin_d = din("final_g", [D])
        cst_d = din("cst", [128, 400])
        out_d = nc.dram_tensor("out", [2, S, D], F32, kind="ExternalOutput").ap()
        dbg_d = {}
        if self.dbg:
            for name, shape in self.dbg.items():
                dbg_d[name] = nc.dram_tensor("dbg_" + name, shape, F32, kind="ExternalOutput").ap()

        w_ada_v = w_ada_d.rearrange("(c p) n -> p c n", p=128)
        w_in_v = w_in_d.rearrange("(c p) n -> p c n", p=128)
        w_out_v = w_out_d.rearrange("(c p) n -> p c n", p=128)
        w_q_v = w_q_d.rearrange("(c p) n -> p c n", p=128)

        sb = self.sb
        cst = sb([128, 400], F32, "cst")
        ident_f = cst[:, 0:128]
        maskU_f = cst[:, 128:256]
        ones_f = cst[:, 256:384]
        iota16 = cst[:, 384:400]
        ident_b = sb([128, 128], BF16, "ident_b")
        biasT = sb([128, 4, 256], F32, "biasT")
        cfar = sb([128, 4], F32, "cfar")
        condT = sb([128, 8, 2], F32, "condT")
        modT = sb([128, 48, 2], F32, "modT")
        b_adaT = sb([128, 48], F32, "b_adaT")
        n1gT = sb([128, 8], F32, "n1gT")
        n2gT = sb([128, 8], F32, "n2gT")
        convwT = sb([128, 8, 4], F32, "convwT")
        convbT = sb([128, 8], F32, "convbT")
        A1 = sb([128, 2, 8], F32, "A1")
        A2 = sb([128, 2, 8], F32, "A2")
        fin_bc = sb([128, D], F32, "fin_bc")
        gsub_bc = sb([128, 128], F32, "gsub_bc")
        gm_bc = sb([128, 512], F32, "gm_bc")
        bg_bc = sb([128, 8], F32, "bg_bc")
        A2_bc = sb([128, D], F32, "A2_bc")
        B2_bc = sb([128, D], F32, "B2_bc")
        g2_bc = sb([128, D], F32, "g2_bc")
        lamv = sb([128, 4, 64], F32, "lamv")
        lams = sb([128, 8], F32, "lams")
        small = sb([128, 16], F32, "small")
        mixT = sb([128, 8, S], BF16, "mixT")
        arena1 = sb([128, 8, S], BF16, "arena1")
        hT = arena1
        wq = arena1
        woutp = sb([128, 8, D], BF16, "woutp")
        skT = sb([128, 16, 128], BF16, "skT")
        arena = sb([128, ARENA_W], F32, "arena")
        wst = [arena[:, i * 2048:(i + 1) * 2048].rearrange("p (c n) -> p c n", c=8) for i in range(2)]
        wbf = [arena[:, 4096 + i * 1024:4096 + (i + 1) * 1024].bitcast(BF16).rearrange("p (c n) -> p c n", c=8) for i in range(2)]

        pA = nc.alloc_psum_tensor("pA", [128, 2048], F32)
        pT = nc.alloc_psum_tensor("pT", [128, 1024], BF16)
        pB = nc.alloc_psum_tensor("pB", [128, 512], F32)
        pC = nc.alloc_psum_tensor("pC", [128, 512], F32)
        pD = nc.alloc_psum_tensor("pD", [128, 512], F32)

        s_c = p.new_sem("s_const")
        s_xt = p.new_sem("s_xt")
        s_w = [p.new_sem("s_w0"), p.new_sem("s_w1")]
        s_uv = [p.new_sem("s_uv%d" % i) for i in range(NB)]
        s_pu = [p.new_sem("s_pu%d" % i) for i in range(2)]
        s_pv = [p.new_sem("s_pv%d" % i) for i in range(2)]
        s_ps = [p.new_sem("s_ps%d" % i) for i in range(2)]
        s_xts = [p.new_sem("s_xts%d" % i) for i in range(2)]
        s_x1s = [p.new_sem("s_x1s%d" % i) for i in range(2)]
        s_x1 = [p.new_sem("s_x1_%d" % i) for i in range(2)]
        x1_dram = nc.dram_tensor("x1_scratch", [S, D], F32).ap()
        s_out = p.new_sem("s_out")
        s_dbg = p.new_sem("s_dbg")

        ar = {"off": 0}

        def areset(off=WREG):
            ar["off"] = off

        def aF(n):
            o = ar["off"]
            ar["off"] += n
            assert ar["off"] <= ARENA_W, ar["off"]
            return arena[:, o:o + n]

        def aB(n):
            w = (n + 1) // 2
            return aF(w).bitcast(BF16)[:, 0:n]

        def aU(n):
            return aF(n).bitcast(U32)

        def aI(n):
            return aF(n).bitcast(I32)

        wstate = {"k": 0}

        def load_w(dram_v, c0, ncols, scale_bc=None, dst=None):
            k = wstate["k"]
            wstate["k"] ^= 1
            self.dma("sp", wst[k][:, :, 0:ncols], dram_v[:, :, c0:c0 + ncols], [], ["wst%d" % k], s_w[k])
            if dst is None:
                dst = wbf[k][:, :, 0:ncols]
                key = "wbf%d" % k
            else:
                dst, key = dst
            if scale_bc is None:
                self.cp("pool", dst, wst[k][:, :, 0:ncols], ["wst%d" % k], [key])
            else:
                bc, bkey = scale_bc
                self.tt("pool", dst, wst[k][:, :, 0:ncols], cap(bc, 0, [[0, 8], [1, ncols]]), ALU.mult,
                        ["wst%d" % k, bkey], [key])
            return dst, key

        def dbg_out(name, src, keys):
            if name in dbg_d:
                self.dma("sp", dbg_d[name], src, keys, [], s_dbg)

        cl = lambda o, i, w: self.dma("sp", o, i, [], [w], s_c)
        cl(cst[:], cst_d, "cst")
        cl(biasT[:], biasT_d, "biasT")
        cl(cfar[:], relb_d[31, :].partition_broadcast(128), "cfar")
        cl(condT[:], cT_d, "condT")
        cl(b_adaT[:], b_adaT_d, "b_adaT")
        cl(n1gT[:], n1gT_d, "n1gT")
        cl(n2gT[:], n2gT_d, "n2gT")
        cl(convwT[:], convwT_d, "convwT")
        cl(convbT[:], convbT_d, "convbT")
        cl(fin_bc[:], fin_d.partition_broadcast(128), "fin_bc")
        cl(gsub_bc[:], gsub_d.partition_broadcast(128), "gsub_bc")
        cl(gm_bc[:], gm_d.partition_broadcast(128), "gm_bc")
        cl(bg_bc[:], bgate_d.partition_broadcast(128), "bg_bc")
        for i in range(4):
            cl(lamv[:, i, :], lam_d[i, :].partition_broadcast(128), "lamv")
        uv_dram = nc.dram_tensor("uv_scratch", [16384, 2048], BF16).ap()
        TB = ARENA_W - 6144
        ust = [arena[:, TB + k * 1024:TB + (k + 1) * 1024] for k in range(2)]
        vst = [arena[:, TB + 2048 + k * 1024:TB + 2048 + (k + 1) * 1024] for k in range(2)]
        uvb = [arena[:, TB + 4096 + k * 1024:TB + 4096 + (k + 1) * 1024].bitcast(BF16) for k in range(2)]

        def table_build():
            for n in range(129):
                if n < 128:
                    k = n % 2
                    rows = slice(n * 128, (n + 1) * 128)
                    self.dma("sp", ust[k], u_d[rows, :], [], ["ust%d" % k], s_pu[k])
                    self.dma("sp", vst[k], v_d[rows, :], [], ["vst%d" % k], s_pv[k])
                if n >= 1:
                    m = n - 1
                    k = m % 2
                    rows = slice(m * 128, (m + 1) * 128)
                    self.cp("pool", uvb[k][:, 0:1024], ust[k], ["ust%d" % k], ["uvbA%d" % k])
                    self.cp("pool", uvb[k][:, 1024:2048], vst[k], ["vst%d" % k], ["uvbB%d" % k])
                    self.dma("pool", uv_dram[rows, :], uvb[k], ["uvbA%d" % k, "uvbB%d" % k], [], s_ps[k])
                yield

        bgen = table_build()
        areset()
        mod_rows = aF(6144)
        brow = aF(6144)
        cl(brow[0:2, :], b_ada_d.partition_broadcast(2), "brow")
        skst = wst[0].rearrange("p c n -> p (c n)")
        cl(skst, skT_d, "wst0")
        p.barrier()
        self.cp("pool", skT[:].rearrange("p c n -> p (c n)"), skst, ["wst0"], ["skT"])
        self.act(condT[:], condT[:], AF.Silu, ["condT"], ["condT"])
        self.cp("dve", ident_b[:], ident_f, ["cst"], ["ident_b"])
        for h_ in range(4):
            self.ts("dve", biasT[:, h_, :], biasT[:, h_, :], cfar[:, h_:h_ + 1], ALU.subtract, ["biasT", "cfar"], ["biasT"])
        self.ts("dve", gsub_bc[:], gsub_bc[:], 0.8, ALU.mult, ["gsub_bc"], ["gsub_bc"])
        junk64 = aF(64)
        for j in range(2):
            self.tt("dve", junk64, lamv[:, 2 * j, :], lamv[:, 2 * j + 1, :], ALU.mult, ["lamv"], ["junk64"])
            self.red(lams[:, j:j + 1], junk64, ALU.add, ["junk64"], ["lams"])
        self.act(lams[:, 2:4], lams[:, 0:2], AF.Exp, ["lams"], ["lams"])
        self.tt("dve", lams[:, 4:5], lams[:, 3:4], lams[:, 2:3], ALU.subtract, ["lams"], ["lams"])
        self.ts("dve", lams[:, 4:5], lams[:, 4:5], -0.2, ALU.add, ["lams"], ["lams"])
        neglam = lams[:, 4:5]
        for blk in range(24):
            k = wstate["k"]
            wstate["k"] ^= 1
            wk_ = "wst%d" % k
            self.dma("sp", wst[k], w_ada_v[:, :, blk * 256:(blk + 1) * 256], [], [wk_], s_w[k])
            for kc in range(8):
                self.mm(pB[0:2, 0:256], condT[:, kc, :], wst[k][:, kc, :], kc == 0, kc == 7, ["condT", wk_], ["pB"])
            self.tt("dve", mod_rows[0:2, blk * 256:(blk + 1) * 256], pB[0:2, 0:256], brow[0:2, blk * 256:(blk + 1) * 256],
                    ALU.add, ["pB", "brow"], ["mod_rows"])
            for jl in range(2):
                j = blk * 2 + jl
                for kc in range(8):
                    self.mm(pC[:, 2 * j:2 * j + 2], wst[k][:, kc, jl * 128:(jl + 1) * 128], condT[:, kc, :],
                            kc == 0, kc == 7, ["condT", wk_], ["pC"])
        self.tt("dve", modT[:], pC[:, 0:96].rearrange("p (j b) -> p j b", b=2), cap(b_adaT[:], 0, [[1, 48], [0, 2]]),
                ALU.add, ["pC", "b_adaT"], ["modT"])
        for b in range(2):
            self.stt(A1[:, b, :], modT[:, 8:16, b], 1.0, n1gT[:], ALU.add, ALU.mult, ["modT", "n1gT"], ["A1"])
            self.stt(A2[:, b, :], modT[:, 32:40, b], 1.0, n2gT[:], ALU.add, ALU.mult, ["modT", "n2gT"], ["A2"])
        if "modT" in dbg_d:
            dbg_out("modT", modT[:].rearrange("p j b -> p (j b)"), ["modT"])
        mod_dram = nc.dram_tensor("mod_scratch", [2, 6144], F32).ap()
        self.dma("sp", mod_dram, mod_rows[0:2, :], ["mod_rows"], [], s_c)
        p.barrier()

        self.chk("p0")
        for b in range(self.nseq):
            areset()
            g1_bc = aF(D)
            n2g_bc = aF(D)
            self.dma("sp", n2g_bc, n2g_d.partition_broadcast(128), [], ["n2g_bc"], s_c)
            for dst, key, col0 in ((g1_bc, "g1_bc", 2048), (B2_bc[:], "B2_bc", 3072), (A2_bc[:], "A2_bc", 4096), (g2_bc[:], "g2_bc", 5120)):
                self.dma("sp", dst, mod_dram[b, col0:col0 + 1024].partition_broadcast(128), [], [key], s_c)
            p.barrier()
            self.stt(A2_bc[:], A2_bc[:], 1.0, n2g_bc, ALU.add, ALU.mult, ["A2_bc", "n2g_bc"], ["A2_bc"])
            for blk in range(4):
                load_w(w_out_v, blk * 256, 256, scale_bc=(g1_bc[:, blk * 256:(blk + 1) * 256], "g1_bc"),
                       dst=(woutp[:, :, blk * 256:(blk + 1) * 256], "woutp"))
            p.barrier()

            areset()
            xt = aF(D)
            junk = aF(D)
            xn = aB(D)
            for i in range(NT):
                self.dma("sp", xt, x_d[b, i * 128:(i + 1) * 128, :], [], ["xt"], s_xt)
                self.act(junk, xt, AF.Square, ["xt"], ["junk", "ss"], accum=small[:, 0:1])
                self.rstd(small[:, 0:1], small[:, 1:2], small[:, 2:3], 1.0 / D, "ss", "tmpr", "rstd")
                self.ts("dve", xn, xt, small[:, 2:3], ALU.mult, ["xt", "rstd"], ["xn"])
                for c in range(8):
                    self.tr(pT[:, c * 128:(c + 1) * 128], xn[:, c * 128:(c + 1) * 128], ident_b[:], ["xn", "ident_b"], ["pT"])
                for c in range(8):
                    self.act(hT[:, c, i * 128:(i + 1) * 128], pT[:, c * 128:(c + 1) * 128], AF.Identity, ["pT", "A1", "modT"], ["hT"],
                             scale=A1[:, b, c:c + 1], bias=modT[:, c, b:b + 1])
            p.barrier()

            self.chk("p1a")

            def proj_fm(wcols, wkey, evac):
                for g in range(4):
                    for kc in range(8):
                        self.mm(pB[:, 0:512], wcols[:, kc, :], hT[:, kc, g * 512:(g + 1) * 512], kc == 0, kc == 7,
                                [wkey, "hT"], ["pB"])
                    evac(g)

            def proj_tm(wcols, wkey, ncols, evac):
                for i in range(NT):
                    for kc in range(8):
                        self.mm(pC[:, 0:ncols], hT[:, kc, i * 128:(i + 1) * 128], wcols[:, kc, 0:ncols], kc == 0, kc == 7,
                                [wkey, "hT"], ["pC"])
                    evac(i)

            areset()
            qT = aB(S)
            kT = aB(S)
            vh = aB(16 * 130).rearrange("p (i e) -> p i e", e=130)
            PTs = [aB(1024) for _ in range(2)]
            tmpSs = [aF(256) for _ in range(2)]
            o_sb = aF(128)
            o_bf = aB(128)
            junk128 = aF(128)
            self.memset("pool", vh[:, :, 128:129], 1.0, ["vh"])
            for h in range(4):
                wc, wkey = load_w(w_in_v, h * 128, 128)
                proj_fm(wc, wkey, lambda g: self.act(qT[:, g * 512:(g + 1) * 512], pB[:, 0:512], AF.Copy, ["pB"], ["qT"], scale=0.125))
                wc, wkey = load_w(w_in_v, 512 + h * 128, 128)
                proj_fm(wc, wkey, lambda g: self.act(kT[:, g * 512:(g + 1) * 512], pB[:, 0:512], AF.Copy, ["pB"], ["kT"]))
                wc, wkey = load_w(w_in_v, 1024 + h * 128, 128)
                proj_tm(wc, wkey, 128, lambda i: self.act(vh[:, i, 0:128], pC[:, 0:128], AF.Copy, ["pC"], ["vh"]))
                self.chk("p1b_proj")
                units = []
                for qb in range(NT):
                    for t in range(2):
                        if qb < 8:
                            units.append((qb, t, 0, qb))
                        else:
                            units.append((qb, t, 0, 7))
                            units.append((qb, t, 8, qb))
                Os = (pC, pD)

                def s1(n):
                    qb, t, kb0, kb1 = units[n]
                    reg = n % 2
                    ps = slice(t * 64, (t + 1) * 64)
                    for kb in range(kb0, kb1 + 1):
                        col = reg * 1024 + (kb - kb0) * 128
                        self.mm(pA[:, col:col + 128], kT[ps, kb * 128:(kb + 1) * 128], qT[ps, qb * 128:(qb + 1) * 128],
                                True, True, ["kT", "qT"], ["pA%d" % reg])

                def s2(n):
                    qb, t, kb0, kb1 = units[n]
                    reg = n % 2
                    rk, pk, tk = "pA%d" % reg, "PT%d" % reg, "tmpS%d" % reg
                    base = reg * 1024
                    far_hi = min(kb1, qb - 2)
                    nears = []
                    for kb, boff in ((qb - 1, 0), (qb, 128)):
                        if kb < kb0 or kb > kb1 or kb < 0:
                            continue
                        lc = (kb - kb0) * 128
                        self.tt("dve", tmpSs[reg][:, boff:boff + 128], pA[:, base + lc:base + lc + 128], biasT[:, h, boff:boff + 128], ALU.add,
                                [rk, "biasT"], [tk])
                        nears.append((lc, boff))
                    if far_hi >= kb0:
                        ncol = (far_hi - kb0 + 1) * 128
                        self.act(PTs[reg][:, 0:ncol], pA[:, base:base + ncol], AF.Exp, [rk, tk], [pk])
                    for lc, boff in nears:
                        self.act(PTs[reg][:, lc:lc + 128], tmpSs[reg][:, boff:boff + 128], AF.Exp, [tk], [pk])

                def s3(n):
                    qb, t, kb0, kb1 = units[n]
                    reg = n % 2
                    okey = "pC" if t == 0 else "pD"
                    for kb in range(kb0, kb1 + 1):
                        lc = (kb - kb0) * 128
                        self.mm(Os[t][:, 0:129], PTs[reg][:, lc:lc + 128], vh[:, kb, 0:129], kb == 0, kb == qb,
                                ["PT%d" % reg, "vh"], [okey])
                    if t == 1 and kb1 == qb:
                        self.recip(small[:, 4:5], pC[:, 128:129], ["pC"], ["r1"])
                        self.recip(small[:, 5:6], pD[:, 128:129], ["pD"], ["r2"])
                        self.tt("dve", small[:, 5:6], small[:, 5:6], neglam, ALU.mult, ["r2", "lams"], ["r2"])
                        self.act(o_sb, pC[:, 0:128], AF.Identity, ["pC", "r1"], ["o_sb"], scale=small[:, 4:5])
                        self.stt(o_sb, pD[:, 0:128], small[:, 5:6], o_sb, ALU.mult, ALU.add, ["pD", "r2", "o_sb"], ["o_sb"])
                        self.act(junk128, o_sb, AF.Square, ["o_sb"], ["junk128", "ss"], accum=small[:, 0:1])
                        self.rstd(small[:, 0:1], small[:, 1:2], small[:, 2:3], 1.0 / 128, "ss", "tmpr", "rstd")
                        self.stt(o_bf, o_sb, small[:, 2:3], gsub_bc[:], ALU.mult, ALU.mult, ["o_sb", "rstd", "gsub_bc"], ["o_bf"])
                        self.tr(pT[:, 0:128], o_bf, ident_b[:], ["o_bf", "ident_b"], ["pT"])
                        self.act(mixT[:, h, qb * 128:(qb + 1) * 128], pT[:, 0:128], AF.Copy, ["pT"], ["mixT"])

                for n in range(len(units) + 1):
                    if n < len(units):
                        s1(n)
                        s2(n)
                    if n >= 1:
                        s3(n - 1)
                    if b == 0 and n % 3 != 2:
                        next(bgen, None)
            if b == 0:
                for _ in bgen:
                    pass
            p.barrier()

            self.chk("p1b")
            areset()
            mraw = aF(2052)
            cacc = aF(S)
            mqT = aB(S)
            mkT = aB(S)
            mk_tm = aB(S).rearrange("p (i d) -> p i d", d=128)
            Vp = aB(16 * 130).rearrange("p (i e) -> p i e", e=130)
            sigo = aF(S).rearrange("p (i d) -> p i d", d=128)
            gates = aF(128).rearrange("p (i g) -> p i g", g=8)
            ef = aF(64).rearrange("p (i g) -> p i g", g=4)
            logf = aF(64)
            a_t = aF(64).rearrange("p (i g) -> p i g", g=4)
            e_t = aF(64).rearrange("p (i g) -> p i g", g=4)
            ebl_t = aF(64).rearrange("p (i g) -> p i g", g=4)
            tmp64 = aF(64).rearrange("p (i g) -> p i g", g=4)
            C_f = aF(130)
            tmpC = aF(130)
            C_b = aB(130)
            MS = aB(128)
            hm = aF(128)
            hm_bf = aB(128)
            junk128 = aF(128)
            sm2 = aF(8)
            self.memset("pool", mraw[:, 0:4], 0.0, ["mraw"])
            wc, wkey = load_w(w_in_v, 3584, 8)
            for i in range(NT):
                for kc in range(8):
                    self.mm(pB[:, i * 8:(i + 1) * 8], hT[:, kc, i * 128:(i + 1) * 128], wc[:, kc, 0:8], kc == 0, kc == 7, [wkey, "hT"], ["pB"])
            self.tt("dve", gates, pB[:, 0:128].rearrange("p (i g) -> p i g", g=8), cap(bg_bc[:], 0, [[0, 16], [1, 8]]), ALU.add,
                    ["pB", "bg_bc"], ["gates"])
            self.act(ef, gates[:, :, 4:8], AF.Exp, ["gates"], ["ef"], scale=-1.0)
            self.act(ef, ef, AF.Ln, ["ef"], ["ef"], bias=1.0, scale=1.0)
            self.ts("dve", logf, ef.rearrange("p i g -> p (i g)"), -1.0, ALU.mult, ["ef"], ["logf"])
            self.mm(pC[:, 0:64], maskU_f, logf, True, True, ["cst", "logf"], ["pC"])
            self.mm(pC[:, 64:128], ones_f, logf, True, True, ["cst", "logf"], ["pC"])
            self.tt("dve", tmp64, gates[:, :, 0:4], pC[:, 0:64].rearrange("p (i g) -> p i g", g=4), ALU.subtract, ["gates", "pC"], ["tmp64"])
            self.act(a_t, tmp64, AF.Exp, ["tmp64"], ["a_t"])
            self.act(e_t, pC[:, 0:64].rearrange("p (i g) -> p i g", g=4), AF.Exp, ["pC", "tmp64"], ["e_t"])
            self.act(ebl_t, pC[:, 64:128].rearrange("p (i g) -> p i g", g=4), AF.Exp, ["pC", "tmp64"], ["ebl_t"])
            for h in range(4):
                for which, dstT in ((0, mqT), (1, mkT)):
                    cc = which * 4 + h
                    wc, wkey = load_w(w_in_v, 1536 + which * 512 + h * 128, 128)
                    proj_fm(wc, wkey, lambda g: self.act(mraw[:, 3 + g * 512:3 + (g + 1) * 512], pB[:, 0:512], AF.Copy, ["pB"], ["mraw"]))
                    self.ts("dve", cacc, mraw[:, 0:S], convwT[:, cc, 0:1], ALU.mult, ["mraw", "convwT"], ["cacc"])
                    for j in range(1, 4):
                        self.stt(cacc, mraw[:, j:j + S], convwT[:, cc, j:j + 1], cacc, ALU.mult, ALU.add, ["mraw", "convwT", "cacc"], ["cacc"])
                    if which == 0:
                        self.act(mqT, cacc, AF.Silu, ["cacc", "convbT", "cst"], ["mqT"], bias=convbT[:, cc:cc + 1], scale=ones_f[:, 0:1])
                    else:
                        self.act(cacc, cacc, AF.Silu, ["cacc", "convbT", "cst"], ["cacc"], bias=convbT[:, cc:cc + 1], scale=ones_f[:, 0:1])
                        self.ts("dve", mkT, cacc, 128.0 ** -0.5, ALU.mult, ["cacc"], ["mkT"])
                for i0 in range(0, NT, 8):
                    for j in range(8):
                        self.tr(pT[:, j * 128:(j + 1) * 128], mkT[:, (i0 + j) * 128:(i0 + j + 1) * 128], ident_b[:], ["mkT", "ident_b"], ["pT"])
                    self.act(mk_tm[:, i0:i0 + 8, :], pT[:, 0:1024].rearrange("p (i d) -> p i d", d=128), AF.Copy, ["pT"], ["mk_tm"])
                wc, wkey = load_w(w_in_v, 2560 + h * 128, 128)
                proj_tm(wc, wkey, 128, lambda i: self.act(Vp[:, i, 0:128], pC[:, 0:128], AF.Identity, ["pC", "a_t"], ["Vp"], scale=a_t[:, i, h:h + 1]))
                self.cp("dve", Vp[:, :, 128:129], a_t[:, :, h:h + 1], ["a_t"], ["Vp"])
                wc, wkey = load_w(w_in_v, 3072 + h * 128, 128)
                proj_tm(wc, wkey, 128, lambda i: self.act(sigo[:, i, :], pC[:, 0:128], AF.Sigmoid, ["pC"], ["sigo"]))
                for i in range(NT):
                    tl = slice(i * 128, (i + 1) * 128)
                    self.mm(pA[:, 0:128], mkT[:, tl], mqT[:, tl], True, True, ["mkT", "mqT"], ["pA"])
                    self.tt("dve", MS, pA[:, 0:128], maskU_f, ALU.mult, ["pA", "cst"], ["MS"])
                    self.mm(pC[:, 0:129], MS, Vp[:, i, 0:129], True, i == 0, ["MS", "Vp"], ["pC"])
                    if i > 0:
                        self.mm(pC[:, 0:129], mqT[:, tl], C_b[:, 0:129], False, True, ["mqT", "C_b"], ["pC"])
                    if i < NT - 1:
                        self.mm(pD[:, 0:129], mk_tm[:, i, :], Vp[:, i, 0:129], True, True, ["mk_tm", "Vp"], ["pD"])
                        if i == 0:
                            self.cp("dve", tmpC[:, 0:129], pD[:, 0:129], ["pD"], ["tmpC"])
                        else:
                            self.tt("dve", tmpC[:, 0:129], pD[:, 0:129], C_f[:, 0:129], ALU.add, ["pD", "C_f"], ["tmpC"])
                        self.act(C_f[:, 0:129], tmpC[:, 0:129], AF.Identity, ["tmpC", "ebl_t"], ["C_f"], scale=ebl_t[:, i, h:h + 1])
                        self.cp("dve", C_b[:, 0:129], C_f[:, 0:129], ["C_f"], ["C_b"])
                    self.tt("dve", sm2[:, 0:1], pC[:, 128:129], e_t[:, i, h:h + 1], ALU.mult, ["pC", "e_t"], ["sm2"])
                    self.stt(sm2[:, 1:2], sm2[:, 0:1], -1.0, sm2[:, 0:1], ALU.mult, ALU.max, ["sm2"], ["sm2"])
                    self.ts("dve", sm2[:, 1:2], sm2[:, 1:2], 1.0, ALU.max, ["sm2"], ["sm2"])
                    self.recip(sm2[:, 2:3], sm2[:, 1:2], ["sm2"], ["sm2"])
                    self.tt("dve", sm2[:, 3:4], sm2[:, 2:3], e_t[:, i, h:h + 1], ALU.mult, ["sm2", "e_t"], ["sm2"])
                    self.stt(hm, pC[:, 0:128], sm2[:, 3:4], sigo[:, i, :], ALU.mult, ALU.mult, ["pC", "sm2", "sigo"], ["hm"])
                    self.act(junk128, hm, AF.Square, ["hm"], ["junk128", "ss"], accum=small[:, 0:1])
                    self.rstd(small[:, 0:1], small[:, 1:2], small[:, 2:3], 1.0 / 128, "ss", "tmpr", "rstd")
                    self.stt(hm_bf, hm, small[:, 2:3], gm_bc[:, h * 128:(h + 1) * 128], ALU.mult, ALU.mult, ["hm", "rstd", "gm_bc"], ["hm_bf"])
                    self.tr(pT[:, 0:128], hm_bf, ident_b[:], ["hm_bf", "ident_b"], ["pT"])
                    self.act(mixT[:, 4 + h, tl], pT[:, 0:128], AF.Copy, ["pT"], ["mixT"])
            p.barrier()

            self.chk("p1c")
            areset()
            xts = [aF(D) for _ in range(2)]
            x1s = [aF(D) for _ in range(2)]
            for i in range(NT):
                k = i % 2
                tl = slice(i * 128, (i + 1) * 128)
                pcol = k * 1024
                self.dma("sp", xts[k], x_d[b, tl, :], [], ["xts%d" % k], s_xts[k])
                for half in range(2):
                    for c in range(8):
                        self.mm(pA[:, pcol + half * 512:pcol + (half + 1) * 512], mixT[:, c, tl], woutp[:, c, half * 512:(half + 1) * 512],
                                c == 0, c == 7, ["mixT", "woutp"], ["pA%d" % k])
                self.tt("dve", x1s[k], pA[:, pcol:pcol + D], xts[k], ALU.add, ["pA%d" % k, "xts%d" % k], ["x1s%d" % k])
                self.dma("sp", x1_dram[tl, :], x1s[k], ["x1s%d" % k], [], s_x1s[k])
            p.barrier()
            for blk in range(8):
                load_w(w_q_v, blk * 256, 256, dst=(wq[:, :, blk * 256:(blk + 1) * 256], "wq"))
            p.barrier()
            areset(0)
            pers = [dict(x1=aF(D), h2=aF(D), idx_i=aI(128), gte=aF(128)) for _ in range(2)]
            R = aF(2048)
            xt = R[:, 0:1024]
            qTp = R[:, 0:1024].bitcast(BF16).rearrange("p (c t) -> p c t", t=128)
            sc = R.rearrange("p (c k) -> p c k", k=128)
            cand = R
            eq = R
            wk = aF(2048)
            wk2 = wk
            xn = aB(D)
            h2T = aB(8 * 128).rearrange("p (c t) -> p c t", t=128)
            sv = aF(256)
            si = aU(256)
            sif = aF(256)
            tops = aF(128)
            pos = aU(128)
            pa_u = aU(128)
            pb_u = aU(128)
            paf = aF(128)
            pbf = aF(128)
            i1f = aF(128)
            i2f = aF(128)
            idxf = aF(128)
            tg = aF(128)
            zs = aF(8)
            uvbuf = [mixT[:, j, :] for j in range(8)] + \
                    [woutp[:, 2 * j:2 * j + 2, :].rearrange("p c n -> p (c n)") for j in range(4)] + \
                    [aB(2048) for _ in range(NB - 12)]
            prod = [aF(D) for _ in range(2)]
            junkb = aB(D)
            diag = [aB(128) for _ in range(2)]
            dots = aF(128)
            t1 = aF(128)
            t2 = aF(128)
            wgt = aF(128)
            acc_sb = aF(D)
            V = lambda fn, r, w: p.op("dve", fn, reads=r, writes=w)

            def top16_multi(items):
                for (vals, idxs, src, scratch, tag) in items:
                    V(lambda e, o=vals, i_=src: e.max(out=o[:, 0:8], in_=i_), ["src" + tag], ["v" + tag])
                for (vals, idxs, src, scratch, tag) in items:
                    V(lambda e, o=idxs, m=vals, i_=src: e.max_index(out=o[:, 0:8], in_max=m[:, 0:8], in_values=i_), ["src" + tag, "v" + tag], ["i" + tag])
                for (vals, idxs, src, scratch, tag) in items:
                    V(lambda e, o=scratch, m=vals, i_=src: e.match_replace(out=o, in_to_replace=m[:, 0:8], in_values=i_, imm_value=-1e30), ["src" + tag, "v" + tag], ["w" + tag])
                for (vals, idxs, src, scratch, tag) in items:
                    V(lambda e, o=vals, i_=scratch: e.max(out=o[:, 8:16], in_=i_), ["w" + tag], ["v" + tag])
                for (vals, idxs, src, scratch, tag) in items:
                    V(lambda e, o=idxs, m=vals, i_=scratch: e.max_index(out=o[:, 8:16], in_max=m[:, 8:16], in_values=i_), ["w" + tag, "v" + tag], ["i" + tag])

            SA = ["srcA%d" % hp for hp in range(16)]
            VA = ["vA%d" % hp for hp in range(16)]
            IA = ["iA%d" % hp for hp in range(16)]
            SB_ = ["srcB%d" % h for h in range(8)]
            TOPS = ["vB%d" % h for h in range(8)]
            POS = ["iB%d" % h for h in range(8)]

            def front(i, par):
                tl = slice(i * 128, (i + 1) * 128)
                P = pers[par]
                x1, h2, idx_i, gte = P["x1"], P["h2"], P["idx_i"], P["gte"]
                kx1, kh2, kidx, kg = "x1_%d" % par, "h2_%d" % par, "idx_%d" % par, "gte_%d" % par
                self.dma("sp", x1, x1_dram[tl, :], [], [kx1], s_x1[par])
                yield
                self.act(junkb, x1, AF.Square, [kx1], ["junkb", "ss"], accum=small[:, 0:1])
                yield
                self.ts("dve", small[:, 1:2], small[:, 0:1], 1.0 / D, ALU.mult, ["ss"], ["tmpr"], s2=EPS, op1=ALU.add)
                yield
                self.act(small[:, 1:2], small[:, 1:2], AF.Sqrt, ["tmpr"], ["tmpr"])
                yield
                self.recip(small[:, 2:3], small[:, 1:2], ["tmpr"], ["rstd"])
                self.ts("dve", xn, x1, small[:, 2:3], ALU.mult, [kx1, "rstd"], ["xn"])
                self.stt(h2, x1, small[:, 2:3], A2_bc[:], ALU.mult, ALU.mult, [kx1, "rstd", "A2_bc"], [kh2])
                self.tt("dve", h2, h2, B2_bc[:], ALU.add, [kh2, "B2_bc"], [kh2])
                yield
                for c in range(8):
                    self.tr(pT[:, c * 128:(c + 1) * 128], xn[:, c * 128:(c + 1) * 128], ident_b[:], ["xn", "ident_b"], ["pT"])
                yield
                for c in range(8):
                    self.act(h2T[:, c, :], pT[:, c * 128:(c + 1) * 128], AF.Identity, ["pT", "A2", "modT"], ["h2T"],
                             scale=A2[:, b, c:c + 1], bias=modT[:, 24 + c, b:b + 1])
                yield
                for hp in range(16):
                    for kc in range(8):
                        self.mm(pA[:, hp * 128:(hp + 1) * 128], wq[:, kc, hp * 128:(hp + 1) * 128], h2T[:, kc, :], kc == 0, kc == 7,
                                ["wq", "h2T"], ["pA"])
                    if hp % 4 == 3:
                        yield
                self.act(qTp.rearrange("p c t -> p (c t)"), pA[:, 0:2048], AF.Copy, ["pA"], ["R"])
                yield
                for hp in range(16):
                    self.mm(pA[:, hp * 128:(hp + 1) * 128], qTp[:, hp, :], skT[:, hp, :], True, True, ["R", "skT"], ["pA"])
                yield
                self.act(sc.rearrange("p c k -> p (c k)"), pA[:, 0:2048], AF.Copy, ["pA"], ["R"] + SA)
                yield
                itemsA = [(sv[:, hp * 16:(hp + 1) * 16], si[:, hp * 16:(hp + 1) * 16], sc[:, hp, :], wk[:, hp * 128:(hp + 1) * 128], "A%d" % hp)
                          for hp in range(16)]
                top16_multi(itemsA[0:8])
                yield
                top16_multi(itemsA[8:16])
                yield
                self.cp("dve", sif, si, IA, ["sif"])
                cand4 = cand.rearrange("p (h a b) -> p h a b", h=8, a=16)
                self.tt("dve", cand4, cap(sv, 0, [[32, 8], [1, 16], [0, 16]]), cap(sv, 16, [[32, 8], [0, 16], [1, 16]]), ALU.add,
                        VA + IA, ["R"] + SA + SB_)
                yield
                top16_multi([(tops[:, h * 16:(h + 1) * 16], pos[:, h * 16:(h + 1) * 16], cand[:, h * 256:(h + 1) * 256], wk2[:, h * 256:(h + 1) * 256], "B%d" % h)
                             for h in range(8)])
                yield
                V(lambda e, o=pa_u, i_=pos: e.tensor_single_scalar(out=o, in_=i_, scalar=4, op=ALU.logical_shift_right), POS, ["pa_u"])
                V(lambda e, o=pb_u, i_=pos: e.tensor_single_scalar(out=o, in_=i_, scalar=15, op=ALU.bitwise_and), POS, ["pb_u"])
                self.cp("dve", paf, pa_u, ["pa_u"], ["paf"])
                self.cp("dve", pbf, pb_u, ["pb_u"], ["pbf"])
                yield
                eq4 = eq.rearrange("p (h k a) -> p h k a", h=8, k=16)
                for which, pf, outf, key in ((0, paf, i1f, "i1f"), (1, pbf, i2f, "i2f")):
                    self.tt("dve", eq4, cap(pf, 0, [[16, 8], [1, 16], [0, 16]]), cap(iota16, 0, [[0, 8], [0, 16], [1, 16]]), ALU.is_equal,
                            ["paf", "pbf", "cst"] + POS + TOPS, ["R"] + SB_)
                    self.tt("dve", eq4, eq4, cap(sif, which * 16, [[32, 8], [0, 16], [1, 16]]), ALU.mult, ["R", "sif"], ["R"])
                    self.red(outf.rearrange("p (h k) -> p h k", h=8), eq4, ALU.add, ["R"], [key])
                    yield
                self.stt(idxf, i1f, 128.0, i2f, ALU.mult, ALU.add, ["i1f", "i2f"], ["idxf"])
                self.cp("dve", idx_i, idxf, ["idxf"], [kidx])
                tops3 = tops.rearrange("p (h k) -> p h k", h=8)
                tg3 = tg.rearrange("p (h k) -> p h k", h=8)
                self.tt("dve", tg3, tops3, cap(tops, 0, [[16, 8], [0, 16]]), ALU.subtract, TOPS, ["tg"])
                yield
                self.act(tg, tg, AF.Exp, ["tg"], ["tg"])
                yield
                self.red(zs, tg3, ALU.add, ["tg"], ["zs"])
                self.recip(zs, zs, ["zs"], ["zs"])
                self.tt("dve", gte.rearrange("p (h k) -> p h k", h=8), tg3, cap(zs, 0, [[1, 8], [0, 16]]), ALU.mult, ["tg", "zs"], [kg])
                if b == 0 and i == 0:
                    dbg_out("x1", x1, [kx1])
                    dbg_out("idxf", idxf, ["idxf"])
                    dbg_out("gte", gte, [kg])
                    dbg_out("h2", h2, [kh2])
                if b == 0 and "x1full" in dbg_d:
                    self.dma("sp", dbg_d["x1full"][tl, :], x1, [kx1], [], s_dbg)
                    self.dma("sp", dbg_d["idxfull"][tl, :], idxf, ["idxf"], [], s_dbg)
                yield

            def back(i, par):
                tl = slice(i * 128, (i + 1) * 128)
                P = pers[par]
                x1, h2, idx_i, gte = P["x1"], P["h2"], P["idx_i"], P["gte"]
                kx1, kh2, kidx, kg = "x1_%d" % par, "h2_%d" % par, "idx_%d" % par, "gte_%d" % par
                NG = 128 // GS

                def stage_a(g):
                    for s in range(g * GS, (g + 1) * GS):
                        j = s % NB
                        uk = "uv%d" % j
                        self.gather(uvbuf[j], uv_dram, idx_i[:, s:s + 1], [kidx], [uk], s_uv[j])
                        pk = "prod%d" % (s % 2)
                        self.stt(prod[s % 2], uvbuf[j][:, 0:1024], 1.0, h2, ALU.mult, ALU.mult, [uk, kh2], [pk, "dots%d" % (g % 4)],
                                 accum=dots[:, s:s + 1])

                def stage_b(g):
                    sl = slice(g * GS, (g + 1) * GS)
                    dk_, t1k, t2k, wk_ = "dots%d" % (g % 4), "t1_%d" % (g % 4), "t2_%d" % (g % 4), "wgt%d" % (g % 4)
                    self.act(t1[:, sl], dots[:, sl], AF.Square, [dk_], [t1k])
                    self.ts("dve", t1[:, sl], t1[:, sl], 0.044715, ALU.mult, [t1k], [t1k], s2=1.0, op1=ALU.add)
                    self.tt("dve", t1[:, sl], t1[:, sl], dots[:, sl], ALU.mult, [t1k, dk_], [t1k])
                    self.act(t2[:, sl], t1[:, sl], AF.Tanh, [t1k], [t2k], scale=0.7978845608028654)
                    self.stt(wgt[:, sl], dots[:, sl], 0.5, gte[:, sl], ALU.mult, ALU.mult, [dk_, kg], [wk_])
                    self.stt(wgt[:, sl], t2[:, sl], 1.0, wgt[:, sl], ALU.add, ALU.mult, [t2k, wk_], [wk_])
                    for s in range(g * GS, (g + 1) * GS):
                        j = s % NB
                        uk = "uv%d" % j
                        dk = "diag%d" % (s % 2)
                        self.act(diag[s % 2], ident_b[:], AF.Identity, ["ident_b", wk_], [dk], scale=wgt[:, s:s + 1])
                        self.mm(pB[:, 0:512], diag[s % 2], uvbuf[j][:, 1024:1536], s == 0, s == 127, [dk, uk], ["pB"])
                        self.mm(pC[:, 0:512], diag[s % 2], uvbuf[j][:, 1536:2048], s == 0, s == 127, [dk, uk], ["pC"])

                for g in range(NG + 1):
                    if g < NG:
                        stage_a(g)
                    if g >= 1:
                        stage_b(g - 1)
                    yield
                if b == 0 and i == 0:
                    dbg_out("dots", dots, ["dots%d" % k_ for k_ in range(4)])
                self.tt("dve", acc_sb[:, 0:512], pB[:, 0:512], g2_bc[:, 0:512], ALU.mult, ["pB", "g2_bc"], ["acc_sb"])
                self.tt("dve", acc_sb[:, 512:1024], pC[:, 0:512], g2_bc[:, 512:1024], ALU.mult, ["pC", "g2_bc"], ["acc_sb"])
                self.tt("dve", acc_sb, acc_sb, x1, ALU.add, ["acc_sb", kx1], ["acc_sb"])
                self.act(junkb, acc_sb, AF.Square, ["acc_sb"], ["junkb", "ssb"], accum=small[:, 8:9])
                self.rstd(small[:, 8:9], small[:, 9:10], small[:, 10:11], 1.0 / D, "ssb", "tmprb", "rstdb")
                self.stt(acc_sb, acc_sb, small[:, 10:11], fin_bc[:], ALU.mult, ALU.mult, ["acc_sb", "rstdb", "fin_bc"], ["acc_sb"])
                self.dma("sp", out_d[b, tl, :], acc_sb, ["acc_sb"], [], s_out)
                yield

            def run_all(gen):
                for _ in gen:
                    pass

            nt2 = self.ntiles2
            run_all(front(0, 0))
            self.chk("p2f")
            for i in range(nt2):
                par = i % 2
                fg = front(i + 1, par ^ 1) if i + 1 < nt2 else None
                if not self.no_back:
                    for _ in back(i, par):
                        if fg is not None:
                            next(fg, None)
                if fg is not None:
                    run_all(fg)
            p.barrier()


def _rel_bucket_np(n):
    n = np.asarray(n)
    max_exact = 16
    nf = np.maximum(n, 1).astype(np.float32)
    large = max_exact + (np.log(nf / np.float32(max_exact)) / np.float32(math.log(128 / max_exact))
                         * np.float32(32 - max_exact)).astype(np.int32)
    large = np.minimum(large, 31)
    return np.where(n < max_exact, n, large)


def _prep_inputs(inp):
    f = lambda a: np.ascontiguousarray(np.asarray(a, dtype=np.float32))
    x = f(inp["x"])
    c = f(inp["c"])
    shared = {}
    shared["w_ada"] = f(inp["w_ada"][0])
    shared["b_ada"] = f(inp["b_ada"][0])
    shared["b_adaT"] = f(inp["b_ada"][0].reshape(48, 128).T)
    shared["n1gT"] = f(inp["norm1_g"][0].reshape(8, 128).T)
    shared["n2gT"] = f(inp["norm2_g"][0].reshape(8, 128).T)
    shared["n2g"] = f(inp["norm2_g"][0])
    shared["w_in"] = f(inp["w_in"][0])
    shared["convwT"] = f(np.asarray(inp["conv_w"][0]).T.reshape(8, 128, 4).transpose(1, 0, 2))
    shared["convbT"] = f(np.asarray(inp["conv_b"][0]).reshape(8, 128).T)
    shared["bgate"] = f(np.concatenate([np.asarray(inp["b_igate"][0]), np.asarray(inp["b_fgate"][0])]))
    shared["lam"] = f(np.stack([np.asarray(inp[k][0]) for k in ("lam_q1", "lam_k1", "lam_q2", "lam_k2")]))
    shared["gsub"] = f(inp["diff_sub_g"][0])
    shared["gm"] = f(inp["mlstm_norm_g"][0])
    shared["w_out"] = f(inp["w_out"][0])
    shared["w_q"] = f(inp["peer_w_q"][0])
    sk = np.asarray(inp["peer_sub_keys"][0])
    shared["skT"] = f(sk.transpose(3, 0, 1, 2).reshape(128, 2048))
    shared["peer_u"] = f(inp["peer_u"][0])
    shared["peer_v"] = f(inp["peer_v"][0])
    rb = f(inp["rel_bias"])
    shared["rel_bias"] = rb
    kk = np.arange(128)[:, None]
    qq = np.arange(128)[None, :]
    rel1 = qq - kk + 128
    rel0 = qq - kk
    b1 = rb[_rel_bucket_np(rel1)]
    b0 = rb[_rel_bucket_np(np.maximum(rel0, 0))]
    biasT = np.empty((128, 4, 256), np.float32)
    biasT[:, :, 0:128] = b1.transpose(0, 2, 1)
    biasT[:, :, 128:256] = b0.transpose(0, 2, 1)
    biasT[:, :, 128:256][np.broadcast_to((rel0 < 0)[:, None, :], (128, 4, 128))] = -1e9
    shared["biasT"] = biasT
    shared["final_g"] = f(inp["final_g"])
    cst = np.zeros((128, 400), np.float32)
    cst[:, 0:128] = np.eye(128)
    cst[:, 128:256] = np.triu(np.ones((128, 128)))
    cst[:, 256:384] = 1.0
    cst[:, 384:400] = np.arange(16)[None, :]
    shared["cst"] = cst
    in_maps = []
    for core in range(8):
        m = dict(shared)
        m["x"] = np.ascontiguousarray(x[2 * core:2 * core + 2])
        cc = c[2 * core:2 * core + 2]
        m["cT"] = np.ascontiguousarray(cc.reshape(2, 8, 128).transpose(2, 1, 0))
        in_maps.append(m)
    return in_maps


_NC_CACHE = {}


def kernel(**inputs):
    in_maps = _prep_inputs(inputs)
    if "nc" not in _NC_CACHE:
        _NC_CACHE["nc"] = KB().build()
    nc = _NC_CACHE["nc"]
    res = run_bass_kernel_spmd(nc, in_maps, core_ids=list(range(8)))
    out = np.concatenate([np.asarray(r["out"]) for r in res.results], axis=0)
    return out.astype(np.float32)
```

```python
import math
import numpy as np
import concourse.bass as bass
import concourse.mybir as mybir
from concourse.bass_utils import run_bass_kernel_spmd

F32 = mybir.dt.float32
BF16 = mybir.dt.bfloat16
U32 = mybir.dt.uint32
I32 = mybir.dt.int32
ALU = mybir.AluOpType
AF = mybir.ActivationFunctionType
AX = mybir.AxisListType

ENGS = ("pe", "act", "dve", "pool", "sp")
SAME_ENGINE_SYNC = {"pe": False, "act": True, "dve": True, "pool": True, "sp": True}

S = 2048
D = 1024
NT = 16
EPS = 1e-6
NU = 3
NV = 3
ARENA_W = 24320
WREG = 6144
GS = 4
NB = 16


class Prog:
    def __init__(self, nc):
        self.nc = nc
        self.q = {e: [] for e in ENGS}
        self.cnt = {}
        self.semobj = {}
        self.lastw = {}
        self.readers = {}
        self.seen = {e: {} for e in ENGS}
        self.ownsem = {}
        for e in ENGS:
            s = nc.alloc_semaphore("es_" + e)
            self.ownsem[e] = s.name
            self.semobj[s.name] = s
            self.cnt[s.name] = 0
        self.nsem = 0

    def new_sem(self, name=None):
        self.nsem += 1
        s = self.nc.alloc_semaphore(name or ("ds_%d" % self.nsem))
        self.semobj[s.name] = s
        self.cnt[s.name] = 0
        return s.name

    def op(self, eng, fn, reads=(), writes=(), sem=None, inc=None):
        if sem is None:
            sem = self.ownsem[eng]
            inc = 1
        elif inc is None:
            inc = 16
        deps = {}

        def add(d):
            if d is None:
                return
            s, v = d
            if deps.get(s, 0) < v:
                deps[s] = v

        for k in reads:
            add(self.lastw.get(k))
        for k in writes:
            add(self.lastw.get(k))
            for s, v in self.readers.get(k, {}).items():
                add((s, v))
        waits = []
        for s, v in deps.items():
            if s == self.ownsem[eng] and not SAME_ENGINE_SYNC[eng]:
                continue
            if self.seen[eng].get(s, 0) >= v:
                continue
            self.seen[eng][s] = v
            waits.append((s, v))
        self.cnt[sem] += inc
        val = self.cnt[sem]
        for k in reads:
            r = self.readers.setdefault(k, {})
            if r.get(sem, 0) < val:
                r[sem] = val
        for k in writes:
            self.lastw[k] = (sem, val)
            self.readers[k] = {}
        self.q[eng].append((waits, fn, sem, inc))

    def barrier(self):
        snap = dict(self.cnt)
        for e in ENGS:
            waits = []
            for s, v in snap.items():
                if v > self.seen[e].get(s, 0):
                    self.seen[e][s] = v
                    waits.append((s, v))
            own = self.ownsem[e]
            self.cnt[own] += 1
            self.q[e].append((waits, (lambda eng: eng.nop()), own, 1))
        self.lastw.clear()
        self.readers.clear()

    def emit(self):
        nc = self.nc
        finals = [(s, v) for s, v in self.cnt.items() if v > 0]
        with nc.Block() as block:
            def run(eng_name, wait_all=False):
                def f(eng):
                    for waits, fn, sem, inc in self.q[eng_name]:
                        for s, v in waits:
                            eng.wait_ge(self.semobj[s], v)
                        ins = fn(eng)
                        ins.then_inc(self.semobj[sem], inc)
                    if wait_all:
                        for s, v in finals:
                            eng.wait_ge(self.semobj[s], v)
                return f
            block.tensor(run("pe"))
            block.scalar(run("act"))
            block.vector(run("dve"))
            block.gpsimd(run("pool"))
            block.sync(run("sp", wait_all=True))


def cap(t, off, dims):
    return bass.AP(t.tensor, t.offset + off, [list(t.ap[0])] + [list(d) for d in dims])


class _Stop(Exception):
    pass


class KB:
    def __init__(self, dbg=None, nseq=2, ntiles2=NT, stop=None, no_back=False):
        self.no_back = no_back
        self.dbg = dbg
        self.stop = stop
        self.nseq = nseq
        self.ntiles2 = ntiles2
        nc = self.nc = bass.Bass("TRN2", target_bir_lowering=False)
        self.p = Prog(nc)
        self._n = 0

    def mm(self, out, lhsT, rhs, start, stop, r, w):
        self.p.op("pe", lambda e: e.matmul(out, lhsT=lhsT, rhs=rhs, start=start, stop=stop), reads=r, writes=w)

    def tr(self, out, in_, ident, r, w):
        self.p.op("pe", lambda e: e.transpose(out=out, in_=in_, identity=ident), reads=r, writes=w)

    def act(self, out, in_, func, r, w, scale=None, bias=None, accum=None):
        kw = {}
        if scale is not None:
            kw["scale"] = scale
        if bias is not None:
            kw["bias"] = bias
        if accum is not None:
            kw["accum_out"] = accum
        self.p.op("act", lambda e: e.activation(out=out, in_=in_, func=func, **kw), reads=r, writes=w)

    def tt(self, eng, out, in0, in1, op, r, w):
        self.p.op(eng, lambda e: e.tensor_tensor(out=out, in0=in0, in1=in1, op=op), reads=r, writes=w)

    def ts(self, eng, out, in0, s1, op0, r, w, s2=None, op1=None):
        if op1 is None:
            self.p.op(eng, lambda e: e.tensor_scalar(out=out, in0=in0, scalar1=s1, scalar2=None, op0=op0), reads=r, writes=w)
        else:
            self.p.op(eng, lambda e: e.tensor_scalar(out=out, in0=in0, scalar1=s1, scalar2=s2, op0=op0, op1=op1), reads=r, writes=w)

    def stt(self, out, in0, scalar, in1, op0, op1, r, w, accum=None):
        if accum is None:
            self.p.op("dve", lambda e: e.scalar_tensor_tensor(out=out, in0=in0, scalar=scalar, in1=in1, op0=op0, op1=op1), reads=r, writes=w)
        else:
            self.p.op("dve", lambda e: e.scalar_tensor_tensor(out=out, in0=in0, scalar=scalar, in1=in1, op0=op0, op1=op1, accum_out=accum), reads=r, writes=w)

    def cp(self, eng, out, in_, r, w):
        self.p.op(eng, lambda e: e.tensor_copy(out=out, in_=in_), reads=r, writes=w)

    def red(self, out, in_, op, r, w):
        self.p.op("dve", lambda e: e.tensor_reduce(out=out, in_=in_, axis=AX.X, op=op), reads=r, writes=w)

    def recip(self, out, in_, r, w):
        self.p.op("dve", lambda e: e.reciprocal(out=out, in_=in_), reads=r, writes=w)

    def memset(self, eng, ap, val, w):
        self.p.op(eng, lambda e: e.memset(ap, val), writes=w)

    def dma(self, eng, out, in_, r, w, sem):
        self.p.op(eng, lambda e: e.dma_start(out=out, in_=in_), reads=r, writes=w, sem=sem)

    def gather(self, out, table, idx, r, w, sem):
        self.p.op("pool", lambda e: e.indirect_dma_start(
            out=out, out_offset=None, in_=table,
            in_offset=bass.IndirectOffsetOnAxis(ap=idx, axis=0)), reads=r, writes=w, sem=sem)

    def sb(self, shape, dt, name=None):
        self._n += 1
        return self.nc.alloc_sbuf_tensor("sb_" + (name or ("t%d" % self._n)), shape, dt)

    def rstd(self, ss, tmp, out, inv_n, key_ss, key_tmp, key_out):
        self.ts("dve", tmp, ss, inv_n, ALU.mult, [key_ss], [key_tmp], s2=EPS, op1=ALU.add)
        self.act(tmp, tmp, AF.Sqrt, [key_tmp], [key_tmp])
        self.recip(out, tmp, [key_tmp], [key_out])

    def build(self):
        try:
            self._build()
        except _Stop:
            pass
        self.p.emit()
        return self.nc

    def chk(self, tag):
        if self.stop == tag:
            raise _Stop()

    def _build(self):
        nc, p = self.nc, self.p

        def din(name, shape, dt=F32):
            return nc.dram_tensor(name, shape, dt, kind="ExternalInput").ap()

        x_d = din("x", [2, S, D])
        cT_d = din("cT", [128, 8, 2])
        w_ada_d = din("w_ada", [D, 6 * D])
        b_ada_d = din("b_ada", [6 * D])
        b_adaT_d = din("b_adaT", [128, 48])
        n1gT_d = din("n1gT", [128, 8])
        n2gT_d = din("n2gT", [128, 8])
        n2g_d = din("n2g", [D])
        w_in_d = din("w_in", [D, 3592])
        convwT_d = din("convwT", [128, 8, 4])
        convbT_d = din("convbT", [128, 8])
        bgate_d = din("bgate", [8])
        lam_d = din("lam", [4, 64])
        gsub_d = din("gsub", [128])
        gm_d = din("gm", [512])
        w_out_d = din("w_out", [D, D])
        w_q_d = din("w_q", [D, 2048])
        skT_d = din("skT", [128, 2048])
        u_d = din("peer_u", [16384, D])
        v_d = din("peer_v", [16384, D])
        relb_d = din("rel_bias", [32, 4])
        biasT_d = din("biasT", [128, 4, 256])
        fin_d = din("final_g", [D])
        cst_d = din("cst", [128, 400])
        out_d = nc.dram_tensor("out", [2, S, D], F32, kind="ExternalOutput").ap()
        dbg_d = {}
        if self.dbg:
            for name, shape in self.dbg.items():
                dbg_d[name] = nc.dram_tensor("dbg_" + name, shape, F32, kind="ExternalOutput").ap()

        w_ada_v = w_ada_d.rearrange("(c p) n -> p c n", p=128)
        w_in_v = w_in_d.rearrange("(c p) n -> p c n", p=128)
        w_out_v = w_out_d.rearrange("(c p) n -> p c n", p=128)
        w_q_v = w_q_d.rearrange("(c p) n -> p c n", p=128)

        sb = self.sb
        cst = sb([128, 400], F32, "cst")
        ident_f = cst[:, 0:128]
        maskU_f = cst[:, 128:256]
        ones_f = cst[:, 256:384]
        iota16 = cst[:, 384:400]
        ident_b = sb([128, 128], BF16, "ident_b")
        biasT = sb([128, 4, 256], F32, "biasT")
        cfar = sb([128, 4], F32, "cfar")
        condT = sb([128, 8, 2], F32, "condT")
        modT = sb([128, 48, 2], F32, "modT")
        b_adaT = sb([128, 48], F32, "b_adaT")
        n1gT = sb([128, 8], F32, "n1gT")
        n2gT = sb([128, 8], F32, "n2gT")
        convwT = sb([128, 8, 4], F32, "convwT")
        convbT = sb([128, 8], F32, "convbT")
        A1 = sb([128, 2, 8], F32, "A1")
        A2 = sb([128, 2, 8], F32, "A2")
        fin_bc = sb([128, D], F32, "fin_bc")
        gsub_bc = sb([128, 128], F32, "gsub_bc")
        gm_bc = sb([128, 512], F32, "gm_bc")
        bg_bc = sb([128, 8], F32, "bg_bc")
        A2_bc = sb([128, D], F32, "A2_bc")
        B2_bc = sb([128, D], F32, "B2_bc")
        g2_bc = sb([128, D], F32, "g2_bc")
        lamv = sb([128, 4, 64], F32, "lamv")
        lams = sb([128, 8], F32, "lams")
        small = sb([128, 16], F32, "small")
        mixT = sb([128, 8, S], BF16, "mixT")
        arena1 = sb([128, 8, S], BF16, "arena1")
        hT = arena1
        wq = arena1
        woutp = sb([128, 8, D], BF16, "woutp")
        skT = sb([128, 16, 128], BF16, "skT")
        arena = sb([128, ARENA_W], F32, "arena")
        wst = [arena[:, i * 2048:(i + 1) * 2048].rearrange("p (c n) -> p c n", c=8) for i in range(2)]
        wbf = [arena[:, 4096 + i * 1024:4096 + (i + 1) * 1024].bitcast(BF16).rearrange("p (c n) -> p c n", c=8) for i in range(2)]

        pA = nc.alloc_psum_tensor("pA", [128, 2048], F32)
        pT = nc.alloc_psum_tensor("pT", [128, 1024], BF16)
        pB = nc.alloc_psum_tensor("pB", [128, 512], F32)
        pC = nc.alloc_psum_tensor("pC", [128, 512], F32)
        pD = nc.alloc_psum_tensor("pD", [128, 512], F32)

        s_c = p.new_sem("s_const")
        s_xt = p.new_sem("s_xt")
        s_w = [p.new_sem("s_w0"), p.new_sem("s_w1")]
        s_uv = [p.new_sem("s_uv%d" % i) for i in range(NB)]
        s_pu = [p.new_sem("s_pu%d" % i) for i in range(2)]
        s_pv = [p.new_sem("s_pv%d" % i) for i in range(2)]
        s_ps = [p.new_sem("s_ps%d" % i) for i in range(2)]
        s_xts = [p.new_sem("s_xts%d" % i) for i in range(2)]
        s_x1s = [p.new_sem("s_x1s%d" % i) for i in range(2)]
        s_x1 = [p.new_sem("s_x1_%d" % i) for i in range(2)]
        x1_dram = nc.dram_tensor("x1_scratch", [S, D], F32).ap()
        s_out = p.new_sem("s_out")
        s_dbg = p.new_sem("s_dbg")

        ar = {"off": 0}

        def areset(off=WREG):
            ar["off"] = off

        def aF(n):
            o = ar["off"]
            ar["off"] += n
            assert ar["off"] <= ARENA_W, ar["off"]
            return arena[:, o:o + n]

        def aB(n):
            w = (n + 1) // 2
            return aF(w).bitcast(BF16)[:, 0:n]

        def aU(n):
            return aF(n).bitcast(U32)

        def aI(n):
            return aF(n).bitcast(I32)

        wstate = {"k": 0}

        def load_w(dram_v, c0, ncols, scale_bc=None, dst=None):
            k = wstate["k"]
            wstate["k"] ^= 1
            self.dma("sp", wst[k][:, :, 0:ncols], dram_v[:, :, c0:c0 + ncols], [], ["wst%d" % k], s_w[k])
            if dst is None:
                dst = wbf[k][:, :, 0:ncols]
                key = "wbf%d" % k
            else:
                dst, key = dst
            if scale_bc is None:
                self.cp("pool", dst, wst[k][:, :, 0:ncols], ["wst%d" % k], [key])
            else:
                bc, bkey = scale_bc
                self.tt("pool", dst, wst[k][:, :, 0:ncols], cap(bc, 0, [[0, 8], [1, ncols]]), ALU.mult,
                        ["wst%d" % k, bkey], [key])
            return dst, key

        def dbg_out(name, src, keys):
            if name in dbg_d:
                self.dma("sp", dbg_d[name], src, keys, [], s_dbg)

        cl = lambda o, i, w: self.dma("sp", o, i, [], [w], s_c)
        cl(cst[:], cst_d, "cst")
        cl(biasT[:], biasT_d, "biasT")
        cl(cfar[:], relb_d[31, :].partition_broadcast(128), "cfar")
        cl(condT[:], cT_d, "condT")
        cl(b_adaT[:], b_adaT_d, "b_adaT")
        cl(n1gT[:], n1gT_d, "n1gT")
        cl(n2gT[:], n2gT_d, "n2gT")
        cl(convwT[:], convwT_d, "convwT")
        cl(convbT[:], convbT_d, "convbT")
        cl(fin_bc[:], fin_d.partition_broadcast(128), "fin_bc")
        cl(gsub_bc[:], gsub_d.partition_broadcast(128), "gsub_bc")
        cl(gm_bc[:], gm_d.partition_broadcast(128), "gm_bc")
        cl(bg_bc[:], bgate_d.partition_broadcast(128), "bg_bc")
        for i in range(4):
            cl(lamv[:, i, :], lam_d[i, :].partition_broadcast(128), "lamv")
        uv_dram = nc.dram_tensor("uv_scratch", [16384, 2048], BF16).ap()
        TB = ARENA_W - 6144
        ust = [arena[:, TB + k * 1024:TB + (k + 1) * 1024] for k in range(2)]
        vst = [arena[:, TB + 2048 + k * 1024:TB + 2048 + (k + 1) * 1024] for k in range(2)]
        uvb = [arena[:, TB + 4096 + k * 1024:TB + 4096 + (k + 1) * 1024].bitcast(BF16) for k in range(2)]

        def table_build():
            for n in range(129):
                if n < 128:
                    k = n % 2
                    rows = slice(n * 128, (n + 1) * 128)
                    self.dma("sp", ust[k], u_d[rows, :], [], ["ust%d" % k], s_pu[k])
                    self.dma("sp", vst[k], v_d[rows, :], [], ["vst%d" % k], s_pv[k])
                if n >= 1:
                    m = n - 1
                    k = m % 2
                    rows = slice(m * 128, (m + 1) * 128)
                    self.cp("pool", uvb[k][:, 0:1024], ust[k], ["ust%d" % k], ["uvbA%d" % k])
                    self.cp("pool", uvb[k][:, 1024:2048], vst[k], ["vst%d" % k], ["uvbB%d" % k])
                    self.dma("pool", uv_dram[rows, :], uvb[k], ["uvbA%d" % k, "uvbB%d" % k], [], s_ps[k])
                yield

        bgen = table_build()
        areset()
        mod_rows = aF(6144)
        brow = aF(6144)
        cl(brow[0:2, :], b_ada_d.partition_broadcast(2), "brow")
        skst = wst[0].rearrange("p c n -> p (c n)")
        cl(skst, skT_d, "wst0")
        p.barrier()
        self.cp("pool", skT[:].rearrange("p c n -> p (c n)"), skst, ["wst0"], ["skT"])
        self.act(condT[:], condT[:], AF.Silu, ["condT"], ["condT"])
        self.cp("dve", ident_b[:], ident_f, ["cst"], ["ident_b"])
        for h_ in range(4):
            self.ts("dve", biasT[:, h_, :], biasT[:, h_, :], cfar[:, h_:h_ + 1], ALU.subtract, ["biasT", "cfar"], ["biasT"])
        self.ts("dve", gsub_bc[:], gsub_bc[:], 0.8, ALU.mult, ["gsub_bc"], ["gsub_bc"])
        junk64 = aF(64)
        for j in range(2):
            self.tt("dve", junk64, lamv[:, 2 * j, :], lamv[:, 2 * j + 1, :], ALU.mult, ["lamv"], ["junk64"])
            self.red(lams[:, j:j + 1], junk64, ALU.add, ["junk64"], ["lams"])
        self.act(lams[:, 2:4], lams[:, 0:2], AF.Exp, ["lams"], ["lams"])
        self.tt("dve", lams[:, 4:5], lams[:, 3:4], lams[:, 2:3], ALU.subtract, ["lams"], ["lams"])
        self.ts("dve", lams[:, 4:5], lams[:, 4:5], -0.2, ALU.add, ["lams"], ["lams"])
        neglam = lams[:, 4:5]
        for blk in range(24):
            k = wstate["k"]
            wstate["k"] ^= 1
            wk_ = "wst%d" % k
            self.dma("sp", wst[k], w_ada_v[:, :, blk * 256:(blk + 1) * 256], [], [wk_], s_w[k])
            for kc in range(8):
                self.mm(pB[0:2, 0:256], condT[:, kc, :], wst[k][:, kc, :], kc == 0, kc == 7, ["condT", wk_], ["pB"])
            self.tt("dve", mod_rows[0:2, blk * 256:(blk + 1) * 256], pB[0:2, 0:256], brow[0:2, blk * 256:(blk + 1) * 256],
                    ALU.add, ["pB", "brow"], ["mod_rows"])
            for jl in range(2):
                j = blk * 2 + jl
                for kc in range(8):
                    self.mm(pC[:, 2 * j:2 * j + 2], wst[k][:, kc, jl * 128:(jl + 1) * 128], condT[:, kc, :],
                            kc == 0, kc == 7, ["condT", wk_], ["pC"])
        self.tt("dve", modT[:], pC[:, 0:96].rearrange("p (j b) -> p j b", b=2), cap(b_adaT[:], 0, [[1, 48], [0, 2]]),
                ALU.add, ["pC", "b_adaT"], ["modT"])
        for b in range(2):
            self.stt(A1[:, b, :], modT[:, 8:16, b], 1.0, n1gT[:], ALU.add, ALU.mult, ["modT", "n1gT"], ["A1"])
            self.stt(A2[:, b, :], modT[:, 32:40, b], 1.0, n2gT[:], ALU.add, ALU.mult, ["modT", "n2gT"], ["A2"])
        if "modT" in dbg_d:
            dbg_out("modT", modT[:].rearrange("p j b -> p (j b)"), ["modT"])
        mod_dram = nc.dram_tensor("mod_scratch", [2, 6144], F32).ap()
        self.dma("sp", mod_dram, mod_rows[0:2, :], ["mod_rows"], [], s_c)
        p.barrier()

        self.chk("p0")
        for b in range(self.nseq):
            areset()
            g1_bc = aF(D)
            n2g_bc = aF(D)
            self.dma("sp", n2g_bc, n2g_d.partition_broadcast(128), [], ["n2g_bc"], s_c)
            for dst, key, col0 in ((g1_bc, "g1_bc", 2048), (B2_bc[:], "B2_bc", 3072), (A2_bc[:], "A2_bc", 4096), (g2_bc[:], "g2_bc", 5120)):
                self.dma("sp", dst, mod_dram[b, col0:col0 + 1024].partition_broadcast(128), [], [key], s_c)
            p.barrier()
            self.stt(A2_bc[:], A2_bc[:], 1.0, n2g_bc, ALU.add, ALU.mult, ["A2_bc", "n2g_bc"], ["A2_bc"])
            for blk in range(4):
                load_w(w_out_v, blk * 256, 256, scale_bc=(g1_bc[:, blk * 256:(blk + 1) * 256], "g1_bc"),
                       dst=(woutp[:, :, blk * 256:(blk + 1) * 256], "woutp"))
            p.barrier()

            areset()
            xt = aF(D)
            junk = aF(D)
            xn = aB(D)
            for i in range(NT):
                self.dma("sp", xt, x_d[b, i * 128:(i + 1) * 128, :], [], ["xt"], s_xt)
                self.act(junk, xt, AF.Square, ["xt"], ["junk", "ss"], accum=small[:, 0:1])
                self.rstd(small[:, 0:1], small[:, 1:2], small[:, 2:3], 1.0 / D, "ss", "tmpr", "rstd")
                self.ts("dve", xn, xt, small[:, 2:3], ALU.mult, ["xt", "rstd"], ["xn"])
                for c in range(8):
                    self.tr(pT[:, c * 128:(c + 1) * 128], xn[:, c * 128:(c + 1) * 128], ident_b[:], ["xn", "ident_b"], ["pT"])
                for c in range(8):
                    self.act(hT[:, c, i * 128:(i + 1) * 128], pT[:, c * 128:(c + 1) * 128], AF.Identity, ["pT", "A1", "modT"], ["hT"],
                             scale=A1[:, b, c:c + 1], bias=modT[:, c, b:b + 1])
            p.barrier()

            self.chk("p1a")

            def proj_fm(wcols, wkey, evac):
                for g in range(4):
                    for kc in range(8):
                        self.mm(pB[:, 0:512], wcols[:, kc, :], hT[:, kc, g * 512:(g + 1) * 512], kc == 0, kc == 7,
                                [wkey, "hT"], ["pB"])
                    evac(g)

            def proj_tm(wcols, wkey, ncols, evac):
                for i in range(NT):
                    for kc in range(8):
                        self.mm(pC[:, 0:ncols], hT[:, kc, i * 128:(i + 1) * 128], wcols[:, kc, 0:ncols], kc == 0, kc == 7,
                                [wkey, "hT"], ["pC"])
                    evac(i)

            areset()
            qT = aB(S)
            kT = aB(S)
            vh = aB(16 * 130).rearrange("p (i e) -> p i e", e=130)
            PTs = [aB(1024) for _ in range(2)]
            tmpSs = [aF(256) for _ in range(2)]
            o_sb = aF(128)
            o_bf = aB(128)
            junk128 = aF(128)
            self.memset("pool", vh[:, :, 128:129], 1.0, ["vh"])
            for h in range(4):
                wc, wkey = load_w(w_in_v, h * 128, 128)
                proj_fm(wc, wkey, lambda g: self.act(qT[:, g * 512:(g + 1) * 512], pB[:, 0:512], AF.Copy, ["pB"], ["qT"], scale=0.125))
                wc, wkey = load_w(w_in_v, 512 + h * 128, 128)
                proj_fm(wc, wkey, lambda g: self.act(kT[:, g * 512:(g + 1) * 512], pB[:, 0:512], AF.Copy, ["pB"], ["kT"]))
                wc, wkey = load_w(w_in_v, 1024 + h * 128, 128)
                proj_tm(wc, wkey, 128, lambda i: self.act(vh[:, i, 0:128], pC[:, 0:128], AF.Copy, ["pC"], ["vh"]))
                self.chk("p1b_proj")
                units = []
                for qb in range(NT):
                    for t in range(2):
                        if qb < 8:
                            units.append((qb, t, 0, qb))
                        else:
                            units.append((qb, t, 0, 7))
                            units.append((qb, t, 8, qb))
                Os = (pC, pD)

                def s1(n):
                    qb, t, kb0, kb1 = units[n]
                    reg = n % 2
                    ps = slice(t * 64, (t + 1) * 64)
                    for kb in range(kb0, kb1 + 1):
                        col = reg * 1024 + (kb - kb0) * 128
                        self.mm(pA[:, col:col + 128], kT[ps, kb * 128:(kb + 1) * 128], qT[ps, qb * 128:(qb + 1) * 128],
                                True, True, ["kT", "qT"], ["pA%d" % reg])

                def s2(n):
                    qb, t, kb0, kb1 = units[n]
                    reg = n % 2
                    rk, pk, tk = "pA%d" % reg, "PT%d" % reg, "tmpS%d" % reg
                    base = reg * 1024
                    far_hi = min(kb1, qb - 2)
                    nears = []
                    for kb, boff in ((qb - 1, 0), (qb, 128)):
                        if kb < kb0 or kb > kb1 or kb < 0:
                            continue
                        lc = (kb - kb0) * 128
                        self.tt("dve", tmpSs[reg][:, boff:boff + 128], pA[:, base + lc:base + lc + 128], biasT[:, h, boff:boff + 128], ALU.add,
                                [rk, "biasT"], [tk])
                        nears.append((lc, boff))
                    if far_hi >= kb0:
                        ncol = (far_hi - kb0 + 1) * 128
                        self.act(PTs[reg][:, 0:ncol], pA[:, base:base + ncol], AF.Exp, [rk, tk], [pk])
                    for lc, boff in nears:
                        self.act(PTs[reg][:, lc:lc + 128], tmpSs[reg][:, boff:boff + 128], AF.Exp, [tk], [pk])

                def s3(n):
                    qb, t, kb0, kb1 = units[n]
                    reg = n % 2
                    okey = "pC" if t == 0 else "pD"
                    for kb in range(kb0, kb1 + 1):
                        lc = (kb - kb0) * 128
                        self.mm(Os[t][:, 0:129], PTs[reg][:, lc:lc + 128], vh[:, kb, 0:129], kb == 0, kb == qb,
                                ["PT%d" % reg, "vh"], [okey])
                    if t == 1 and kb1 == qb:
                        self.recip(small[:, 4:5], pC[:, 128:129], ["pC"], ["r1"])
                        self.recip(small[:, 5:6], pD[:, 128:129], ["pD"], ["r2"])
                        self.tt("dve", small[:, 5:6], small[:, 5:6], neglam, ALU.mult, ["r2", "lams"], ["r2"])
                        self.act(o_sb, pC[:, 0:128], AF.Identity, ["pC", "r1"], ["o_sb"], scale=small[:, 4:5])
                        self.stt(o_sb, pD[:, 0:128], small[:, 5:6], o_sb, ALU.mult, ALU.add, ["pD", "r2", "o_sb"], ["o_sb"])
                        self.act(junk128, o_sb, AF.Square, ["o_sb"], ["junk128", "ss"], accum=small[:, 0:1])
                        self.rstd(small[:, 0:1], small[:, 1:2], small[:, 2:3], 1.0 / 128, "ss", "tmpr", "rstd")
                        self.stt(o_bf, o_sb, small[:, 2:3], gsub_bc[:], ALU.mult, ALU.mult, ["o_sb", "rstd", "gsub_bc"], ["o_bf"])
                        self.tr(pT[:, 0:128], o_bf, ident_b[:], ["o_bf", "ident_b"], ["pT"])
                        self.act(mixT[:, h, qb * 128:(qb + 1) * 128], pT[:, 0:128], AF.Copy, ["pT"], ["mixT"])

                for n in range(len(units) + 1):
                    if n < len(units):
                        s1(n)
                        s2(n)
                    if n >= 1:
                        s3(n - 1)
                    if b == 0 and n % 3 != 2:
                        next(bgen, None)
            if b == 0:
                for _ in bgen:
                    pass
            p.barrier()

            self.chk("p1b")
            areset()
            mraw = aF(2052)
            cacc = aF(S)
            mqT = aB(S)
            mkT = aB(S)
            mk_tm = aB(S).rearrange("p (i d) -> p i d", d=128)
            Vp = aB(16 * 130).rearrange("p (i e) -> p i e", e=130)
            sigo = aF(S).rearrange("p (i d) -> p i d", d=128)
            gates = aF(128).rearrange("p (i g) -> p i g", g=8)
            ef = aF(64).rearrange("p (i g) -> p i g", g=4)
            logf = aF(64)
            a_t = aF(64).rearrange("p (i g) -> p i g", g=4)
            e_t = aF(64).rearrange("p (i g) -> p i g", g=4)
            ebl_t = aF(64).rearrange("p (i g) -> p i g", g=4)
            tmp64 = aF(64).rearrange("p (i g) -> p i g", g=4)
            C_f = aF(130)
            tmpC = aF(130)
            C_b = aB(130)
            MS = aB(128)
            hm = aF(128)
            hm_bf = aB(128)
            junk128 = aF(128)
            sm2 = aF(8)
            self.memset("pool", mraw[:, 0:4], 0.0, ["mraw"])
            wc, wkey = load_w(w_in_v, 3584, 8)
            for i in range(NT):
                for kc in range(8):
                    self.mm(pB[:, i * 8:(i + 1) * 8], hT[:, kc, i * 128:(i + 1) * 128], wc[:, kc, 0:8], kc == 0, kc == 7, [wkey, "hT"], ["pB"])
            self.tt("dve", gates, pB[:, 0:128].rearrange("p (i g) -> p i g", g=8), cap(bg_bc[:], 0, [[0, 16], [1, 8]]), ALU.add,
                    ["pB", "bg_bc"], ["gates"])
            self.act(ef, gates[:, :, 4:8], AF.Exp, ["gates"], ["ef"], scale=-1.0)
            self.act(ef, ef, AF.Ln, ["ef"], ["ef"], bias=1.0, scale=1.0)
            self.ts("dve", logf, ef.rearrange("p i g -> p (i g)"), -1.0, ALU.mult, ["ef"], ["logf"])
            self.mm(pC[:, 0:64], maskU_f, logf, True, True, ["cst", "logf"], ["pC"])
            self.mm(pC[:, 64:128], ones_f, logf, True, True, ["cst", "logf"], ["pC"])
            self.tt("dve", tmp64, gates[:, :, 0:4], pC[:, 0:64].rearrange("p (i g) -> p i g", g=4), ALU.subtract, ["gates", "pC"], ["tmp64"])
            self.act(a_t, tmp64, AF.Exp, ["tmp64"], ["a_t"])
            self.act(e_t, pC[:, 0:64].rearrange("p (i g) -> p i g", g=4), AF.Exp, ["pC", "tmp64"], ["e_t"])
            self.act(ebl_t, pC[:, 64:128].rearrange("p (i g) -> p i g", g=4), AF.Exp, ["pC", "tmp64"], ["ebl_t"])
            for h in range(4):
                for which, dstT in ((0, mqT), (1, mkT)):
                    cc = which * 4 + h
                    wc, wkey = load_w(w_in_v, 1536 + which * 512 + h * 128, 128)
                    proj_fm(wc, wkey, lambda g: self.act(mraw[:, 3 + g * 512:3 + (g + 1) * 512], pB[:, 0:512], AF.Copy, ["pB"], ["mraw"]))
                    self.ts("dve", cacc, mraw[:, 0:S], convwT[:, cc, 0:1], ALU.mult, ["mraw", "convwT"], ["cacc"])
                    for j in range(1, 4):
                        self.stt(cacc, mraw[:, j:j + S], convwT[:, cc, j:j + 1], cacc, ALU.mult, ALU.add, ["mraw", "convwT", "cacc"], ["cacc"])
                    if which == 0:
                        self.act(mqT, cacc, AF.Silu, ["cacc", "convbT", "cst"], ["mqT"], bias=convbT[:, cc:cc + 1], scale=ones_f[:, 0:1])
                    else:
                        self.act(cacc, cacc, AF.Silu, ["cacc", "convbT", "cst"], ["cacc"], bias=convbT[:, cc:cc + 1], scale=ones_f[:, 0:1])
                        self.ts("dve", mkT, cacc, 128.0 ** -0.5, ALU.mult, ["cacc"], ["mkT"])
                for i0 in range(0, NT, 8):
                    for j in range(8):
                        self.tr(pT[:, j * 128:(j + 1) * 128], mkT[:, (i0 + j) * 128:(i0 + j + 1) * 128], ident_b[:], ["mkT", "ident_b"], ["pT"])
                    self.act(mk_tm[:, i0:i0 + 8, :], pT[:, 0:1024].rearrange("p (i d) -> p i d", d=128), AF.Copy, ["pT"], ["mk_tm"])
                wc, wkey = load_w(w_in_v, 2560 + h * 128, 128)
                proj_tm(wc, wkey, 128, lambda i: self.act(Vp[:, i, 0:128], pC[:, 0:128], AF.Identity, ["pC", "a_t"], ["Vp"], scale=a_t[:, i, h:h + 1]))
                self.cp("dve", Vp[:, :, 128:129], a_t[:, :, h:h + 1], ["a_t"], ["Vp"])
                wc, wkey = load_w(w_in_v, 3072 + h * 128, 128)
                proj_tm(wc, wkey, 128, lambda i: self.act(sigo[:, i, :], pC[:, 0:128], AF.Sigmoid, ["pC"], ["sigo"]))
                for i in range(NT):
                    tl = slice(i * 128, (i + 1) * 128)
                    self.mm(pA[:, 0:128], mkT[:, tl], mqT[:, tl], True, True, ["mkT", "mqT"], ["pA"])
                    self.tt("dve", MS, pA[:, 0:128], maskU_f, ALU.mult, ["pA", "cst"], ["MS"])
                    self.mm(pC[:, 0:129], MS, Vp[:, i, 0:129], True, i == 0, ["MS", "Vp"], ["pC"])
                    if i > 0:
                        self.mm(pC[:, 0:129], mqT[:, tl], C_b[:, 0:129], False, True, ["mqT", "C_b"], ["pC"])
                    if i < NT - 1:
                        self.mm(pD[:, 0:129], mk_tm[:, i, :], Vp[:, i, 0:129], True, True, ["mk_tm", "Vp"], ["pD"])
                        if i == 0:
                            self.cp("dve", tmpC[:, 0:129], pD[:, 0:129], ["pD"], ["tmpC"])
                        else:
                            self.tt("dve", tmpC[:, 0:129], pD[:, 0:129], C_f[:, 0:129], ALU.add, ["pD", "C_f"], ["tmpC"])
                        self.act(C_f[:, 0:129], tmpC[:, 0:129], AF.Identity, ["tmpC", "ebl_t"], ["C_f"], scale=ebl_t[:, i, h:h + 1])
                        self.cp("dve", C_b[:, 0:129], C_f[:, 0:129], ["C_f"], ["C_b"])
                    self.tt("dve", sm2[:, 0:1], pC[:, 128:129], e_t[:, i, h:h + 1], ALU.mult, ["pC", "e_t"], ["sm2"])
                    self.stt(sm2[:, 1:2], sm2[:, 0:1], -1.0, sm2[:, 0:1], ALU.mult, ALU.max, ["sm2"], ["sm2"])
                    self.ts("dve", sm2[:, 1:2], sm2[:, 1:2], 1.0, ALU.max, ["sm2"], ["sm2"])
                    self.recip(sm2[:, 2:3], sm2[:, 1:2], ["sm2"], ["sm2"])
                    self.tt("dve", sm2[:, 3:4], sm2[:, 2:3], e_t[:, i, h:h + 1], ALU.mult, ["sm2", "e_t"], ["sm2"])
                    self.stt(hm, pC[:, 0:128], sm2[:, 3:4], sigo[:, i, :], ALU.mult, ALU.mult, ["pC", "sm2", "sigo"], ["hm"])
                    self.act(junk128, hm, AF.Square, ["hm"], ["junk128", "ss"], accum=small[:, 0:1])
                    self.rstd(small[:, 0:1], small[:, 1:2], small[:, 2:3], 1.0 / 128, "ss", "tmpr", "rstd")
                    self.stt(hm_bf, hm, small[:, 2:3], gm_bc[:, h * 128:(h + 1) * 128], ALU.mult, ALU.mult, ["hm", "rstd", "gm_bc"], ["hm_bf"])
                    self.tr(pT[:, 0:128], hm_bf, ident_b[:], ["hm_bf", "ident_b"], ["pT"])
                    self.act(mixT[:, 4 + h, tl], pT[:, 0:128], AF.Copy, ["pT"], ["mixT"])
            p.barrier()

            self.chk("p1c")
            areset()
            xts = [aF(D) for _ in range(2)]
            x1s = [aF(D) for _ in range(2)]
            for i in range(NT):
                k = i % 2
                tl = slice(i * 128, (i + 1) * 128)
                pcol = k * 1024
                self.dma("sp", xts[k], x_d[b, tl, :], [], ["xts%d" % k], s_xts[k])
                for half in range(2):
                    for c in range(8):
                        self.mm(pA[:, pcol + half * 512:pcol + (half + 1) * 512], mixT[:, c, tl], woutp[:, c, half * 512:(half + 1) * 512],
                                c == 0, c == 7, ["mixT", "woutp"], ["pA%d" % k])
                self.tt("dve", x1s[k], pA[:, pcol:pcol + D], xts[k], ALU.add, ["pA%d" % k, "xts%d" % k], ["x1s%d" % k])
                self.dma("pool", x1_dram[tl, :], x1s[k], ["x1s%d" % k], [], s_x1s[k])
            p.barrier()
            for blk in range(8):
                load_w(w_q_v, blk * 256, 256, dst=(wq[:, :, blk * 256:(blk + 1) * 256], "wq"))
            p.barrier()
            areset(0)
            pers = [dict(x1=aF(D), h2=aF(D), idx_i=aI(128), gte=aF(128)) for _ in range(2)]
            R = aF(2048)
            xt = R[:, 0:1024]
            qTp = R[:, 0:1024].bitcast(BF16).rearrange("p (c t) -> p c t", t=128)
            sc = R.rearrange("p (c k) -> p c k", k=128)
            cand = R
            eq = R
            wk = aF(2048)
            wk2 = wk
            xn = aB(D)
            h2T = aB(8 * 128).rearrange("p (c t) -> p c t", t=128)
            sv = aF(256)
            si = aU(256)
            sif = aF(256)
            tops = aF(128)
            pos = aU(128)
            pa_u = aU(128)
            pb_u = aU(128)
            paf = aF(128)
            pbf = aF(128)
            i1f = aF(128)
            i2f = aF(128)
            idxf = aF(128)
            tg = aF(128)
            zs = aF(8)
            uvbuf = [mixT[:, j, :] for j in range(8)] + \
                    [woutp[:, 2 * j:2 * j + 2, :].rearrange("p c n -> p (c n)") for j in range(4)] + \
                    [aB(2048) for _ in range(NB - 12)]
            prod = [aF(D) for _ in range(2)]
            junkb = aB(D)
            diag = [aB(128) for _ in range(2)]
            dots = aF(128)
            t1 = aF(128)
            t2 = aF(128)
            wgt = aF(128)
            acc_sb = aF(D)
            V = lambda fn, r, w: p.op("dve", fn, reads=r, writes=w)

            def top16_multi(items):
                for (vals, idxs, src, scratch, tag) in items:
                    V(lambda e, o=vals, i_=src: e.max(out=o[:, 0:8], in_=i_), ["src" + tag], ["v" + tag])
                for (vals, idxs, src, scratch, tag) in items:
                    V(lambda e, o=idxs, m=vals, i_=src: e.max_index(out=o[:, 0:8], in_max=m[:, 0:8], in_values=i_), ["src" + tag, "v" + tag], ["i" + tag])
                for (vals, idxs, src, scratch, tag) in items:
                    V(lambda e, o=scratch, m=vals, i_=src: e.match_replace(out=o, in_to_replace=m[:, 0:8], in_values=i_, imm_value=-1e30), ["src" + tag, "v" + tag], ["w" + tag])
                for (vals, idxs, src, scratch, tag) in items:
                    V(lambda e, o=vals, i_=scratch: e.max(out=o[:, 8:16], in_=i_), ["w" + tag], ["v" + tag])
                for (vals, idxs, src, scratch, tag) in items:
                    V(lambda e, o=idxs, m=vals, i_=scratch: e.max_index(out=o[:, 8:16], in_max=m[:, 8:16], in_values=i_), ["w" + tag, "v" + tag], ["i" + tag])

            SA = ["srcA%d" % hp for hp in range(16)]
            VA = ["vA%d" % hp for hp in range(16)]
            IA = ["iA%d" % hp for hp in range(16)]
            SB_ = ["srcB%d" % h for h in range(8)]
            TOPS = ["vB%d" % h for h in range(8)]
            POS = ["iB%d" % h for h in range(8)]

            def front(i, par):
                tl = slice(i * 128, (i + 1) * 128)
                P = pers[par]
                x1, h2, idx_i, gte = P["x1"], P["h2"], P["idx_i"], P["gte"]
                kx1, kh2, kidx, kg = "x1_%d" % par, "h2_%d" % par, "idx_%d" % par, "gte_%d" % par
                self.dma("sp", x1, x1_dram[tl, :], [], [kx1], s_x1[par])
                yield
                self.act(junkb, x1, AF.Square, [kx1], ["junkb", "ss"], accum=small[:, 0:1])
                yield
                self.ts("dve", small[:, 1:2], small[:, 0:1], 1.0 / D, ALU.mult, ["ss"], ["tmpr"], s2=EPS, op1=ALU.add)
                yield
                self.act(small[:, 1:2], small[:, 1:2], AF.Sqrt, ["tmpr"], ["tmpr"])
                yield
                self.recip(small[:, 2:3], small[:, 1:2], ["tmpr"], ["rstd"])
                self.ts("dve", xn, x1, small[:, 2:3], ALU.mult, [kx1, "rstd"], ["xn"])
                self.stt(h2, x1, small[:, 2:3], A2_bc[:], ALU.mult, ALU.mult, [kx1, "rstd", "A2_bc"], [kh2])
                self.tt("dve", h2, h2, B2_bc[:], ALU.add, [kh2, "B2_bc"], [kh2])
                yield
                for c in range(8):
                    self.tr(pT[:, c * 128:(c + 1) * 128], xn[:, c * 128:(c + 1) * 128], ident_b[:], ["xn", "ident_b"], ["pT"])
                yield
                for c in range(8):
                    self.act(h2T[:, c, :], pT[:, c * 128:(c + 1) * 128], AF.Identity, ["pT", "A2", "modT"], ["h2T"],
                             scale=A2[:, b, c:c + 1], bias=modT[:, 24 + c, b:b + 1])
                yield
                for hp in range(16):
                    for kc in range(8):
                        self.mm(pA[:, hp * 128:(hp + 1) * 128], wq[:, kc, hp * 128:(hp + 1) * 128], h2T[:, kc, :], kc == 0, kc == 7,
                                ["wq", "h2T"], ["pA"])
                    if hp % 4 == 3:
                        yield
                self.act(qTp.rearrange("p c t -> p (c t)"), pA[:, 0:2048], AF.Copy, ["pA"], ["R"])
                yield
                for hp in range(16):
                    self.mm(pA[:, hp * 128:(hp + 1) * 128], qTp[:, hp, :], skT[:, hp, :], True, True, ["R", "skT"], ["pA"])
                yield
                self.act(sc.rearrange("p c k -> p (c k)"), pA[:, 0:2048], AF.Copy, ["pA"], ["R"] + SA)
                yield
                itemsA = [(sv[:, hp * 16:(hp + 1) * 16], si[:, hp * 16:(hp + 1) * 16], sc[:, hp, :], wk[:, hp * 128:(hp + 1) * 128], "A%d" % hp)
                          for hp in range(16)]
                top16_multi(itemsA[0:8])
                yield
                top16_multi(itemsA[8:16])
                yield
                self.cp("dve", sif, si, IA, ["sif"])
                cand4 = cand.rearrange("p (h a b) -> p h a b", h=8, a=16)
                self.tt("dve", cand4, cap(sv, 0, [[32, 8], [1, 16], [0, 16]]), cap(sv, 16, [[32, 8], [0, 16], [1, 16]]), ALU.add,
                        VA + IA, ["R"] + SA + SB_)
                yield
                top16_multi([(tops[:, h * 16:(h + 1) * 16], pos[:, h * 16:(h + 1) * 16], cand[:, h * 256:(h + 1) * 256], wk2[:, h * 256:(h + 1) * 256], "B%d" % h)
                             for h in range(8)])
                yield
                V(lambda e, o=pa_u, i_=pos: e.tensor_single_scalar(out=o, in_=i_, scalar=4, op=ALU.logical_shift_right), POS, ["pa_u"])
                V(lambda e, o=pb_u, i_=pos: e.tensor_single_scalar(out=o, in_=i_, scalar=15, op=ALU.bitwise_and), POS, ["pb_u"])
                self.cp("dve", paf, pa_u, ["pa_u"], ["paf"])
                self.cp("dve", pbf, pb_u, ["pb_u"], ["pbf"])
                yield
                eq4 = eq.rearrange("p (h k a) -> p h k a", h=8, k=16)
                for which, pf, outf, key in ((0, paf, i1f, "i1f"), (1, pbf, i2f, "i2f")):
                    self.tt("dve", eq4, cap(pf, 0, [[16, 8], [1, 16], [0, 16]]), cap(iota16, 0, [[0, 8], [0, 16], [1, 16]]), ALU.is_equal,
                            ["paf", "pbf", "cst"] + POS + TOPS, ["R"] + SB_)
                    self.tt("dve", eq4, eq4, cap(sif, which * 16, [[32, 8], [0, 16], [1, 16]]), ALU.mult, ["R", "sif"], ["R"])
                    self.red(outf.rearrange("p (h k) -> p h k", h=8), eq4, ALU.add, ["R"], [key])
                    yield
                self.stt(idxf, i1f, 128.0, i2f, ALU.mult, ALU.add, ["i1f", "i2f"], ["idxf"])
                self.cp("dve", idx_i, idxf, ["idxf"], [kidx])
                tops3 = tops.rearrange("p (h k) -> p h k", h=8)
                tg3 = tg.rearrange("p (h k) -> p h k", h=8)
                self.tt("dve", tg3, tops3, cap(tops, 0, [[16, 8], [0, 16]]), ALU.subtract, TOPS, ["tg"])
                yield
                self.act(tg, tg, AF.Exp, ["tg"], ["tg"])
                yield
                self.red(zs, tg3, ALU.add, ["tg"], ["zs"])
                self.recip(zs, zs, ["zs"], ["zs"])
                self.tt("dve", gte.rearrange("p (h k) -> p h k", h=8), tg3, cap(zs, 0, [[1, 8], [0, 16]]), ALU.mult, ["tg", "zs"], [kg])
                if b == 0 and i == 0:
                    dbg_out("x1", x1, [kx1])
                    dbg_out("idxf", idxf, ["idxf"])
                    dbg_out("gte", gte, [kg])
                    dbg_out("h2", h2, [kh2])
                if b == 0 and "x1full" in dbg_d:
                    self.dma("sp", dbg_d["x1full"][tl, :], x1, [kx1], [], s_dbg)
                    self.dma("sp", dbg_d["idxfull"][tl, :], idxf, ["idxf"], [], s_dbg)
                yield

            def back(i, par):
                tl = slice(i * 128, (i + 1) * 128)
                P = pers[par]
                x1, h2, idx_i, gte = P["x1"], P["h2"], P["idx_i"], P["gte"]
                kx1, kh2, kidx, kg = "x1_%d" % par, "h2_%d" % par, "idx_%d" % par, "gte_%d" % par
                NG = 128 // GS

                def stage_a(g):
                    for s in range(g * GS, (g + 1) * GS):
                        j = s % NB
                        uk = "uv%d" % j
                        self.gather(uvbuf[j], uv_dram, idx_i[:, s:s + 1], [kidx], [uk], s_uv[j])
                        pk = "prod%d" % (s % 2)
                        self.stt(prod[s % 2], uvbuf[j][:, 0:1024], 1.0, h2, ALU.mult, ALU.mult, [uk, kh2], [pk, "dots%d" % (g % 4)],
                                 accum=dots[:, s:s + 1])

                def stage_b(g):
                    sl = slice(g * GS, (g + 1) * GS)
                    dk_, t1k, t2k, wk_ = "dots%d" % (g % 4), "t1_%d" % (g % 4), "t2_%d" % (g % 4), "wgt%d" % (g % 4)
                    self.act(t1[:, sl], dots[:, sl], AF.Square, [dk_], [t1k])
                    self.ts("dve", t1[:, sl], t1[:, sl], 0.044715, ALU.mult, [t1k], [t1k], s2=1.0, op1=ALU.add)
                    self.tt("dve", t1[:, sl], t1[:, sl], dots[:, sl], ALU.mult, [t1k, dk_], [t1k])
                    self.act(t2[:, sl], t1[:, sl], AF.Tanh, [t1k], [t2k], scale=0.7978845608028654)
                    self.stt(wgt[:, sl], dots[:, sl], 0.5, gte[:, sl], ALU.mult, ALU.mult, [dk_, kg], [wk_])
                    self.stt(wgt[:, sl], t2[:, sl], 1.0, wgt[:, sl], ALU.add, ALU.mult, [t2k, wk_], [wk_])
                    for s in range(g * GS, (g + 1) * GS):
                        j = s % NB
                        uk = "uv%d" % j
                        dk = "diag%d" % (s % 2)
                        self.act(diag[s % 2], ident_b[:], AF.Identity, ["ident_b", wk_], [dk], scale=wgt[:, s:s + 1])
                        self.mm(pB[:, 0:512], diag[s % 2], uvbuf[j][:, 1024:1536], s == 0, s == 127, [dk, uk], ["pB"])
                        self.mm(pC[:, 0:512], diag[s % 2], uvbuf[j][:, 1536:2048], s == 0, s == 127, [dk, uk], ["pC"])

                for g in range(NG + 1):
                    if g < NG:
                        stage_a(g)
                    if g >= 1:
                        stage_b(g - 1)
                    yield
                if b == 0 and i == 0:
                    dbg_out("dots", dots, ["dots%d" % k_ for k_ in range(4)])
                self.tt("dve", acc_sb[:, 0:512], pB[:, 0:512], g2_bc[:, 0:512], ALU.mult, ["pB", "g2_bc"], ["acc_sb"])
                self.tt("dve", acc_sb[:, 512:1024], pC[:, 0:512], g2_bc[:, 512:1024], ALU.mult, ["pC", "g2_bc"], ["acc_sb"])
                self.tt("dve", acc_sb, acc_sb, x1, ALU.add, ["acc_sb", kx1], ["acc_sb"])
                self.act(junkb, acc_sb, AF.Square, ["acc_sb"], ["junkb", "ssb"], accum=small[:, 8:9])
                self.rstd(small[:, 8:9], small[:, 9:10], small[:, 10:11], 1.0 / D, "ssb", "tmprb", "rstdb")
                self.stt(acc_sb, acc_sb, small[:, 10:11], fin_bc[:], ALU.mult, ALU.mult, ["acc_sb", "rstdb", "fin_bc"], ["acc_sb"])
                self.dma("sp", out_d[b, tl, :], acc_sb, ["acc_sb"], [], s_out)
                yield

            def run_all(gen):
                for _ in gen:
                    pass

            nt2 = self.ntiles2
            run_all(front(0, 0))
            self.chk("p2f")
            for i in range(nt2):
                par = i % 2
                fg = front(i + 1, par ^ 1) if i + 1 < nt2 else None
                if not self.no_back:
                    for _ in back(i, par):
                        if fg is not None:
                            next(fg, None)
                if fg is not None:
                    run_all(fg)
            p.barrier()


def _rel_bucket_np(n):
    n = np.asarray(n)
    max_exact = 16
    nf = np.maximum(n, 1).astype(np.float32)
    large = max_exact + (np.log(nf / np.float32(max_exact)) / np.float32(math.log(128 / max_exact))
                         * np.float32(32 - max_exact)).astype(np.int32)
    large = np.minimum(large, 31)
    return np.where(n < max_exact, n, large)


def _prep_inputs(inp):
    f = lambda a: np.ascontiguousarray(np.asarray(a, dtype=np.float32))
    x = f(inp["x"])
    c = f(inp["c"])
    shared = {}
    shared["w_ada"] = f(inp["w_ada"][0])
    shared["b_ada"] = f(inp["b_ada"][0])
    shared["b_adaT"] = f(inp["b_ada"][0].reshape(48, 128).T)
    shared["n1gT"] = f(inp["norm1_g"][0].reshape(8, 128).T)
    shared["n2gT"] = f(inp["norm2_g"][0].reshape(8, 128).T)
    shared["n2g"] = f(inp["norm2_g"][0])
    shared["w_in"] = f(inp["w_in"][0])
    shared["convwT"] = f(np.asarray(inp["conv_w"][0]).T.reshape(8, 128, 4).transpose(1, 0, 2))
    shared["convbT"] = f(np.asarray(inp["conv_b"][0]).reshape(8, 128).T)
    shared["bgate"] = f(np.concatenate([np.asarray(inp["b_igate"][0]), np.asarray(inp["b_fgate"][0])]))
    shared["lam"] = f(np.stack([np.asarray(inp[k][0]) for k in ("lam_q1", "lam_k1", "lam_q2", "lam_k2")]))
    shared["gsub"] = f(inp["diff_sub_g"][0])
    shared["gm"] = f(inp["mlstm_norm_g"][0])
    shared["w_out"] = f(inp["w_out"][0])
    shared["w_q"] = f(inp["peer_w_q"][0])
    sk = np.asarray(inp["peer_sub_keys"][0])
    shared["skT"] = f(sk.transpose(3, 0, 1, 2).reshape(128, 2048))
    shared["peer_u"] = f(inp["peer_u"][0])
    shared["peer_v"] = f(inp["peer_v"][0])
    rb = f(inp["rel_bias"])
    shared["rel_bias"] = rb
    kk = np.arange(128)[:, None]
    qq = np.arange(128)[None, :]
    rel1 = qq - kk + 128
    rel0 = qq - kk
    b1 = rb[_rel_bucket_np(rel1)]
    b0 = rb[_rel_bucket_np(np.maximum(rel0, 0))]
    biasT = np.empty((128, 4, 256), np.float32)
    biasT[:, :, 0:128] = b1.transpose(0, 2, 1)
    biasT[:, :, 128:256] = b0.transpose(0, 2, 1)
    biasT[:, :, 128:256][np.broadcast_to((rel0 < 0)[:, None, :], (128, 4, 128))] = -1e9
    shared["biasT"] = biasT
    shared["final_g"] = f(inp["final_g"])
    cst = np.zeros((128, 400), np.float32)
    cst[:, 0:128] = np.eye(128)
    cst[:, 128:256] = np.triu(np.ones((128, 128)))
    cst[:, 256:384] = 1.0
    cst[:, 384:400] = np.arange(16)[None, :]
    shared["cst"] = cst
    in_maps = []
    for core in range(8):
        m = dict(shared)
        m["x"] = np.ascontiguousarray(x[2 * core:2 * core + 2])
        cc = c[2 * core:2 * core + 2]
        m["cT"] = np.ascontiguousarray(cc.reshape(2, 8, 128).transpose(2, 1, 0))
        in_maps.append(m)
    return in_maps


_NC_CACHE = {}


def kernel(**inputs):
    in_maps = _prep_inputs(inputs)
    if "nc" not in _NC_CACHE:
        _NC_CACHE["nc"] = KB().build()
    nc = _NC_CACHE["nc"]
    res = run_bass_kernel_spmd(nc, in_maps, core_ids=list(range(8)))
    out = np.concatenate([np.asarray(r["out"]) for r in res.results], axis=0)
    return out.astype(np.float32)
```
